# Optimizing a Trainium2 kernel written in Bass

```python
import jax, jax.numpy as jnp
from jax import lax
import numpy as np

D_MODEL = 2048
BATCH = 2
SEQ = 16384
DEPTH = 2

PLE_DIM = 256
N_HEADS = 8
HEAD_DIM = 128
ATTN_WIDTH = N_HEADS * HEAD_DIM
ROPE_DIM = HEAD_DIM // 4
ROPE_THETA = 500000.0
MOBA_BLOCK = 256
MOBA_TOPK = 3
Q_CHUNK = 64
CONV_CHANNELS = 512
CONV_KERNEL = 31
SGU_GROUPS = 4
SGU_GROUP_DIM = 128
SGU_WIDTH = SGU_GROUPS * SGU_GROUP_DIM
SGU_CHUNK = 128
D_FF = 5632
FFN_CONV_KERNEL = 3
N_BRANCHES = 3
IN_WIDTH = 3 * ATTN_WIDTH + 2 * CONV_CHANNELS + 2 * SGU_WIDTH + N_BRANCHES * D_MODEL
EPS = 1e-6

kernel_name = "hybrid_gated_conv_sgu_moba_block"


def rms_norm(x, g):
    xf = x.astype(jnp.float32)
    y = xf * lax.rsqrt(jnp.mean(xf * xf, axis=-1, keepdims=True) + EPS)
    return (y * g.astype(jnp.float32)).astype(x.dtype)


def layer_norm(x, g, b):
    xf = x.astype(jnp.float32)
    mu = jnp.mean(xf, axis=-1, keepdims=True)
    var = jnp.mean(jnp.square(xf - mu), axis=-1, keepdims=True)
    y = (xf - mu) * lax.rsqrt(var + EPS)
    return (y * g.astype(jnp.float32) + b.astype(jnp.float32)).astype(x.dtype)


def causal_dwconv(x, w, b):
    k = w.shape[0]
    y = lax.conv_general_dilated(
        x, w[:, None, :].astype(x.dtype), window_strides=(1,), padding=[(k - 1, 0)],
        dimension_numbers=("NWC", "WIO", "NWC"), feature_group_count=x.shape[-1])
    return y + b.astype(x.dtype)


def partial_rope(x, pos):
    half = ROPE_DIM // 2
    inv = jnp.float32(ROPE_THETA) ** (-jnp.arange(0, ROPE_DIM, 2, dtype=jnp.float32) / ROPE_DIM)
    ang = pos.astype(jnp.float32)[:, None] * inv[None, :]
    cos, sin = jnp.cos(ang), jnp.sin(ang)
    xr = x[..., :ROPE_DIM].astype(jnp.float32)
    x1, x2 = xr[..., :half], xr[..., half:]
    rot = jnp.concatenate([x1 * cos - x2 * sin, x2 * cos + x1 * sin], axis=-1).astype(x.dtype)
    return jnp.concatenate([rot, x[..., ROPE_DIM:]], axis=-1)


def moba_attention(q, k, v):
    bsz, nh, s, dh = q.shape
    nb = -(-s // MOBA_BLOCK)
    pad = nb * MOBA_BLOCK - s
    kb = jnp.pad(k, ((0, 0), (0, 0), (0, pad), (0, 0))).reshape(bsz, nh, nb, MOBA_BLOCK, dh)
    vb = jnp.pad(v, ((0, 0), (0, 0), (0, pad), (0, 0))).reshape(bsz, nh, nb, MOBA_BLOCK, dh)
    k_mean = jnp.mean(kb.astype(jnp.float32), axis=3).astype(q.dtype)
    topk = min(MOBA_TOPK, nb)
    gather = jax.vmap(jax.vmap(lambda blocks, idx: blocks[idx]))
    sel_len = topk * MOBA_BLOCK

    def chunk(c):
        q0 = c * Q_CHUNK
        qc = lax.dynamic_slice_in_dim(q, q0, Q_CHUNK, axis=2)
        own = q0 // MOBA_BLOCK
        k_own = lax.dynamic_index_in_dim(kb, own, axis=2, keepdims=False)
        v_own = lax.dynamic_index_in_dim(vb, own, axis=2, keepdims=False)
        gate = jnp.einsum("bhqd,bhnd->bhqn", qc, k_mean).astype(jnp.float32)
        gate = jnp.where(jnp.arange(nb) < own, gate, -jnp.inf)
        _, idx = lax.top_k(gate, topk)
        valid = jnp.arange(topk) < own
        k_sel = gather(kb, idx)
        v_sel = gather(vb, idx)
        s_sel = jnp.einsum("bhqd,bhqjkd->bhqjk", qc, k_sel).astype(jnp.float32)
        s_sel = jnp.where(valid[:, None], s_sel, -jnp.inf)
        q_pos = q0 + jnp.arange(Q_CHUNK)
        k_pos = own * MOBA_BLOCK + jnp.arange(MOBA_BLOCK)
        s_own = jnp.einsum("bhqd,bhkd->bhqk", qc, k_own).astype(jnp.float32)
        s_own = jnp.where(k_pos[None, :] <= q_pos[:, None], s_own, -jnp.inf)
        scores = jnp.concatenate([s_sel.reshape(bsz, nh, Q_CHUNK, sel_len), s_own], axis=-1)
        probs = jax.nn.softmax(scores, axis=-1).astype(v.dtype)
        p_sel = probs[..., :sel_len].reshape(bsz, nh, Q_CHUNK, topk, MOBA_BLOCK)
        p_own = probs[..., sel_len:]
        return (jnp.einsum("bhqjk,bhqjkd->bhqd", p_sel, v_sel)
                + jnp.einsum("bhqk,bhkd->bhqd", p_own, v_own))

    outs = lax.map(chunk, jnp.arange(s // Q_CHUNK))
    return outs.transpose(1, 0, 3, 2, 4).reshape(bsz, s, nh * dh)


def token_mixer(xn, w_in, conv_dw_w, conv_dw_b, conv_norm_g, conv_norm_b, conv_out,
                sgu_norm_g, sgu_norm_b, sgu_w, sgu_b, sgu_out, attn_out, w_o):
    bsz, s, _ = xn.shape
    proj = xn @ w_in
    cuts = np.cumsum([ATTN_WIDTH, ATTN_WIDTH, ATTN_WIDTH, CONV_CHANNELS, CONV_CHANNELS,
                      SGU_WIDTH, SGU_WIDTH, D_MODEL, D_MODEL]).tolist()
    q, k, v, ca, cg, su, sv, g_a, g_b, g_c = jnp.split(proj, cuts, axis=-1)

    a = ca * jax.nn.sigmoid(cg)
    a = causal_dwconv(a, conv_dw_w, conv_dw_b)
    a = jax.nn.silu(layer_norm(a, conv_norm_g, conv_norm_b))
    y_a = a @ conv_out

    su = jax.nn.gelu(su, approximate=True)
    sv = layer_norm(jax.nn.gelu(sv, approximate=True), sgu_norm_g, sgu_norm_b)
    svr = sv.reshape(bsz, s // SGU_CHUNK, SGU_CHUNK, SGU_GROUPS, SGU_GROUP_DIM)
    mask = jnp.tril(jnp.ones((SGU_CHUNK, SGU_CHUNK), dtype=sgu_w.dtype))
    mixed = jnp.einsum("gts,bnsgc->bntgc", sgu_w * mask, svr) + sgu_b.T[None, None, :, :, None]
    y_b = (su * mixed.reshape(bsz, s, SGU_WIDTH)) @ sgu_out

    pos = jnp.arange(s, dtype=jnp.int32)
    to_heads = lambda t: t.reshape(bsz, s, N_HEADS, HEAD_DIM).transpose(0, 2, 1, 3)
    qh = partial_rope(to_heads(q), pos) * jnp.asarray(HEAD_DIM ** -0.5, dtype=q.dtype)
    kh = partial_rope(to_heads(k), pos)
    y_c = moba_attention(qh, kh, to_heads(v)) @ attn_out

    merged = jax.nn.sigmoid(g_a) * y_a + jax.nn.sigmoid(g_b) * y_b + jax.nn.sigmoid(g_c) * y_c
    return merged @ w_o


def conv_ffn(hn, ffn_in, ffn_dw_w, ffn_dw_b, ffn_out):
    a, b = jnp.split(hn @ ffn_in, 2, axis=-1)
    a = causal_dwconv(a, ffn_dw_w, ffn_dw_b)
    return (jax.nn.gelu(a, approximate=True) * b) @ ffn_out


def setup_inputs(seed: int = 0) -> dict:
    key = jax.random.key(seed)
    ks = jax.random.split(key, 32)
    f32 = jnp.float32
    nrm = lambda k, shape, fan_in: jax.random.normal(k, shape, f32) * (fan_in ** -0.5)
    gain = lambda k, shape: 1.0 + 0.05 * jax.random.normal(k, shape, f32)
    small = lambda k, shape: 0.02 * jax.random.normal(k, shape, f32)
    L = DEPTH
    return {
        "x": jax.random.normal(ks[0], (BATCH, SEQ, D_MODEL), f32),
        "p": jax.random.normal(ks[1], (DEPTH, BATCH, SEQ, PLE_DIM), f32),
        "mix_norm_pre": gain(ks[2], (L, D_MODEL)),
        "mix_norm_post": gain(ks[3], (L, D_MODEL)),
        "w_in": nrm(ks[4], (L, D_MODEL, IN_WIDTH), D_MODEL),
        "conv_dw_w": nrm(ks[5], (L, CONV_KERNEL, CONV_CHANNELS), CONV_KERNEL),
        "conv_dw_b": small(ks[6], (L, CONV_CHANNELS)),
        "conv_norm_g": gain(ks[7], (L, CONV_CHANNELS)),
        "conv_norm_b": small(ks[8], (L, CONV_CHANNELS)),
        "conv_out": nrm(ks[9], (L, CONV_CHANNELS, D_MODEL), CONV_CHANNELS),
        "sgu_norm_g": gain(ks[10], (L, SGU_WIDTH)),
        "sgu_norm_b": small(ks[11], (L, SGU_WIDTH)),
        "sgu_w": nrm(ks[12], (L, SGU_GROUPS, SGU_CHUNK, SGU_CHUNK), SGU_CHUNK),
        "sgu_b": small(ks[13], (L, SGU_GROUPS, SGU_CHUNK)),
        "sgu_out": nrm(ks[14], (L, SGU_WIDTH, D_MODEL), SGU_WIDTH),
        "attn_out": nrm(ks[15], (L, ATTN_WIDTH, D_MODEL), ATTN_WIDTH),
        "w_o": nrm(ks[16], (L, D_MODEL, D_MODEL), D_MODEL),
        "ffn_norm_pre": gain(ks[17], (L, D_MODEL)),
        "ffn_norm_post": gain(ks[18], (L, D_MODEL)),
        "ffn_in": nrm(ks[19], (L, D_MODEL, 2 * D_FF), D_MODEL),
        "ffn_dw_w": nrm(ks[20], (L, FFN_CONV_KERNEL, D_FF), FFN_CONV_KERNEL),
        "ffn_dw_b": small(ks[21], (L, D_FF)),
        "ffn_out": nrm(ks[22], (L, D_FF, D_MODEL), D_FF),
        "ple_norm": gain(ks[23], (L, D_MODEL)),
        "ple_gate": nrm(ks[24], (L, D_MODEL, D_MODEL), D_MODEL),
        "ple_proj": nrm(ks[25], (L, PLE_DIM, D_MODEL), PLE_DIM),
    }


def reference(x, p, mix_norm_pre, mix_norm_post, w_in, conv_dw_w, conv_dw_b, conv_norm_g,
              conv_norm_b, conv_out, sgu_norm_g, sgu_norm_b, sgu_w, sgu_b, sgu_out, attn_out,
              w_o, ffn_norm_pre, ffn_norm_post, ffn_in, ffn_dw_w, ffn_dw_b, ffn_out,
              ple_norm, ple_gate, ple_proj):
    for i in range(DEPTH):
        xn = rms_norm(x, mix_norm_pre[i])
        h = token_mixer(xn, w_in[i], conv_dw_w[i], conv_dw_b[i], conv_norm_g[i], conv_norm_b[i],
                        conv_out[i], sgu_norm_g[i], sgu_norm_b[i], sgu_w[i], sgu_b[i], sgu_out[i],
                        attn_out[i], w_o[i])
        x = x + rms_norm(h, mix_norm_post[i])
        hn = rms_norm(x, ffn_norm_pre[i])
        f = conv_ffn(hn, ffn_in[i], ffn_dw_w[i], ffn_dw_b[i], ffn_out[i])
        x = x + rms_norm(f, ffn_norm_post[i])
        e = p[i] @ ple_proj[i]
        g = jax.nn.sigmoid(rms_norm(x, ple_norm[i]) @ ple_gate[i])
        x = x + g * e
    return x
```

```python
import numpy as np
from contextlib import ExitStack
import concourse.bass as bass
import concourse.mybir as mybir
from concourse.bass_utils import run_bass_kernel_spmd

F32 = mybir.dt.float32
BF16 = mybir.dt.bfloat16
AF = mybir.ActivationFunctionType
ALU = mybir.AluOpType
AX = mybir.AxisListType

D = 2048
NH = 8
HD = 128
ROPE = 32
BLK = 256
TOPK = 3
CC = 512
CK = 31
SW = 512
DFF = 5632
PLE = 256
INW = 11264
EPS = 1e-6
T = 512
NJ = 4
NEG = -1.0e30
SEM_ROT = 30000


class Buf:
    def __init__(self, name, parent=None):
        self.name = name
        self.parent = parent
        self.children = []
        if parent is not None:
            parent.children.append(self)
        self.w = {}
        self.r = {}
        self.dsem = None
        self.dtotal = 0
        self.excl = False

    def family(self):
        out = [self]
        p = self.parent
        while p is not None:
            out.append(p)
            p = p.parent
        stack = list(self.children)
        while stack:
            c = stack.pop()
            out.append(c)
            stack.extend(c.children)
        return out


class Eng:
    def __init__(self, K, name, eng, is_pe=False):
        self.K = K
        self.name = name
        self.eng = eng
        self.is_pe = is_pe
        self.sem = K.new_sem("e_" + name)
        self.own = [self.sem]
        self.cnt = 0
        self.known = {}
        self.pend_r = []
        self.pend_w = []

    def _wait(self, deps):
        for sem, val in deps.items():
            if self.is_pe and any(sem is o for o in self.own):
                continue
            if self.known.get(sem, 0) >= val:
                continue
            self.eng.wait_ge(sem, val)
            self.known[sem] = val

    def _deps(self, reads, writes):
        deps = {}
        for b in reads:
            for f in b.family():
                for s, v in f.w.items():
                    if deps.get(s, 0) < v:
                        deps[s] = v
        for b in writes:
            for f in b.family():
                for s, v in f.w.items():
                    if deps.get(s, 0) < v:
                        deps[s] = v
                for s, v in f.r.items():
                    if deps.get(s, 0) < v:
                        deps[s] = v
        return deps

    def op(self, fn, reads=(), writes=(), signal=True):
        ex = [b for b in reads if b.excl]
        if ex:
            reads = [b for b in reads if not b.excl]
            writes = list(writes) + ex
        self._wait(self._deps(reads, writes))
        ins = fn(self.eng)
        self.pend_r.extend(reads)
        self.pend_w.extend(writes)
        if signal:
            if self.cnt >= SEM_ROT:
                self.sem = self.K.new_sem("e_" + self.name)
                self.own.append(self.sem)
                self.cnt = 0
            self.cnt += 1
            ins.then_inc(self.sem, 1)
            tok = (self.sem, self.cnt)
            for b in self.pend_r:
                if b.r.get(tok[0], 0) < tok[1]:
                    b.r[tok[0]] = tok[1]
            for b in self.pend_w:
                b.w = {tok[0]: tok[1]}
                b.r = {}
            self.pend_r = []
            self.pend_w = []
        return ins

    def dma(self, out, in_, reads, writes, sbuf, **kw):
        self._wait(self._deps(reads, writes))
        if sbuf.dsem is None or sbuf.dtotal >= SEM_ROT:
            sbuf.dsem = self.K.new_sem("d_" + sbuf.name)
            sbuf.dtotal = 0
        ins = self.eng.dma_start(out=out, in_=in_, **kw)
        sbuf.dtotal += 16
        ins.then_inc(sbuf.dsem, 16)
        tok = (sbuf.dsem, sbuf.dtotal)
        for b in reads:
            if b.r.get(tok[0], 0) < tok[1]:
                b.r[tok[0]] = tok[1]
        for b in writes:
            b.w = {tok[0]: tok[1]}
            b.r = {}
        return ins

    def wait_all(self, bufs):
        deps = {}
        for b in bufs:
            for f in b.family():
                for dct in (f.w, f.r):
                    for s, v in dct.items():
                        if deps.get(s, 0) < v:
                            deps[s] = v
        self._wait(deps)


class Kb:
    def __init__(self):
        self.nc = bass.Bass("TRN2", target_bir_lowering=False)
        self.es = ExitStack()
        self.nsem = 0
        nc = self.nc
        self.pe = Eng(self, "pe", nc.tensor, is_pe=True)
        self.act = Eng(self, "act", nc.scalar)
        self.dve = Eng(self, "dve", nc.vector)
        self.pool = Eng(self, "pool", nc.gpsimd)
        self.sp = Eng(self, "sp", nc.sync)

    def new_sem(self, name):
        self.nsem += 1
        return self.es.enter_context(self.nc.semaphore(name + "_%d" % self.nsem))

    def sb(self, name, shape, dt):
        t = self.es.enter_context(self.nc.sbuf_tensor(name, shape, dt))
        return t, Buf(name)

    def ps(self, name, shape, dt):
        t = self.es.enter_context(self.nc.psum_tensor(name, shape, dt))
        b = Buf(name)
        b.excl = True
        return t, b

    def din(self, name, shape, dt=F32):
        return self.nc.dram_tensor(name, list(shape), dt, kind="ExternalInput").ap()

    def dout(self, name, shape, dt=F32):
        return self.nc.dram_tensor(name, list(shape), dt, kind="ExternalOutput").ap()

    def dint(self, name, shape, dt):
        return self.nc.dram_tensor(name, list(shape), dt).ap()


def weight_blocks():
    blks = []
    for i in range(10):
        col = {0: 0, 1: 512, 2: 1024, 3: 1536, 4: 2048, 5: 2560, 6: 3584, 7: 3072, 8: 4096, 9: 4608}[i]
        blks.append(("P%d" % i, 512, [("w_in", 0, 2048, col)]))
    for c in range(16):
        blks.append(("YG%d" % c, 128, [("conv_out", 0, 512, c * 128), ("sgu_out", 0, 512, c * 128),
                                       ("attn_out", 0, 1024, c * 128), ("w_in", 0, 2048, 5120 + c * 128),
                                       ("w_in", 0, 2048, 7168 + c * 128), ("w_in", 0, 2048, 9216 + c * 128)]))
    for nb in range(4):
        blks.append(("O%d" % nb, 512, [("w_o", 0, 2048, nb * 512)]))
    for i in range(11):
        blks.append(("FA%d" % i, 512, [("ffn_in", 0, 2048, i * 512)]))
        blks.append(("FB%d" % i, 512, [("ffn_in", 0, 2048, DFF + i * 512)]))
    for nb in range(4):
        for kg in range(4):
            blks.append(("FO%d_%d" % (nb, kg), 512, [("ffn_out", kg * 1408, 1408, nb * 512)]))
    for nb in range(4):
        blks.append(("PG%d" % nb, 512, [("ple_gate", 0, 2048, nb * 512)]))
        blks.append(("PP%d" % nb, 512, [("ple_proj", 0, 256, nb * 512)]))
    return blks


WBLKS = weight_blocks()
NWB = len(WBLKS)
GRP = 4
NKB = 3
NPT = 4
SBANKS = (0, 1, 6, 7)


class Arena:
    def __init__(self, K, nbytes):
        self.K = K
        self.n = nbytes
        slab = K.es.enter_context(K.nc.sbuf_tensor("arena", [128, nbytes // 2], BF16))
        self.base = K.nc.lookup_mloc(slab).addr
        self.bufs = {}
        self.top = 0
        self.gen = 0

    def begin(self):
        self.top = 0
        self.gen += 1

    def carve(self, name, shape, dt, at=None):
        esz = 4 if dt == F32 else 2
        n = 1
        for d in shape[1:]:
            n *= d
        nb = n * esz
        nb_al = (nb + 31) // 32 * 32
        if at is None:
            at = self.top
            self.top += nb_al
        assert at % 4 == 0 and at + nb <= self.n, (name, at, nb, self.n)
        h = self.K.nc.alloc_sbuf_tensor_at("%s_g%d" % (name, self.gen), list(shape), dt, offset=self.base + at)
        ap = h.ap()
        b = self.bufs.get(name)
        if b is None:
            b = Buf(name)
            b.lo, b.hi = at, at + nb
            b.fam = None
            self.bufs[name] = b
        else:
            assert (b.lo, b.hi) == (at, at + nb), name
        return ap, b

    def sub(self, parent, name, lo_off, nbytes):
        b = self.bufs.get(name)
        if b is None:
            b = Buf(name)
            b.lo, b.hi = parent.lo + lo_off, parent.lo + lo_off + nbytes
            b.fam = None
            self.bufs[name] = b
        return b

    def finalize(self):
        bl = list(self.bufs.values())
        for b in bl:
            b.fam = [o for o in bl if o.lo < b.hi and b.lo < o.hi]


def _family(self):
    f = getattr(self, "fam", None)
    if f is not None:
        return f
    return [self]


Buf.family = _family


def build(S, depth):
    NT = S // T
    NB = S // BLK
    assert NB <= 64
    K = Kb()
    nc = K.nc
    pe, act, dve, pool, sp = K.pe, K.act, K.dve, K.pool, K.sp

    x_in = K.din("x", [S, D])
    p_in = K.din("p", [depth, S, PLE])
    win = {"w_in": K.din("w_in", [depth, D, INW]), "conv_out": K.din("conv_out", [depth, CC, D]),
           "sgu_out": K.din("sgu_out", [depth, SW, D]), "attn_out": K.din("attn_out", [depth, NH * HD, D]),
           "w_o": K.din("w_o", [depth, D, D]), "ffn_in": K.din("ffn_in", [depth, D, 2 * DFF]),
           "ffn_out": K.din("ffn_out", [depth, DFF, D]), "ple_gate": K.din("ple_gate", [depth, D, D]),
           "ple_proj": K.din("ple_proj", [depth, PLE, D])}
    vecP = K.din("vecP", [depth, 128, 3, 16])
    vecB = K.din("vecB", [depth, 2, 128, D])
    convw = K.din("convw", [depth, 128, 4, CK + 3])
    sguB = K.din("sguB", [depth, 2, 128, SW])
    sguwT = K.din("sguwT", [depth, 128, 4, 128])
    sgub = K.din("sgub", [depth, 1, 4 * 128])
    ffnw = K.din("ffnw", [depth, 128, 44, 4])
    cst = K.din("cst", [128, 3, 128])
    ropet = K.din("ropet", [ROPE, 2, S])
    out = K.dout("out", [S, D])

    wb = [K.dint("wb%d" % l, [NWB, 128, 16 * 512], BF16) for l in range(depth)]
    wbB = [Buf("wbB%d" % l) for l in range(depth)]
    kT = [K.dint("kT%d" % l, [NH, HD, S], BF16) for l in range(depth)]
    vS = [K.dint("vS%d" % l, [S, NH, HD], BF16) for l in range(depth)]
    kTB = [Buf("kT%d" % l) for l in range(depth)]
    vSB = [Buf("vS%d" % l) for l in range(depth)]
    xs = [K.dint("xs%d" % l, [S, D], F32) for l in range(max(depth - 1, 1))]
    xsB = [Buf("xs%d" % l) for l in range(max(depth - 1, 1))]
    outB = Buf("outB")
    NOB = Buf("extern")

    for l in range(depth):
        for bi, (tag, ncol, parts) in enumerate(WBLKS):
            kc0 = 0
            for (src, r0, nr, c0) in parts:
                nkc = nr // 128
                src_ap = win[src][l, r0:r0 + nr, c0:c0 + ncol].rearrange("(c p) n -> p c n", p=128)
                dst_ap = wb[l][bi, :, kc0 * ncol:(kc0 + nkc) * ncol].rearrange("p (c n) -> p c n", n=ncol)
                pool.dma(dst_ap, src_ap, [NOB], [], wbB[l])
                kc0 += nkc
        wbB[l].w = {wbB[l].dsem: wbB[l].dtotal}

    A = Arena(K, 207 * 1024)
    cv = A.carve
    wsl = xt = xtB = xtJ = vB = vBB = sgB = sgBB = vP = vPB = cw = cwB = wmT = wmTB = sbrb = sbrbB = onesr = onesrB = fw = fwB = idb = idbB = trib = tribB = pmb = pmbB = onesf = onesfB = cstf = cstfB = kmean = kmeanB = Gs = GsB = ahalo = ahaloB = fhalo = fhaloB = ss = ssB = rstd = rstdB = ssq = ssqB = top8 = top8B = bst = bstB = bag = bagB = kms = kmsB = rinv = rinvB = junk = junkB = ptmp = ptmpB = R0 = xnT = xnTB = xnTC = o = qs = qsB = qsH = ks = ksB = ksH = vs = vsB = vsJ = o2 = aext = aextB = o3 = cacc = caccB = caccM = sgc = sgcB = o4 = sus = susB = svn = svnB = acta = actaB = ybin = ybinB = o5 = lnm = lnmB = lnr = lnrB = csq = csqB = svg = svgB = o6 = atf = atfB = o7 = sel = acc = atk = o8 = kbuf = vbuf = ptb = o9 = xnb = xnbB = xnbJ = rp = rpB = rt1 = rt1B = rt2 = rt2B = mrg = mrgB = mrgC = sga = sgaB = mt1 = mt1B = mt2 = mt2B = hbm = hbmB = hbmJ = ae = aeB = aeM = fca = fcaB = fcaM = fga = fgaB = fgaM = gff = gffB = gffC = hbf = hbfB = hbfJ = pt = ptB = ptb16 = ptb16B = pT = pTB = gsb = gsbB = wmf = wmfB = sbr = sbrB = None

    def declare():
        nonlocal wsl, xt, xtB, xtJ, vB, vBB, sgB, sgBB, vP, vPB, cw, cwB, wmT, wmTB, sbrb, sbrbB, onesr, onesrB, fw, fwB, idb, idbB, trib, tribB, pmb, pmbB, onesf, onesfB, cstf, cstfB, kmean, kmeanB, Gs, GsB, ahalo, ahaloB, fhalo, fhaloB, ss, ssB, rstd, rstdB, ssq, ssqB, top8, top8B, bst, bstB, bag, bagB, kms, kmsB, rinv, rinvB, junk, junkB, ptmp, ptmpB, R0, xnT, xnTB, xnTC, o, qs, qsB, qsH, ks, ksB, ksH, vs, vsB, vsJ, o2, aext, aextB, o3, cacc, caccB, caccM, sgc, sgcB, o4, sus, susB, svn, svnB, acta, actaB, ybin, ybinB, o5, lnm, lnmB, lnr, lnrB, csq, csqB, svg, svgB, o6, atf, atfB, o7, sel, acc, atk, o8, kbuf, vbuf, ptb, o9, xnb, xnbB, xnbJ, rp, rpB, rt1, rt1B, rt2, rt2B, mrg, mrgB, mrgC, sga, sgaB, mt1, mt1B, mt2, mt2B, hbm, hbmB, hbmJ, ae, aeB, aeM, fca, fcaB, fcaM, fga, fgaB, fgaM, gff, gffB, gffC, hbf, hbfB, hbfJ, pt, ptB, ptb16, ptb16B, pT, pTB, gsb, gsbB, wmf, wmfB, sbr, sbrB
        A.begin()
        wsl = [cv("wsl%d" % i, [128, 16 * 512], BF16) for i in range(2)]
        xt, xtB = cv("xt", [128, NJ, D], F32)
        xtJ = [A.sub(xtB, "xt%d" % j, j * D * 4, D * 4) for j in range(NJ)]
        vB, vBB = cv("vB", [128, D], F32)
        sgB, sgBB = cv("sgB", [128, 2, SW], F32)
        vP, vPB = cv("vP", [128, 3, 16], F32)
        cw, cwB = cv("cw", [128, 4, CK + 3], F32)
        wmT, wmTB = cv("wmT", [128, 4, 128], BF16)
        sbrb, sbrbB = cv("sbrb", [1, 4 * 128], BF16)
        onesr, onesrB = cv("onesr", [1, 128], BF16)
        fw, fwB = cv("fw", [128, 44, 4], F32)
        idb, idbB = cv("idb", [128, 128], BF16)
        trib, tribB = cv("trib", [128, 128], BF16)
        pmb, pmbB = cv("pmb", [32, 32], BF16)
        onesf, onesfB = cv("onesf", [128, 128], F32)
        cstf, cstfB = cv("cstf", [128, 3, 128], F32)
        kmean, kmeanB = cv("kmean", [128, NH, 64], BF16)
        Gs, GsB = cv("Gs", [128, NJ, 64], F32)
        ahalo, ahaloB = cv("ahalo", [128, 4, CK - 1], F32)
        fhalo, fhaloB = cv("fhalo", [128, 44, 2], F32)
        ss, ssB = cv("ss", [128, 8], F32)
        rstd, rstdB = cv("rstd", [128, 8], F32)
        ssq, ssqB = cv("ssq", [128, NJ, 4], F32)
        top8, top8B = cv("top8", [128, 8], F32)
        bst, bstB = cv("bst", [128, 6], F32)
        bag, bagB = cv("bag", [128, 2], F32)
        kms, kmsB = cv("kms", [128, 2], F32)
        rinv, rinvB = cv("rinv", [128, NJ], F32)
        junk, junkB = cv("junk", [128, D], BF16)
        ptmp, ptmpB = cv("ptmp", [128, T], F32)
        R0 = A.top
        xnT, xnTB = cv("xnT", [128, 16, T], BF16, at=R0)
        xnTC = [A.sub(xnTB, "xnT%d" % c, c * T * 2, T * 2) for c in range(16)]
        o = R0 + 16384
        qs, qsB = cv("qs", [128, NH, T], BF16, at=o)
        qsH = [A.sub(qsB, "qs%d" % h, h * T * 2, T * 2) for h in range(NH)]
        ks, ksB = cv("ks", [128, NH, T], BF16, at=o + 8192)
        ksH = [A.sub(ksB, "ks%d" % h, h * T * 2, T * 2) for h in range(NH)]
        vs, vsB = cv("vs", [128, NJ, NH, HD + 1], BF16, at=o + 16384)
        vsJ = [A.sub(vsB, "vs%d" % j, j * NH * (HD + 1) * 2, NH * (HD + 1) * 2) for j in range(NJ)]
        o2 = o + 16384 + 8256
        aext, aextB = cv("aext", [128, 4, CK - 1 + T], F32, at=o2)
        o3 = o2 + 8672
        cacc, caccB = cv("cacc", [128, 4, T], F32, at=o3)
        caccM = [A.sub(caccB, "cacc%d" % m, m * T * 4, T * 4) for m in range(4)]
        sgc, sgcB = cv("sgc", [128, 4, T], F32, at=o3)
        o4 = o3 + 8192
        sus, susB = cv("sus", [128, 4, T], BF16, at=o4)
        svn, svnB = cv("svn", [128, NJ, SW], BF16, at=o4 + 4096)
        acta, actaB = cv("acta", [128, 4, T], BF16, at=o4 + 8192)
        ybin, ybinB = cv("ybin", [128, 4, T], BF16, at=o4 + 12288)
        o5 = o4 + 16384
        lnm, lnmB = cv("lnm", [128, T], F32, at=o5)
        lnr, lnrB = cv("lnr", [128, T], F32, at=o5 + 2048)
        csq, csqB = cv("csq", [128, T], F32, at=o5 + 4096)
        svg, svgB = cv("svg", [128, SW], F32, at=o5 + 4096)
        o6 = o5 + 6144
        atf, atfB = cv("atf", [128, NH, T], BF16, at=o6)
        o7 = o6 + 8192
        sel = [cv("sel%d" % i, [128, NJ, 64], F32, at=o7 + i * 1024) for i in range(2)]
        acc = [cv("acc%d" % i, [128, NJ, HD + 1], F32, at=o7 + 2048 + i * 2080) for i in range(2)]
        atk = [cv("atk%d" % i, [128, NJ, HD], BF16, at=o7 + 6208 + i * 1024) for i in range(2)]
        o8 = o7 + 8256
        kbuf = [cv("kbuf%d" % i, [128, GRP * BLK], BF16, at=o8 + i * 2048) for i in range(NKB)]
        vbuf = [cv("vbuf%d" % i, [128, GRP * 2, HD + 1], BF16, at=o8 + 6144 + i * 2080) for i in range(NKB)]
        ptb = [cv("ptb%d" % i, [128, T], BF16, at=o8 + 12384 + i * 1024) for i in range(NPT)]
        o9 = o8 + 12384 + 4096
        xnb, xnbB = cv("xnb", [128, NJ, D], BF16, at=o7)
        xnbJ = [A.sub(xnbB, "xnb%d" % j, j * D * 2, D * 2) for j in range(NJ)]
        rp, rpB = cv("rp", [ROPE, 2, T], F32, at=o7 + 16384)
        rt1, rt1B = cv("rt1", [ROPE, T], F32, at=o7 + 16384 + 4096)
        rt2, rt2B = cv("rt2", [ROPE, T], F32, at=o7 + 16384 + 6144)
        assert o7 + 16384 + 8192 <= o9, (o7, o9)
        mrg, mrgB = cv("mrg", [128, 16, T], BF16, at=o)
        mrgC = [A.sub(mrgB, "mrg%d" % c, c * T * 2, T * 2) for c in range(16)]
        sga, sgaB = cv("sga", [128, T], F32, at=o + 16384)
        mt1, mt1B = cv("mt1", [128, T], F32, at=o + 16384 + 2048)
        mt2, mt2B = cv("mt2", [128, T], F32, at=o + 16384 + 4096)
        hbm, hbmB = cv("hbm", [128, NJ, D], F32, at=o2)
        hbmJ = [A.sub(hbmB, "hbm%d" % j, j * D * 4, D * 4) for j in range(NJ)]
        assert o2 + 32768 <= o9
        ae, aeB = cv("ae", [128, 4, T + 2], F32, at=o)
        aeM = [A.sub(aeB, "ae%d" % m, m * (T + 2) * 4, (T + 2) * 4) for m in range(4)]
        fca, fcaB = cv("fca", [128, 4, T], F32, at=o + 8224)
        fcaM = [A.sub(fcaB, "fca%d" % m, m * T * 4, T * 4) for m in range(4)]
        fga, fgaB = cv("fga", [128, 4, T], BF16, at=o + 16416)
        fgaM = [A.sub(fgaB, "fga%d" % m, m * T * 2, T * 2) for m in range(4)]
        gff, gffB = cv("gff", [128, 44, T], BF16, at=o + 20512)
        gffC = [A.sub(gffB, "gff%d" % c, c * T * 2, T * 2) for c in range(44)]
        assert o + 20512 + 45056 <= o7, (o + 20512 + 45056, o7)
        hbf, hbfB = cv("hbf", [128, NJ, D], F32, at=R0)
        hbfJ = [A.sub(hbfB, "hbf%d" % j, j * D * 4, D * 4) for j in range(NJ)]
        assert R0 + 32768 <= o + 20512
        pt, ptB = cv("pt", [128, NJ, PLE], F32, at=o)
        ptb16, ptb16B = cv("ptb16", [128, NJ, PLE], BF16, at=o + 4096)
        pT, pTB = cv("pT", [128, 2, T], BF16, at=o + 6144)
        gsb, gsbB = cv("gsb", [128, T], F32, at=o + 8192)
        wmf, wmfB = cv("wmf", [128, 4, 128], F32, at=o9 - 4096)
        sbr, sbrB = cv("sbr", [1, 4 * 128], F32, at=o9 - 2048)
        assert o9 <= A.n, (o9, A.n)

    declare()
    A.finalize()

    bankB = []
    for i in range(8):
        bb = Buf("bank%d" % i)
        bb.excl = True
        bankB.append(bb)
    bank = None
    pes = [None]

    def declare_banks():
        nonlocal bank
        if pes[0] is not None:
            pes[0].close()
        pes[0] = ExitStack()
        bank = [(pes[0].enter_context(nc.psum_tensor("bank%d_g%d" % (i, A.gen), [128, 512], F32)), bankB[i]) for i in range(8)]
    declare_banks()

    def bk(i):
        return bank[i][0]

    def bkB(i):
        return bank[i][1]

    sp.dma(cstf, cst[:, :, :], [NOB], [cstfB], cstfB)
    dve.op(lambda e: e.tensor_copy(idb, cstf[:, 0, :]), [cstfB], [idbB])
    dve.op(lambda e: e.tensor_copy(trib, cstf[:, 1, :]), [cstfB], [tribB])
    dve.op(lambda e: e.tensor_copy(pmb, cstf[0:32, 2, 0:32]), [cstfB], [pmbB])
    pool.op(lambda e: e.memset(onesf, 1.0), [], [onesfB])
    pool.op(lambda e: e.memset(onesr, 1.0), [], [onesrB])

    wstate = {"g": 0, "issued": 0}
    NSLOT = 2
    wseq = [(l, bi) for l in range(depth) for ti in range(NT) for bi in range(NWB)]

    def w_issue():
        g = wstate["issued"]
        l, bi = wseq[g]
        s = g % NSLOT
        ncol = WBLKS[bi][1]
        n = sum(p[2] for p in WBLKS[bi][2]) // 128 * ncol
        sp.dma(wsl[s][0][:, 0:n], wb[l][bi, :, 0:n], [wbB[l]], [wsl[s][1]], wsl[s][1])
        wstate["issued"] += 1

    def w_next(expect_tag):
        g = wstate["g"]
        l, bi = wseq[g]
        assert WBLKS[bi][0] == expect_tag, (WBLKS[bi][0], expect_tag)
        while wstate["issued"] < min(g + NSLOT, len(wseq)):
            w_issue()
        wstate["g"] += 1
        t, b = wsl[g % NSLOT]
        ncol = WBLKS[bi][1]
        return t.rearrange("p (c n) -> p c n", n=ncol), b

    evq = {"i": 0}

    def ev_eng():
        evq["i"] += 1
        return act if evq["i"] % 2 else dve

    def rmsnorm_to_xnT(gidx):
        for j in range(NJ):
            act.op(lambda e, j=j: e.activation(out=junk, in_=xt[:, j, :], func=AF.Square, accum_out=ss[:, j:j + 1]),
                   [xtJ[j]], [junkB, ssB])
        act.op(lambda e: e.activation(out=rstd[:, 0:NJ], in_=ss[:, 0:NJ], func=AF.Sqrt, scale=1.0 / D, bias=EPS),
               [ssB], [rstdB])
        dve.op(lambda e: e.reciprocal(rstd[:, 0:NJ], rstd[:, 0:NJ]), [rstdB], [rstdB])
        for j in range(NJ):
            if j % 2 == 0:
                dve.op(lambda e, j=j: e.tensor_scalar(xnb[:, j, :], xt[:, j, :], rstd[:, j:j + 1], None, op0=ALU.mult),
                       [xtJ[j], rstdB], [xnbJ[j]])
            else:
                act.op(lambda e, j=j: e.activation(out=xnb[:, j, :], in_=xt[:, j, :], func=AF.Copy, scale=rstd[:, j:j + 1]),
                       [xtJ[j], rstdB], [xnbJ[j]])
        for c in range(16):
            b = 4 + (c % 2)
            pv = bk(b)[:].bitcast(BF16)
            for j in range(NJ):
                pe.op(lambda e, j=j, c=c, pv=pv: e.transpose(pv[:, j * 128:(j + 1) * 128], xnb[:, j, c * 128:(c + 1) * 128], idb),
                      [xnbJ[j], idbB], [bkB(b)], signal=(j == NJ - 1))
            eng = ev_eng()
            if eng is dve:
                dve.op(lambda e, c=c, pv=pv: e.tensor_scalar(xnT[:, c, :], pv[:, 0:T], vP[:, gidx, c:c + 1], None, op0=ALU.mult),
                       [bkB(b), vPB], [xnTC[c]])
            else:
                act.op(lambda e, c=c, pv=pv: e.activation(out=xnT[:, c, :], in_=pv[:, 0:T], func=AF.Copy, scale=vP[:, gidx, c:c + 1]),
                       [bkB(b), vPB], [xnTC[c]])

    mmq = {"i": 0}

    def mm_bank():
        mmq["i"] += 1
        return mmq["i"] % 4

    def _sb(srcB, c):
        return [srcB[c]] if isinstance(srcB, list) else [srcB]

    def fm_chunk(wt, wB, m, src, srcB, nkc=16, kc0=0, wcol=128):
        b = mm_bank()
        for c in range(nkc):
            pe.op(lambda e, c=c, b=b: e.matmul(bk(b)[:], wt[:, kc0 + c, m * 128:(m + 1) * 128], src[:, c, :],
                                               start=(c == 0), stop=(c == nkc - 1)),
                  [wB] + _sb(srcB, c), [bkB(b)], signal=(c == nkc - 1))
        return b

    def tm_block(wt, wB, j, src, srcB, nkc=16, b=None, first=True, last=True, c_off=0):
        if b is None:
            b = mm_bank()
        for c in range(nkc):
            pe.op(lambda e, c=c, b=b: e.matmul(bk(b)[:], src[:, c_off + c, j * 128:(j + 1) * 128], wt[:, c, :],
                                               start=(first and c == 0), stop=(last and c == nkc - 1)),
                  [wB] + _sb(srcB, c_off + c), [bkB(b)], signal=(c == nkc - 1))
        return b

    def post_norm_residual(l, gi, hb, hbJ):
        sp.dma(vB, vecB[l, gi], [NOB], [vBB], vBB)
        dve.op(lambda e: e.tensor_reduce(out=ss[:, 4:8], in_=ssq, axis=AX.X, op=ALU.add), [ssqB], [ssB])
        act.op(lambda e: e.activation(out=rstd[:, 4:8], in_=ss[:, 4:8], func=AF.Sqrt, scale=1.0 / D, bias=EPS), [ssB], [rstdB])
        dve.op(lambda e: e.reciprocal(rstd[:, 4:8], rstd[:, 4:8]), [rstdB], [rstdB])
        for j in range(NJ):
            if j % 2 == 0:
                dve.op(lambda e, j=j: e.tensor_tensor(hb[:, j, :], hb[:, j, :], vB, ALU.mult), [hbJ[j], vBB], [hbJ[j]])
                dve.op(lambda e, j=j: e.scalar_tensor_tensor(xt[:, j, :], hb[:, j, :], rstd[:, 4 + j:5 + j], xt[:, j, :], ALU.mult, ALU.add),
                       [hbJ[j], rstdB, xtJ[j]], [xtJ[j]])
            else:
                pool.op(lambda e, j=j: e.tensor_tensor(hb[:, j, :], hb[:, j, :], vB, ALU.mult), [hbJ[j], vBB], [hbJ[j]])
                pool.op(lambda e, j=j: e.tensor_scalar(hb[:, j, :], hb[:, j, :], rstd[:, 4 + j:5 + j], None, op0=ALU.mult), [hbJ[j], rstdB], [hbJ[j]])
                pool.op(lambda e, j=j: e.tensor_tensor(xt[:, j, :], xt[:, j, :], hb[:, j, :], ALU.add), [hbJ[j], xtJ[j]], [xtJ[j]])

    def tm_evac(b, j, nb, hb, hbJ):
        act.op(lambda e: e.activation(out=junk[:, 0:512], in_=bk(b)[:], func=AF.Square, accum_out=ssq[:, j, nb:nb + 1]),
               [bkB(b)], [junkB, ssqB])
        dve.op(lambda e: e.tensor_copy(hb[:, j, nb * 512:(nb + 1) * 512], bk(b)[:]), [bkB(b)], [hbJ[j]])

    cnt = {"kv": 0, "pt": 0, "st": 0, "o": 0}

    def attention(l, ti):
        a0 = 2 * ti
        passes = []

        def gate_pre(h):
            def f():
                hs = h % 2
                sel_t, sel_B = sel[hs]
                acc_t, acc_B = acc[hs]
                for j in range(NJ):
                    own = a0 + j // 2
                    if own == 0:
                        continue
                    pe.op(lambda e, j=j: e.matmul(bk(6)[:, j * 64:(j + 1) * 64], qs[:, h, j * 128:(j + 1) * 128], kmean[:, h, :], start=True, stop=True),
                          [qsH[h], kmeanB], [bkB(6)])
                if a0 + 1 > 0:
                    j0 = 0 if a0 > 0 else 2
                    own_max = a0 + 1
                    for j in range(j0, NJ):
                        own = a0 + j // 2
                        dve.op(lambda e, j=j, own=own: e.tensor_copy(Gs[:, j, 0:own], bk(6)[:, j * 64:j * 64 + own]), [bkB(6)], [GsB])
                        if own <= TOPK:
                            pool.op(lambda e, j=j, own=own: e.memset(sel_t[:, j, 0:own], 1.0), [], [sel_B])
                        else:
                            dve.op(lambda e, j=j: e.max(out=top8, in_=Gs[:, j, :]), [GsB], [top8B])
                            dve.op(lambda e, j=j, own=own: e.tensor_scalar(sel_t[:, j, 0:own], Gs[:, j, 0:own], top8[:, TOPK - 1:TOPK], None, op0=ALU.is_ge),
                                   [GsB, top8B], [sel_B])
                pool.op(lambda e: e.memset(acc_t, 0.0), [], [acc_B])
            return f

        def head_post(h):
            def f():
                hs = h % 2
                acc_t, acc_B = acc[hs]
                atk_t, atk_B = atk[hs]
                dve.op(lambda e: e.reciprocal(rinv, acc_t[:, :, HD]), [acc_B], [rinvB])
                for j in range(NJ):
                    eng = pool if j % 2 else dve
                    eng.op(lambda e, j=j: e.tensor_scalar(atk_t[:, j, :], acc_t[:, j, 0:HD], rinv[:, j:j + 1], None, op0=ALU.mult), [acc_B, rinvB], [atk_B])
                pv = bk(7)[:].bitcast(BF16)
                for j in range(NJ):
                    pe.op(lambda e, j=j: e.transpose(pv[:, j * 128:(j + 1) * 128], atk_t[:, j, :], idb), [atk_B, idbB], [bkB(7)], signal=(j == NJ - 1))
                if h % 2:
                    act.op(lambda e: e.activation(out=atf[:, h, :], in_=pv[:, 0:T], func=AF.Copy), [bkB(7)], [atfB])
                else:
                    dve.op(lambda e: e.tensor_copy(atf[:, h, :], pv[:, 0:T]), [bkB(7)], [atfB])
            return f

        for h in range(NH):
            hp = []
            for g0 in range(0, a0, GRP):
                nblk = min(GRP, a0 - g0)

                def load(h=h, g0=g0, nblk=nblk):
                    bi = cnt["kv"] % NKB
                    cnt["kv"] += 1
                    kb_t, kb_B = kbuf[bi]
                    vb_t, vb_B = vbuf[bi]
                    pool.op(lambda e: e.memset(vb_t[:, :, HD:HD + 1], 1.0), [], [vb_B])
                    sp.dma(kb_t[:, 0:nblk * BLK], kT[l][h, :, g0 * BLK:(g0 + nblk) * BLK], [kTB[l]], [kb_B], kb_B)
                    sp.dma(vb_t[:, 0:nblk * 2, 0:HD], vS[l][g0 * BLK:(g0 + nblk) * BLK, h, :].rearrange("(k p) d -> p k d", p=128), [vSB[l]], [vb_B], vb_B)
                    return kb_t, kb_B, vb_t, vb_B
                holder = {}
                for n in range(nblk):
                    def kts(holder=holder, n=n, load=load):
                        if "b" not in holder:
                            holder["b"] = load()
                        kb_t, kb_B, vb_t, vb_B = holder["b"]
                        return [(kb_t[:, (n * 2 + kk) * 128:(n * 2 + kk + 1) * 128], kb_B, vb_t[:, n * 2 + kk, :], vb_B) for kk in range(2)]
                    hp.append(dict(h=h, kts=kts, jl=[0, 1, 2, 3], selcol=g0 + n, diag=None))
            hp.append(dict(h=h, kts=(lambda h=h: [(ks[:, h, kk * 128:(kk + 1) * 128], ksH[h], vs[:, kk, h, :], vsJ[kk]) for kk in range(2)]),
                           jl=[2, 3], selcol=a0, diag=None))
            for mb in range(2):
                hp.append(dict(h=h, kts=(lambda h=h, mb=mb: [(ks[:, h, (2 * mb + kk) * 128:(2 * mb + kk + 1) * 128], ksH[h], vs[:, 2 * mb + kk, h, :], vsJ[2 * mb + kk]) for kk in range(2)]),
                               jl=[2 * mb, 2 * mb + 1], selcol=None, diag={0: 2 * mb, 1: 2 * mb + 1}))
            hp[0]["pre"] = gate_pre(h)
            hp[-1]["post"] = head_post(h)
            passes.extend(hp)

        def stageA(P):
            if "pre" in P:
                P["pre"]()
            h = P["h"]
            kt_list = P["kts"]()
            P["ktl"] = kt_list
            P["pts"] = []
            P["plan"] = []
            for ki, (k_ap, kB_, v_ap, vB_) in enumerate(kt_list):
                js = [j for j in P["jl"] if (P["diag"] is None or j >= P["diag"][ki])]
                P["plan"].append(js)
                q0, q1 = js[0] * 128, (js[-1] + 1) * 128
                sb_ = SBANKS[cnt["st"] % 4]
                cnt["st"] += 1
                pe.op(lambda e, sb_=sb_, k_ap=k_ap, q0=q0, q1=q1: e.matmul(bk(sb_)[:, q0:q1], k_ap, qs[:, h, q0:q1], start=True, stop=True),
                      [kB_, qsH[h]], [bkB(sb_)])
                pi = cnt["pt"] % NPT
                cnt["pt"] += 1
                pt_t, pt_B = ptb[pi]
                act.op(lambda e, sb_=sb_, pt_t=pt_t, q0=q0, q1=q1: e.activation(out=pt_t[:, q0:q1], in_=bk(sb_)[:, q0:q1], func=AF.Exp),
                       [bkB(sb_)], [pt_B])
                if P["diag"] is not None:
                    jd = P["diag"][ki]
                    pool.op(lambda e, pt_t=pt_t, jd=jd: e.tensor_tensor(pt_t[:, jd * 128:(jd + 1) * 128], pt_t[:, jd * 128:(jd + 1) * 128], trib, ALU.mult),
                            [pt_B, tribB], [pt_B])
                P["pts"].append((pt_t, pt_B))

        def stageB(P):
            h = P["h"]
            hs = h % 2
            sel_t, sel_B = sel[hs]
            acc_t, acc_B = acc[hs]
            oset = cnt["o"] % 2
            cnt["o"] += 1
            jl = P["jl"]
            ob = {j: (2 + oset * 2 + (j // 2), (j % 2) * (HD + 1)) for j in jl}
            for j in jl:
                kis = [ki for ki in range(len(P["ktl"])) if j in P["plan"][ki]]
                b_, oo = ob[j]
                lastj = (j == jl[-1]) or (ob[jl[jl.index(j) + 1]][0] != b_)
                for idx, ki in enumerate(kis):
                    pt_t, pt_B = P["pts"][ki]
                    _, _, v_ap, vB_ = P["ktl"][ki]
                    pe.op(lambda e, j=j, b_=b_, oo=oo, pt_t=pt_t, v_ap=v_ap, idx=idx, kis=kis: e.matmul(bk(b_)[:, oo:oo + HD + 1], pt_t[:, j * 128:(j + 1) * 128], v_ap,
                                                                                               start=(idx == 0), stop=(idx == len(kis) - 1)),
                          [pt_B, vB_], [bkB(b_)], signal=(lastj and idx == len(kis) - 1))
            for j in jl:
                b_, oo = ob[j]
                if P["selcol"] is None:
                    dve.op(lambda e, j=j, b_=b_, oo=oo: e.tensor_tensor(acc_t[:, j, :], acc_t[:, j, :], bk(b_)[:, oo:oo + HD + 1], ALU.add),
                           [bkB(b_), acc_B], [acc_B])
                else:
                    sc = P["selcol"]
                    dve.op(lambda e, j=j, b_=b_, oo=oo, sc=sc: e.scalar_tensor_tensor(acc_t[:, j, :], bk(b_)[:, oo:oo + HD + 1], sel_t[:, j, sc:sc + 1], acc_t[:, j, :], ALU.mult, ALU.add),
                           [bkB(b_), sel_B, acc_B], [acc_B])
            if "post" in P:
                P["post"]()

        stageA(passes[0])
        for i in range(len(passes)):
            if i + 1 < len(passes):
                stageA(passes[i + 1])
            stageB(passes[i])

    for l in range(depth):
        x_src, x_srcB = (x_in, NOB) if l == 0 else (xs[l - 1], xsB[l - 1])
        x_dst, x_dstB = (out, outB) if l == depth - 1 else (xs[l], xsB[l])
        sp.dma(vP, vecP[l], [NOB], [vPB], vPB)
        sp.dma(cw, convw[l], [NOB], [cwB], cwB)
        sp.dma(sgB, sguB[l].rearrange("a p n -> p a n"), [NOB], [sgBB], sgBB)
        sp.dma(wmf, sguwT[l], [NOB], [wmfB], wmfB)
        sp.dma(sbr, sgub[l], [NOB], [sbrB], sbrB)
        sp.dma(fw, ffnw[l], [NOB], [fwB], fwB)
        for g in range(4):
            dve.op(lambda e, g=g: e.tensor_tensor(wmT[:, g, :], wmf[:, g, :], cstf[:, 1, :], ALU.mult), [wmfB, cstfB], [wmTB])
        dve.op(lambda e: e.tensor_copy(sbrb, sbr), [sbrB], [sbrbB])
        pool.op(lambda e: e.memset(ahalo, 0.0), [], [ahaloB])
        pool.op(lambda e: e.memset(fhalo, 0.0), [], [fhaloB])
        pool.op(lambda e: e.memset(Gs, NEG), [], [GsB])

        for ti in range(NT):
            t0 = ti * T
            declare()
            declare_banks()
            sp.dma(xt, x_src[t0:t0 + T, :].rearrange("(j p) d -> p j d", p=128), [x_srcB], [xtB], xtB)
            rmsnorm_to_xnT(0)
            sp.dma(rp, ropet[:, :, t0:t0 + T], [NOB], [rpB], rpB)

            for qi in range(4):
                wt, wB = w_next("P%d" % qi)
                isq = qi < 2
                for m in range(4):
                    h = (qi % 2) * 4 + m
                    dst, dstH = (qs, qsH) if isq else (ks, ksH)
                    b = fm_chunk(wt, wB, m, xnT, xnTC)
                    sc = HD ** -0.5 if isq else 1.0
                    act.op(lambda e, b=b, h=h, dst=dst, sc=sc: e.activation(out=dst[:, h, :], in_=bk(b)[:], func=AF.Copy, scale=sc),
                           [bkB(b)], [dstH[h]])
                    pe.op(lambda e, h=h, dst=dst: e.matmul(bk(5)[0:32, :], pmb, dst[0:32, h, :], start=True, stop=True),
                          [pmbB, dstH[h]], [bkB(5)])
                    dve.op(lambda e: e.tensor_tensor(rt2, bk(5)[0:32, :], rp[:, 1, :], ALU.mult), [bkB(5), rpB], [rt2B])
                    pool.op(lambda e, h=h, dst=dst: e.tensor_tensor(rt1, dst[0:32, h, :], rp[:, 0, :], ALU.mult), [dstH[h], rpB], [rt1B])
                    dve.op(lambda e, h=h, dst=dst: e.tensor_tensor(dst[0:32, h, :], rt1, rt2, ALU.add), [rt1B, rt2B], [dstH[h]])
                    if not isq:
                        for half in range(2):
                            nblk = 2 * ti + half
                            dve.op(lambda e, h=h, half=half: e.tensor_reduce(out=kms[:, half:half + 1], in_=ks[:, h, half * BLK:(half + 1) * BLK],
                                                                             axis=AX.X, op=ALU.add), [ksH[h]], [kmsB])
                            dve.op(lambda e, h=h, half=half, nblk=nblk: e.tensor_scalar(kmean[:, h, nblk:nblk + 1], kms[:, half:half + 1], 1.0 / BLK, None, op0=ALU.mult),
                                   [kmsB], [kmeanB])
            act.dma(kT[l][:, :, t0:t0 + T].rearrange("h d t -> d h t"), ks, [ksB], [kTB[l]], ksB)
            pool.op(lambda e: e.memset(vs[:, :, :, HD:HD + 1], 1.0), [], [vsB])
            for vi in range(2):
                wt, wB = w_next("P%d" % (4 + vi))
                for j in range(NJ):
                    b = tm_block(wt, wB, j, xnT, xnTC)
                    eng = ev_eng()
                    src_v = bk(b)[:].rearrange("p (h d) -> p h d", h=4)
                    if eng is dve:
                        dve.op(lambda e, j=j, vi=vi, src_v=src_v: e.tensor_copy(vs[:, j, vi * 4:(vi + 1) * 4, 0:HD], src_v), [bkB(b)], [vsJ[j]])
                    else:
                        act.op(lambda e, j=j, vi=vi, src_v=src_v: e.activation(out=vs[:, j, vi * 4:(vi + 1) * 4, 0:HD], in_=src_v, func=AF.Copy), [bkB(b)], [vsJ[j]])
            for j in range(NJ):
                act.dma(vS[l][t0 + j * 128:t0 + (j + 1) * 128, :, :], vs[:, j, :, 0:HD], [vsB], [vSB[l]], vsB)
            wt, wB = w_next("P6")
            for m in range(4):
                b = fm_chunk(wt, wB, m, xnT, xnTC)
                act.op(lambda e, b=b, m=m: e.activation(out=sgc[:, m, :], in_=bk(b)[:], func=AF.Sigmoid), [bkB(b)], [sgcB])
            pool.op(lambda e: e.tensor_copy(aext[:, :, 0:CK - 1], ahalo), [ahaloB], [aextB])
            wt, wB = w_next("P7")
            for m in range(4):
                b = fm_chunk(wt, wB, m, xnT, xnTC)
                dve.op(lambda e, b=b, m=m: e.tensor_tensor(aext[:, m, CK - 1:CK - 1 + T], bk(b)[:], sgc[:, m, :], ALU.mult), [bkB(b), sgcB], [aextB])
            pool.op(lambda e: e.tensor_copy(ahalo, aext[:, :, T:T + CK - 1]), [aextB], [ahaloB])
            wt, wB = w_next("P8")
            for m in range(4):
                b = fm_chunk(wt, wB, m, xnT, xnTC)
                act.op(lambda e, b=b, m=m: e.activation(out=sus[:, m, :], in_=bk(b)[:], func=AF.Gelu_apprx_tanh), [bkB(b)], [susB])
            wt, wB = w_next("P9")
            for j in range(NJ):
                b = tm_block(wt, wB, j, xnT, xnTC)
                act.op(lambda e, b=b: e.activation(out=svg, in_=bk(b)[:], func=AF.Gelu_apprx_tanh), [bkB(b)], [svgB])
                dve.op(lambda e: e.bn_stats(bst, svg), [svgB], [bstB])
                dve.op(lambda e: e.bn_aggr(bag, bst), [bstB], [bagB])
                act.op(lambda e: e.activation(out=bag[:, 1:2], in_=bag[:, 1:2], func=AF.Sqrt, bias=EPS), [bagB], [bagB])
                dve.op(lambda e: e.reciprocal(bag[:, 1:2], bag[:, 1:2]), [bagB], [bagB])
                dve.op(lambda e: e.tensor_scalar(svg, svg, bag[:, 0:1], bag[:, 1:2], op0=ALU.subtract, op1=ALU.mult), [svgB, bagB], [svgB])
                dve.op(lambda e: e.tensor_tensor(svg, svg, sgB[:, 0, :], ALU.mult), [svgB, sgBB], [svgB])
                dve.op(lambda e, j=j: e.tensor_tensor(svn[:, j, :], svg, sgB[:, 1, :], ALU.add), [svgB, sgBB], [svnB])

            for m in (2, 3):
                pool.op(lambda e, m=m: e.tensor_scalar(cacc[:, m, :], aext[:, m, 0:T], cw[:, m, 0:1], cw[:, m, CK:CK + 1], op0=ALU.mult, op1=ALU.add),
                        [aextB, cwB], [caccM[m]])
                for k in range(1, CK):
                    pool.op(lambda e, m=m, k=k: e.tensor_scalar(ptmp, aext[:, m, k:k + T], cw[:, m, k:k + 1], None, op0=ALU.mult), [aextB, cwB], [ptmpB])
                    pool.op(lambda e, m=m: e.tensor_tensor(cacc[:, m, :], cacc[:, m, :], ptmp, ALU.add), [ptmpB, caccM[m]], [caccM[m]])
            conv_dve = []
            for m in (0, 1):
                conv_dve.append((lambda m=m: dve.op(lambda e: e.tensor_scalar(cacc[:, m, :], aext[:, m, 0:T], cw[:, m, 0:1], cw[:, m, CK:CK + 1], op0=ALU.mult, op1=ALU.add),
                                                    [aextB, cwB], [caccM[m]])))
                for k in range(1, CK):
                    conv_dve.append((lambda m=m, k=k: dve.op(lambda e: e.scalar_tensor_tensor(cacc[:, m, :], aext[:, m, k:k + T], cw[:, m, k:k + 1], cacc[:, m, :], ALU.mult, ALU.add),
                                                             [aextB, cwB, caccM[m]], [caccM[m]])))
            for f_ in conv_dve:
                f_()

            for g in range(4):
                for j in range(NJ):
                    pe.op(lambda e, g=g, j=j: e.matmul(bk(5)[:, j * 128:(j + 1) * 128], svn[:, j, g * 128:(g + 1) * 128], wmT[:, g, :], start=True, stop=False),
                          [svnB, wmTB], [bkB(5)], signal=False)
                    pe.op(lambda e, g=g, j=j: e.matmul(bk(5)[:, j * 128:(j + 1) * 128], onesr, sbrb[:, g * 128:(g + 1) * 128], start=False, stop=True),
                          [onesrB, sbrbB], [bkB(5)], signal=(j == NJ - 1))
                dve.op(lambda e, g=g: e.tensor_tensor(ybin[:, g, :], bk(5)[:], sus[:, g, :], ALU.mult), [bkB(5), susB], [ybinB])

            attention(l, ti)

            for m in range(4):
                act.op(lambda e, m=m: e.activation(out=csq, in_=cacc[:, m, :], func=AF.Square), [caccB], [csqB])
                pe.op(lambda e, m=m: e.matmul(bk(4)[:], onesf, cacc[:, m, :], start=(m == 0), stop=(m == 3)), [onesfB, caccB], [bkB(4)], signal=(m == 3))
                pe.op(lambda e, m=m: e.matmul(bk(5)[:], onesf, csq, start=(m == 0), stop=(m == 3)), [onesfB, csqB], [bkB(5)], signal=True)
            dve.op(lambda e: e.tensor_scalar(lnm, bk(4)[:], 1.0 / CC, None, op0=ALU.mult), [bkB(4)], [lnmB])
            dve.op(lambda e: e.tensor_tensor(csq, lnm, lnm, ALU.mult), [lnmB], [csqB])
            dve.op(lambda e: e.scalar_tensor_tensor(lnr, bk(5)[:], 1.0 / CC, csq, ALU.mult, ALU.subtract), [bkB(5), csqB], [lnrB])
            act.op(lambda e: e.activation(out=lnr, in_=lnr, func=AF.Sqrt, bias=EPS), [lnrB], [lnrB])
            dve.op(lambda e: e.reciprocal(lnr, lnr), [lnrB], [lnrB])
            for m in range(4):
                dve.op(lambda e, m=m: e.tensor_tensor(cacc[:, m, :], cacc[:, m, :], lnm, ALU.subtract), [caccB, lnmB], [caccB])
                dve.op(lambda e, m=m: e.tensor_tensor(cacc[:, m, :], cacc[:, m, :], lnr, ALU.mult), [caccB, lnrB], [caccB])
                act.op(lambda e, m=m: e.activation(out=acta[:, m, :], in_=cacc[:, m, :], func=AF.Silu, scale=cw[:, m, CK + 1:CK + 2], bias=cw[:, m, CK + 2:CK + 3]),
                       [caccB, cwB], [actaB])

            srcs = [(acta, actaB, 4, 0), (ybin, ybinB, 4, 4), (atf, atfB, 8, 8)]
            for c in range(16):
                wy, wyB = w_next("YG%d" % c)
                for g in range(3):
                    bg = fm_chunk(wy, wyB, 0, xnT, xnTC, nkc=16, kc0=16 + 16 * g)
                    act.op(lambda e, bg=bg: e.activation(out=sga, in_=bk(bg)[:], func=AF.Sigmoid), [bkB(bg)], [sgaB])
                    src, srcB, nkc, kc0 = srcs[g]
                    by = fm_chunk(wy, wyB, 0, src, srcB, nkc=nkc, kc0=kc0)
                    if g == 0:
                        dve.op(lambda e, by=by: e.tensor_tensor(mt1, bk(by)[:], sga, ALU.mult), [bkB(by), sgaB], [mt1B])
                    elif g == 1:
                        dve.op(lambda e, by=by: e.tensor_tensor(mt2, bk(by)[:], sga, ALU.mult), [bkB(by), sgaB], [mt2B])
                        pool.op(lambda e: e.tensor_tensor(mt1, mt1, mt2, ALU.add), [mt1B, mt2B], [mt1B])
                    else:
                        dve.op(lambda e, by=by: e.tensor_tensor(mt2, bk(by)[:], sga, ALU.mult), [bkB(by), sgaB], [mt2B])
                        pool.op(lambda e, c=c: e.tensor_tensor(mrg[:, c, :], mt1, mt2, ALU.add), [mt1B, mt2B], [mrgC[c]])

            for nb in range(4):
                wt, wB = w_next("O%d" % nb)
                for j in range(NJ):
                    b = tm_block(wt, wB, j, mrg, mrgC)
                    tm_evac(b, j, nb, hbm, hbmJ)
            post_norm_residual(l, 0, hbm, hbmJ)

            rmsnorm_to_xnT(1)
            for i in range(11):
                wt, wB = w_next("FA%d" % i)
                for m in range(4):
                    c = i * 4 + m
                    b = fm_chunk(wt, wB, m, xnT, xnTC)
                    act.op(lambda e, b=b, m=m: e.activation(out=ae[:, m, 2:2 + T], in_=bk(b)[:], func=AF.Copy), [bkB(b)], [aeM[m]])
                    pool.op(lambda e, m=m, c=c: e.tensor_copy(ae[:, m, 0:2], fhalo[:, c, :]), [fhaloB], [aeM[m]])
                    pool.op(lambda e, m=m, c=c: e.tensor_copy(fhalo[:, c, :], ae[:, m, T:T + 2]), [aeM[m]], [fhaloB])
                    if m % 2 == 0:
                        dve.op(lambda e, m=m, c=c: e.tensor_scalar(fca[:, m, :], ae[:, m, 2:2 + T], fw[:, c, 2:3], fw[:, c, 3:4], op0=ALU.mult, op1=ALU.add),
                               [aeM[m], fwB], [fcaM[m]])
                        dve.op(lambda e, m=m, c=c: e.scalar_tensor_tensor(fca[:, m, :], ae[:, m, 1:1 + T], fw[:, c, 1:2], fca[:, m, :], ALU.mult, ALU.add),
                               [aeM[m], fwB, fcaM[m]], [fcaM[m]])
                        dve.op(lambda e, m=m, c=c: e.scalar_tensor_tensor(fca[:, m, :], ae[:, m, 0:T], fw[:, c, 0:1], fca[:, m, :], ALU.mult, ALU.add),
                               [aeM[m], fwB, fcaM[m]], [fcaM[m]])
                    else:
                        pool.op(lambda e, m=m, c=c: e.tensor_scalar(fca[:, m, :], ae[:, m, 2:2 + T], fw[:, c, 2:3], fw[:, c, 3:4], op0=ALU.mult, op1=ALU.add),
                                [aeM[m], fwB], [fcaM[m]])
                        for k in (1, 0):
                            pool.op(lambda e, m=m, c=c, k=k: e.tensor_scalar(ptmp, ae[:, m, k:k + T], fw[:, c, k:k + 1], None, op0=ALU.mult), [aeM[m], fwB], [ptmpB])
                            pool.op(lambda e, m=m: e.tensor_tensor(fca[:, m, :], fca[:, m, :], ptmp, ALU.add), [ptmpB, fcaM[m]], [fcaM[m]])
                    act.op(lambda e, m=m: e.activation(out=fga[:, m, :], in_=fca[:, m, :], func=AF.Gelu_apprx_tanh), [fcaM[m]], [fgaM[m]])
                wt, wB = w_next("FB%d" % i)
                for m in range(4):
                    c = i * 4 + m
                    b = fm_chunk(wt, wB, m, xnT, xnTC)
                    dve.op(lambda e, b=b, m=m, c=c: e.tensor_tensor(gff[:, c, :], bk(b)[:], fga[:, m, :], ALU.mult), [bkB(b), fgaM[m]], [gffC[c]])
            for nb in range(4):
                for kg in range(4):
                    wt, wB = w_next("FO%d_%d" % (nb, kg))
                    for j in range(NJ):
                        tm_block(wt, wB, j, gff, gffC, nkc=11, b=j, first=(kg == 0), last=(kg == 3), c_off=kg * 11)
                for j in range(NJ):
                    tm_evac(j, j, nb, hbf, hbfJ)
            post_norm_residual(l, 1, hbf, hbfJ)

            rmsnorm_to_xnT(2)
            sp.dma(pt, p_in[l, t0:t0 + T, :].rearrange("(j p) d -> p j d", p=128), [NOB], [ptB], ptB)
            dve.op(lambda e: e.tensor_copy(ptb16, pt), [ptB], [ptb16B])
            pv = bk(5)[:].bitcast(BF16)
            for c in range(2):
                for j in range(NJ):
                    pe.op(lambda e, j=j, c=c: e.transpose(pv[:, j * 128:(j + 1) * 128], ptb16[:, j, c * 128:(c + 1) * 128], idb),
                          [ptb16B, idbB], [bkB(5)], signal=(j == NJ - 1))
                dve.op(lambda e, c=c: e.tensor_copy(pT[:, c, :], pv[:, 0:T]), [bkB(5)], [pTB])
            for nb in range(4):
                wt, wB = w_next("PG%d" % nb)
                gbanks = []
                for j in range(NJ):
                    gbanks.append(tm_block(wt, wB, j, xnT, xnTC, b=j))
                wp, wpB = w_next("PP%d" % nb)
                for j in range(NJ):
                    b = gbanks[j]
                    act.op(lambda e, b=b: e.activation(out=gsb, in_=bk(b)[:], func=AF.Sigmoid), [bkB(b)], [gsbB])
                    b2 = 4 + (j % 2)
                    for c in range(2):
                        pe.op(lambda e, c=c, j=j, b2=b2: e.matmul(bk(b2)[:], pT[:, c, j * 128:(j + 1) * 128], wp[:, c, :], start=(c == 0), stop=(c == 1)),
                              [pTB, wpB], [bkB(b2)], signal=(c == 1))
                    dve.op(lambda e, b2=b2: e.tensor_tensor(gsb, gsb, bk(b2)[:], ALU.mult), [gsbB, bkB(b2)], [gsbB])
                    pool.op(lambda e, j=j, nb=nb: e.tensor_tensor(xt[:, j, nb * 512:(nb + 1) * 512], xt[:, j, nb * 512:(nb + 1) * 512], gsb, ALU.add),
                            [gsbB, xtJ[j]], [xtJ[j]])
            act.dma(x_dst[t0:t0 + T, :].rearrange("(j p) d -> p j d", p=128), xt, [xtB], [x_dstB], xtB)

    sp.wait_all([xtB, ksB, vsB, outB] + kTB + vSB)
    act.wait_all([xtB, ksB, vsB, outB])
    pes[0].close()
    K.es.close()
    global _LAST_KB
    _LAST_KB = K
    return nc


def host_consts(S):
    cst = np.zeros((128, 3, 128), np.float32)
    cst[:, 0, :] = np.eye(128, dtype=np.float32)
    cst[:, 1, :] = np.triu(np.ones((128, 128), np.float32))
    for m in range(32):
        cst[(m + 16) % 32, 2, m] = 1.0
    half = ROPE // 2
    inv = np.float32(500000.0) ** (-np.arange(0, ROPE, 2, dtype=np.float32) / np.float32(ROPE))
    ang = np.arange(S, dtype=np.float32)[:, None] * inv[None, :].astype(np.float32)
    cos = np.cos(ang.astype(np.float64)).astype(np.float32).T
    sin = np.sin(ang.astype(np.float64)).astype(np.float32).T
    ropet = np.zeros((ROPE, 2, S), np.float32)
    ropet[0:half, 0] = cos
    ropet[half:, 0] = cos
    ropet[0:half, 1] = -sin
    ropet[half:, 1] = sin
    return cst, ropet


def layout_params(inp, depth):
    f = lambda a: np.ascontiguousarray(np.asarray(a, dtype=np.float32))
    L = depth
    toP = lambda v: v.reshape(L, -1, 128).transpose(0, 2, 1)
    vecP = np.stack([toP(f(inp["mix_norm_pre"])), toP(f(inp["ffn_norm_pre"])), toP(f(inp["ple_norm"]))], axis=2)
    vecB = np.stack([np.broadcast_to(f(inp["mix_norm_post"])[:, None, :], (L, 128, D)),
                     np.broadcast_to(f(inp["ffn_norm_post"])[:, None, :], (L, 128, D))], axis=1)
    cw = f(inp["conv_dw_w"]).reshape(L, CK, 4, 128).transpose(0, 3, 2, 1)
    extra = np.stack([toP(f(inp["conv_dw_b"])), toP(f(inp["conv_norm_g"])), toP(f(inp["conv_norm_b"]))], axis=3)
    convw = np.concatenate([cw, extra], axis=3)
    sguB = np.stack([np.broadcast_to(f(inp["sgu_norm_g"])[:, None, :], (L, 128, SW)),
                     np.broadcast_to(f(inp["sgu_norm_b"])[:, None, :], (L, 128, SW))], axis=1)
    sguwT = f(inp["sgu_w"]).transpose(0, 3, 1, 2)
    sgub = f(inp["sgu_b"]).reshape(L, 1, 4 * 128)
    fwt = f(inp["ffn_dw_w"]).reshape(L, 3, 44, 128).transpose(0, 3, 2, 1)
    fwb = toP(f(inp["ffn_dw_b"]))[..., None]
    ffnw = np.concatenate([fwt, fwb], axis=3)
    c = np.ascontiguousarray
    return dict(vecP=c(vecP), vecB=c(vecB), convw=c(convw), sguB=c(sguB), sguwT=c(sguwT), sgub=c(sgub), ffnw=c(ffnw))


_CACHE = {}


def run(inp, S, depth, nseq):
    key = (S, depth)
    if key not in _CACHE:
        _CACHE[key] = build(S, depth)
    nc = _CACHE[key]
    cst, ropet = host_consts(S)
    prm = layout_params(inp, depth)
    f = lambda a: np.ascontiguousarray(np.asarray(a, dtype=np.float32))
    shared = dict(prm)
    shared.update(cst=cst, ropet=ropet)
    for k in ("w_in", "conv_out", "sgu_out", "attn_out", "w_o", "ffn_in", "ffn_out", "ple_gate", "ple_proj"):
        shared[k] = f(inp[k])
    x = f(inp["x"])
    p = f(inp["p"])
    in_maps = []
    for b in range(nseq):
        m = dict(shared)
        m["x"] = np.ascontiguousarray(x[b])
        m["p"] = np.ascontiguousarray(p[:, b])
        in_maps.append(m)
    res = run_bass_kernel_spmd(nc, in_maps, core_ids=list(range(nseq)))
    return np.stack([np.asarray(res.results[b]["out"], dtype=np.float32) for b in range(nseq)], axis=0)


def kernel(**inputs):
    x = np.asarray(inputs["x"])
    B, S, _ = x.shape
    depth = np.asarray(inputs["w_in"]).shape[0]
    return run(inputs, S, depth, B)
```

```python
import numpy as np
from contextlib import ExitStack
import concourse.bass as bass
import concourse.mybir as mybir
from concourse.bass_utils import run_bass_kernel_spmd

F32 = mybir.dt.float32
BF16 = mybir.dt.bfloat16
AF = mybir.ActivationFunctionType
ALU = mybir.AluOpType
AX = mybir.AxisListType

D = 2048
NH = 8
HD = 128
ROPE = 32
BLK = 256
TOPK = 3
CC = 512
CK = 31
SW = 512
DFF = 5632
PLE = 256
INW = 11264
EPS = 1e-6
T = 512
NJ = 4
NEG = -1.0e30
SEM_ROT = 30000


class Buf:
    def __init__(self, name, parent=None):
        self.name = name
        self.parent = parent
        self.children = []
        if parent is not None:
            parent.children.append(self)
        self.w = {}
        self.r = {}
        self.dsem = None
        self.dtotal = 0
        self.excl = False

    def family(self):
        out = [self]
        p = self.parent
        while p is not None:
            out.append(p)
            p = p.parent
        stack = list(self.children)
        while stack:
            c = stack.pop()
            out.append(c)
            stack.extend(c.children)
        return out


class Eng:
    def __init__(self, K, name, eng, is_pe=False):
        self.K = K
        self.name = name
        self.eng = eng
        self.is_pe = is_pe
        self.sem = K.new_sem("e_" + name)
        self.own = [self.sem]
        self.cnt = 0
        self.known = {}
        self.pend_r = []
        self.pend_w = []

    def _wait(self, deps):
        for sem, val in deps.items():
            if self.is_pe and any(sem is o for o in self.own):
                continue
            if self.known.get(sem, 0) >= val:
                continue
            self.eng.wait_ge(sem, val)
            self.known[sem] = val

    def _deps(self, reads, writes):
        deps = {}
        for b in reads:
            for f in b.family():
                for s, v in f.w.items():
                    if deps.get(s, 0) < v:
                        deps[s] = v
        for b in writes:
            for f in b.family():
                for s, v in f.w.items():
                    if deps.get(s, 0) < v:
                        deps[s] = v
                for s, v in f.r.items():
                    if deps.get(s, 0) < v:
                        deps[s] = v
        return deps

    def op(self, fn, reads=(), writes=(), signal=True):
        ex = [b for b in reads if b.excl]
        if ex:
            reads = [b for b in reads if not b.excl]
            writes = list(writes) + ex
        self._wait(self._deps(reads, writes))
        ins = fn(self.eng)
        self.pend_r.extend(reads)
        self.pend_w.extend(writes)
        if signal:
            if self.cnt >= SEM_ROT:
                self.sem = self.K.new_sem("e_" + self.name)
                self.own.append(self.sem)
                self.cnt = 0
            self.cnt += 1
            ins.then_inc(self.sem, 1)
            tok = (self.sem, self.cnt)
            for b in self.pend_r:
                if b.r.get(tok[0], 0) < tok[1]:
                    b.r[tok[0]] = tok[1]
            for b in self.pend_w:
                b.w = {tok[0]: tok[1]}
                b.r = {}
            self.pend_r = []
            self.pend_w = []
        return ins

    def dma(self, out, in_, reads, writes, sbuf, **kw):
        self._wait(self._deps(reads, writes))
        if sbuf.dsem is None or sbuf.dtotal >= SEM_ROT:
            sbuf.dsem = self.K.new_sem("d_" + sbuf.name)
            sbuf.dtotal = 0
        ins = self.eng.dma_start(out=out, in_=in_, **kw)
        sbuf.dtotal += 16
        ins.then_inc(sbuf.dsem, 16)
        tok = (sbuf.dsem, sbuf.dtotal)
        for b in reads:
            if b.r.get(tok[0], 0) < tok[1]:
                b.r[tok[0]] = tok[1]
        for b in writes:
            b.w = {tok[0]: tok[1]}
            b.r = {}
        return ins

    def wait_all(self, bufs):
        deps = {}
        for b in bufs:
            for f in b.family():
                for dct in (f.w, f.r):
                    for s, v in dct.items():
                        if deps.get(s, 0) < v:
                            deps[s] = v
        self._wait(deps)


class Kb:
    def __init__(self):
        self.nc = bass.Bass("TRN2", target_bir_lowering=False)
        self.es = ExitStack()
        self.nsem = 0
        nc = self.nc
        self.pe = Eng(self, "pe", nc.tensor, is_pe=True)
        self.act = Eng(self, "act", nc.scalar)
        self.dve = Eng(self, "dve", nc.vector)
        self.pool = Eng(self, "pool", nc.gpsimd)
        self.sp = Eng(self, "sp", nc.sync)

    def new_sem(self, name):
        self.nsem += 1
        return self.es.enter_context(self.nc.semaphore(name + "_%d" % self.nsem))

    def sb(self, name, shape, dt):
        t = self.es.enter_context(self.nc.sbuf_tensor(name, shape, dt))
        return t, Buf(name)

    def ps(self, name, shape, dt):
        t = self.es.enter_context(self.nc.psum_tensor(name, shape, dt))
        b = Buf(name)
        b.excl = True
        return t, b

    def din(self, name, shape, dt=F32):
        return self.nc.dram_tensor(name, list(shape), dt, kind="ExternalInput").ap()

    def dout(self, name, shape, dt=F32):
        return self.nc.dram_tensor(name, list(shape), dt, kind="ExternalOutput").ap()

    def dint(self, name, shape, dt):
        return self.nc.dram_tensor(name, list(shape), dt).ap()


def weight_blocks():
    blks = []
    for i in range(10):
        col = {0: 0, 1: 512, 2: 1024, 3: 1536, 4: 2048, 5: 2560, 6: 3584, 7: 3072, 8: 4096, 9: 4608}[i]
        blks.append(("P%d" % i, 512, [("w_in", 0, 2048, col)]))
    for c in range(16):
        blks.append(("YG%d" % c, 128, [("conv_out", 0, 512, c * 128), ("sgu_out", 0, 512, c * 128),
                                       ("attn_out", 0, 1024, c * 128), ("w_in", 0, 2048, 5120 + c * 128),
                                       ("w_in", 0, 2048, 7168 + c * 128), ("w_in", 0, 2048, 9216 + c * 128)]))
    for nb in range(4):
        blks.append(("O%d" % nb, 512, [("w_o", 0, 2048, nb * 512)]))
    for i in range(11):
        blks.append(("FA%d" % i, 512, [("ffn_in", 0, 2048, i * 512)]))
        blks.append(("FB%d" % i, 512, [("ffn_in", 0, 2048, DFF + i * 512)]))
    for nb in range(4):
        for kg in range(4):
            blks.append(("FO%d_%d" % (nb, kg), 512, [("ffn_out", kg * 1408, 1408, nb * 512)]))
    for nb in range(4):
        blks.append(("PG%d" % nb, 512, [("ple_gate", 0, 2048, nb * 512)]))
        blks.append(("PP%d" % nb, 512, [("ple_proj", 0, 256, nb * 512)]))
    return blks


WBLKS = weight_blocks()
NWB = len(WBLKS)
GRP = 4
NKB = 3
NPT = 4
SBANKS = (0, 1, 6, 7)


class Arena:
    def __init__(self, K, nbytes):
        self.K = K
        self.n = nbytes
        slab = K.es.enter_context(K.nc.sbuf_tensor("arena", [128, nbytes // 2], BF16))
        self.base = K.nc.lookup_mloc(slab).addr
        self.bufs = {}
        self.top = 0
        self.gen = 0

    def begin(self):
        self.top = 0
        self.gen += 1

    def carve(self, name, shape, dt, at=None):
        esz = 4 if dt == F32 else 2
        n = 1
        for d in shape[1:]:
            n *= d
        nb = n * esz
        nb_al = (nb + 31) // 32 * 32
        if at is None:
            at = self.top
            self.top += nb_al
        assert at % 4 == 0 and at + nb <= self.n, (name, at, nb, self.n)
        h = self.K.nc.alloc_sbuf_tensor_at("%s_g%d" % (name, self.gen), list(shape), dt, offset=self.base + at)
        ap = h.ap()
        b = self.bufs.get(name)
        if b is None:
            b = Buf(name)
            b.lo, b.hi = at, at + nb
            b.fam = None
            self.bufs[name] = b
        else:
            assert (b.lo, b.hi) == (at, at + nb), name
        return ap, b

    def sub(self, parent, name, lo_off, nbytes):
        b = self.bufs.get(name)
        if b is None:
            b = Buf(name)
            b.lo, b.hi = parent.lo + lo_off, parent.lo + lo_off + nbytes
            b.fam = None
            self.bufs[name] = b
        return b

    def finalize(self):
        bl = list(self.bufs.values())
        for b in bl:
            b.fam = [o for o in bl if o.lo < b.hi and b.lo < o.hi]


def _family(self):
    f = getattr(self, "fam", None)
    if f is not None:
        return f
    return [self]


Buf.family = _family


def build(S, depth):
    NT = S // T
    NB = S // BLK
    assert NB <= 64
    K = Kb()
    nc = K.nc
    pe, act, dve, pool, sp = K.pe, K.act, K.dve, K.pool, K.sp

    x_in = K.din("x", [S, D])
    p_in = K.din("p", [depth, S, PLE])
    win = {"w_in": K.din("w_in", [depth, D, INW]), "conv_out": K.din("conv_out", [depth, CC, D]),
           "sgu_out": K.din("sgu_out", [depth, SW, D]), "attn_out": K.din("attn_out", [depth, NH * HD, D]),
           "w_o": K.din("w_o", [depth, D, D]), "ffn_in": K.din("ffn_in", [depth, D, 2 * DFF]),
           "ffn_out": K.din("ffn_out", [depth, DFF, D]), "ple_gate": K.din("ple_gate", [depth, D, D]),
           "ple_proj": K.din("ple_proj", [depth, PLE, D])}
    vecP = K.din("vecP", [depth, 128, 3, 16])
    vecB = K.din("vecB", [depth, 2, 128, D])
    convw = K.din("convw", [depth, 128, 4, CK + 3])
    sguB = K.din("sguB", [depth, 2, 128, SW])
    sguwT = K.din("sguwT", [depth, 128, 4, 128])
    sgub = K.din("sgub", [depth, 1, 4 * 128])
    ffnw = K.din("ffnw", [depth, 128, 44, 4])
    cst = K.din("cst", [128, 3, 128])
    ropet = K.din("ropet", [ROPE, 2, S])
    out = K.dout("out", [S, D])

    wb = [K.dint("wb%d" % l, [NWB, 128, 16 * 512], BF16) for l in range(depth)]
    wbB = [Buf("wbB%d" % l) for l in range(depth)]
    kT = [K.dint("kT%d" % l, [NH, HD, S], BF16) for l in range(depth)]
    vS = [K.dint("vS%d" % l, [S, NH, HD], BF16) for l in range(depth)]
    kTB = [Buf("kT%d" % l) for l in range(depth)]
    vSB = [Buf("vS%d" % l) for l in range(depth)]
    xs = [K.dint("xs%d" % l, [S, D], F32) for l in range(max(depth - 1, 1))]
    xsB = [Buf("xs%d" % l) for l in range(max(depth - 1, 1))]
    outB = Buf("outB")
    NOB = Buf("extern")

    for l in range(depth):
        for bi, (tag, ncol, parts) in enumerate(WBLKS):
            kc0 = 0
            for (src, r0, nr, c0) in parts:
                nkc = nr // 128
                src_ap = win[src][l, r0:r0 + nr, c0:c0 + ncol].rearrange("(c p) n -> p c n", p=128)
                dst_ap = wb[l][bi, :, kc0 * ncol:(kc0 + nkc) * ncol].rearrange("p (c n) -> p c n", n=ncol)
                pool.dma(dst_ap, src_ap, [NOB], [], wbB[l])
                kc0 += nkc
        wbB[l].w = {wbB[l].dsem: wbB[l].dtotal}

    A = Arena(K, 207 * 1024)
    cv = A.carve
    dg = wsl = xt = xtB = xtJ = vB = vBB = sgB = sgBB = vP = vPB = cw = cwB = wmT = wmTB = sbrb = sbrbB = onesr = onesrB = fw = fwB = idb = idbB = trib = tribB = pmb = pmbB = onesf = onesfB = cstf = cstfB = kmean = kmeanB = Gs = GsB = ahalo = ahaloB = fhalo = fhaloB = ss = ssB = rstd = rstdB = ssq = ssqB = top8 = top8B = bst = bstB = bag = bagB = kms = kmsB = rinv = rinvB = junk = junkB = ptmp = ptmpB = R0 = xnT = xnTB = xnTC = o = qs = qsB = qsH = ks = ksB = ksH = vs = vsB = vsJ = o2 = aext = aextB = o3 = cacc = caccB = caccM = sgc = sgcB = o4 = sus = susB = svn = svnB = acta = actaB = ybin = ybinB = o5 = lnm = lnmB = lnr = lnrB = csq = csqB = svg = svgB = o6 = atf = atfB = o7 = sel = acc = atk = o8 = kbuf = vbuf = ptb = o9 = xnb = xnbB = xnbJ = rp = rpB = rt1 = rt1B = rt2 = rt2B = mrg = mrgB = mrgC = sga = sgaB = mt1 = mt1B = mt2 = mt2B = hbm = hbmB = hbmJ = ae = aeB = aeM = fca = fcaB = fcaM = fga = fgaB = fgaM = gff = gffB = gffC = hbf = hbfB = hbfJ = pt = ptB = ptb16 = ptb16B = pT = pTB = gsb = gsbB = wmf = wmfB = sbr = sbrB = None

    def declare():
        nonlocal dg, wsl, xt, xtB, xtJ, vB, vBB, sgB, sgBB, vP, vPB, cw, cwB, wmT, wmTB, sbrb, sbrbB, onesr, onesrB, fw, fwB, idb, idbB, trib, tribB, pmb, pmbB, onesf, onesfB, cstf, cstfB, kmean, kmeanB, Gs, GsB, ahalo, ahaloB, fhalo, fhaloB, ss, ssB, rstd, rstdB, ssq, ssqB, top8, top8B, bst, bstB, bag, bagB, kms, kmsB, rinv, rinvB, junk, junkB, ptmp, ptmpB, R0, xnT, xnTB, xnTC, o, qs, qsB, qsH, ks, ksB, ksH, vs, vsB, vsJ, o2, aext, aextB, o3, cacc, caccB, caccM, sgc, sgcB, o4, sus, susB, svn, svnB, acta, actaB, ybin, ybinB, o5, lnm, lnmB, lnr, lnrB, csq, csqB, svg, svgB, o6, atf, atfB, o7, sel, acc, atk, o8, kbuf, vbuf, ptb, o9, xnb, xnbB, xnbJ, rp, rpB, rt1, rt1B, rt2, rt2B, mrg, mrgB, mrgC, sga, sgaB, mt1, mt1B, mt2, mt2B, hbm, hbmB, hbmJ, ae, aeB, aeM, fca, fcaB, fcaM, fga, fgaB, fgaM, gff, gffB, gffC, hbf, hbfB, hbfJ, pt, ptB, ptb16, ptb16B, pT, pTB, gsb, gsbB, wmf, wmfB, sbr, sbrB
        A.begin()
        wsl = [cv("wsl%d" % i, [128, 16 * 512], BF16) for i in range(2)]
        xt, xtB = cv("xt", [128, NJ, D], F32)
        xtJ = [A.sub(xtB, "xt%d" % j, j * D * 4, D * 4) for j in range(NJ)]
        vB, vBB = cv("vB", [128, D], F32)
        sgB, sgBB = cv("sgB", [128, 2, SW], F32)
        vP, vPB = cv("vP", [128, 3, 16], F32)
        cw, cwB = cv("cw", [128, 4, CK + 3], F32)
        wmT, wmTB = cv("wmT", [128, 4, 128], BF16)
        sbrb, sbrbB = cv("sbrb", [1, 4 * 128], BF16)
        onesr, onesrB = cv("onesr", [1, 128], BF16)
        fw, fwB = cv("fw", [128, 44, 4], F32)
        idb, idbB = cv("idb", [128, 128], BF16)
        trib, tribB = cv("trib", [128, 128], BF16)
        pmb, pmbB = cv("pmb", [32, 32], BF16)
        onesf, onesfB = cv("onesf", [128, 128], F32)
        cstf, cstfB = cv("cstf", [128, 3, 128], F32)
        kmean, kmeanB = cv("kmean", [128, NH, 64], BF16)
        Gs, GsB = cv("Gs", [128, NJ, 64], F32)
        ahalo, ahaloB = cv("ahalo", [128, 4, CK - 1], BF16)
        fhalo, fhaloB = cv("fhalo", [128, 44, 2], BF16)
        dg = [cv("dg%d" % i, [128, 128], BF16) for i in range(8)]
        ss, ssB = cv("ss", [128, 8], F32)
        rstd, rstdB = cv("rstd", [128, 8], F32)
        ssq, ssqB = cv("ssq", [128, NJ, 4], F32)
        top8, top8B = cv("top8", [128, 8], F32)
        bst, bstB = cv("bst", [128, 6], F32)
        bag, bagB = cv("bag", [128, 2], F32)
        kms, kmsB = cv("kms", [128, 2], F32)
        rinv, rinvB = cv("rinv", [128, NJ], F32)
        junk, junkB = cv("junk", [128, D], BF16)
        ptmp, ptmpB = cv("ptmp", [128, T], F32)
        R0 = A.top
        xnT, xnTB = cv("xnT", [128, 16, T], BF16, at=R0)
        xnTC = [A.sub(xnTB, "xnT%d" % c, c * T * 2, T * 2) for c in range(16)]
        o = R0 + 16384
        qs, qsB = cv("qs", [128, NH, T], BF16, at=o)
        qsH = [A.sub(qsB, "qs%d" % h, h * T * 2, T * 2) for h in range(NH)]
        ks, ksB = cv("ks", [128, NH, T], BF16, at=o + 8192)
        ksH = [A.sub(ksB, "ks%d" % h, h * T * 2, T * 2) for h in range(NH)]
        vs, vsB = cv("vs", [128, NJ, NH, HD + 1], BF16, at=o + 16384)
        vsJ = [A.sub(vsB, "vs%d" % j, j * NH * (HD + 1) * 2, NH * (HD + 1) * 2) for j in range(NJ)]
        o2 = o + 16384 + 8256
        aext, aextB = cv("aext", [128, 4, CK - 1 + T], BF16, at=o2)
        o3 = o2 + 8672
        cacc, caccB = cv("cacc", [128, 4, T], F32, at=o3)
        caccM = [A.sub(caccB, "cacc%d" % m, m * T * 4, T * 4) for m in range(4)]
        sgc, sgcB = cv("sgc", [128, 4, T], F32, at=o3)
        o4 = o3 + 8192
        sus, susB = cv("sus", [128, 4, T], BF16, at=o4)
        svn, svnB = cv("svn", [128, NJ, SW], BF16, at=o4 + 4096)
        acta, actaB = cv("acta", [128, 4, T], BF16, at=o4 + 8192)
        ybin, ybinB = cv("ybin", [128, 4, T], BF16, at=o4 + 12288)
        o5 = o4 + 16384
        lnm, lnmB = cv("lnm", [128, T], F32, at=o5)
        lnr, lnrB = cv("lnr", [128, T], F32, at=o5 + 2048)
        csq, csqB = cv("csq", [128, T], F32, at=o5 + 4096)
        svg, svgB = cv("svg", [128, SW], F32, at=o5 + 4096)
        o6 = o5 + 6144
        atf, atfB = cv("atf", [128, NH, T], BF16, at=o6)
        o7 = o6 + 8192
        sel = [cv("sel%d" % i, [128, NJ, 64], F32, at=o7 + i * 1024) for i in range(2)]
        acc = [cv("acc%d" % i, [128, NJ, HD + 1], F32, at=o7 + 2048 + i * 2080) for i in range(2)]
        atk = [cv("atk%d" % i, [128, NJ, HD], BF16, at=o7 + 6208 + i * 1024) for i in range(2)]
        o8 = o7 + 8256
        kbuf = [cv("kbuf%d" % i, [128, GRP * BLK], BF16, at=o8 + i * 2048) for i in range(NKB)]
        vbuf = [cv("vbuf%d" % i, [128, GRP * 2, HD + 1], BF16, at=o8 + 6144 + i * 2080) for i in range(NKB)]
        ptb = [cv("ptb%d" % i, [128, T], BF16, at=o8 + 12384 + i * 1024) for i in range(NPT)]
        o9 = o8 + 12384 + 4096
        xnb, xnbB = cv("xnb", [128, NJ, D], BF16, at=o7)
        xnbJ = [A.sub(xnbB, "xnb%d" % j, j * D * 2, D * 2) for j in range(NJ)]
        rp, rpB = cv("rp", [ROPE, 2, T], F32, at=o7 + 16384)
        rt1, rt1B = cv("rt1", [ROPE, T], F32, at=o7 + 16384 + 4096)
        rt2, rt2B = cv("rt2", [ROPE, T], F32, at=o7 + 16384 + 6144)
        assert o7 + 16384 + 8192 <= o9, (o7, o9)
        mrg, mrgB = cv("mrg", [128, 16, T], BF16, at=o)
        mrgC = [A.sub(mrgB, "mrg%d" % c, c * T * 2, T * 2) for c in range(16)]
        sga, sgaB = cv("sga", [128, T], F32, at=o + 16384)
        mt1, mt1B = cv("mt1", [128, T], F32, at=o + 16384 + 2048)
        mt2, mt2B = cv("mt2", [128, T], F32, at=o + 16384 + 4096)
        hbm, hbmB = cv("hbm", [128, NJ, D], F32, at=o2)
        hbmJ = [A.sub(hbmB, "hbm%d" % j, j * D * 4, D * 4) for j in range(NJ)]
        assert o2 + 32768 <= o9
        ae, aeB = cv("ae", [128, 4, T + 2], BF16, at=o)
        aeM = [A.sub(aeB, "ae%d" % m, m * (T + 2) * 2, (T + 2) * 2) for m in range(4)]
        fca, fcaB = cv("fca", [128, 4, T], F32, at=o + 8224)
        fcaM = [A.sub(fcaB, "fca%d" % m, m * T * 4, T * 4) for m in range(4)]
        fga, fgaB = cv("fga", [128, 4, T], BF16, at=o + 16416)
        fgaM = [A.sub(fgaB, "fga%d" % m, m * T * 2, T * 2) for m in range(4)]
        gff, gffB = cv("gff", [128, 44, T], BF16, at=o + 20512)
        gffC = [A.sub(gffB, "gff%d" % c, c * T * 2, T * 2) for c in range(44)]
        assert o + 20512 + 45056 <= o7, (o + 20512 + 45056, o7)
        hbf, hbfB = cv("hbf", [128, NJ, D], F32, at=R0)
        hbfJ = [A.sub(hbfB, "hbf%d" % j, j * D * 4, D * 4) for j in range(NJ)]
        assert R0 + 32768 <= o + 20512
        pt, ptB = cv("pt", [128, NJ, PLE], F32, at=o)
        ptb16, ptb16B = cv("ptb16", [128, NJ, PLE], BF16, at=o + 4096)
        pT, pTB = cv("pT", [128, 2, T], BF16, at=o + 6144)
        gsb, gsbB = cv("gsb", [128, T], F32, at=o + 8192)
        wmf, wmfB = cv("wmf", [128, 4, 128], F32, at=o9 - 4096)
        sbr, sbrB = cv("sbr", [1, 4 * 128], F32, at=o9 - 2048)
        assert o9 <= A.n, (o9, A.n)

    declare()
    A.finalize()

    bankB = []
    for i in range(8):
        bb = Buf("bank%d" % i)
        bb.excl = True
        bankB.append(bb)
    bank = None
    pes = [None]

    def declare_banks():
        nonlocal bank
        if pes[0] is not None:
            pes[0].close()
        pes[0] = ExitStack()
        bank = [(pes[0].enter_context(nc.psum_tensor("bank%d_g%d" % (i, A.gen), [128, 512], F32)), bankB[i]) for i in range(8)]
    declare_banks()

    def bk(i):
        return bank[i][0]

    def bkB(i):
        return bank[i][1]

    sp.dma(cstf, cst[:, :, :], [NOB], [cstfB], cstfB)
    dve.op(lambda e: e.tensor_copy(idb, cstf[:, 0, :]), [cstfB], [idbB])
    dve.op(lambda e: e.tensor_copy(trib, cstf[:, 1, :]), [cstfB], [tribB])
    dve.op(lambda e: e.tensor_copy(pmb, cstf[0:32, 2, 0:32]), [cstfB], [pmbB])
    pool.op(lambda e: e.memset(onesf, 1.0), [], [onesfB])
    pool.op(lambda e: e.memset(onesr, 1.0), [], [onesrB])

    wstate = {"g": 0, "issued": 0}
    NSLOT = 2
    wseq = [(l, bi) for l in range(depth) for ti in range(NT) for bi in range(NWB)]

    def w_issue():
        g = wstate["issued"]
        l, bi = wseq[g]
        s = g % NSLOT
        ncol = WBLKS[bi][1]
        n = sum(p[2] for p in WBLKS[bi][2]) // 128 * ncol
        sp.dma(wsl[s][0][:, 0:n], wb[l][bi, :, 0:n], [wbB[l]], [wsl[s][1]], wsl[s][1])
        wstate["issued"] += 1

    def w_next(expect_tag):
        g = wstate["g"]
        l, bi = wseq[g]
        assert WBLKS[bi][0] == expect_tag, (WBLKS[bi][0], expect_tag)
        while wstate["issued"] < min(g + NSLOT, len(wseq)):
            w_issue()
        wstate["g"] += 1
        t, b = wsl[g % NSLOT]
        ncol = WBLKS[bi][1]
        return t.rearrange("p (c n) -> p c n", n=ncol), b

    evq = {"i": 0}

    def ev_eng():
        evq["i"] += 1
        return act if evq["i"] % 2 else dve

    def rmsnorm_to_xnT(gidx):
        for j in range(NJ):
            act.op(lambda e, j=j: e.activation(out=junk, in_=xt[:, j, :], func=AF.Square, accum_out=ss[:, j:j + 1]),
                   [xtJ[j]], [junkB, ssB])
        act.op(lambda e: e.activation(out=rstd[:, 0:NJ], in_=ss[:, 0:NJ], func=AF.Sqrt, scale=1.0 / D, bias=EPS),
               [ssB], [rstdB])
        dve.op(lambda e: e.reciprocal(rstd[:, 0:NJ], rstd[:, 0:NJ]), [rstdB], [rstdB])
        for j in range(NJ):
            if j % 2 == 0:
                dve.op(lambda e, j=j: e.tensor_scalar(xnb[:, j, :], xt[:, j, :], rstd[:, j:j + 1], None, op0=ALU.mult),
                       [xtJ[j], rstdB], [xnbJ[j]])
            else:
                act.op(lambda e, j=j: e.activation(out=xnb[:, j, :], in_=xt[:, j, :], func=AF.Copy, scale=rstd[:, j:j + 1]),
                       [xtJ[j], rstdB], [xnbJ[j]])
        for c in range(16):
            b = 4 + (c % 2)
            pv = bk(b)[:].bitcast(BF16)
            for j in range(NJ):
                pe.op(lambda e, j=j, c=c, pv=pv: e.transpose(pv[:, j * 128:(j + 1) * 128], xnb[:, j, c * 128:(c + 1) * 128], idb),
                      [xnbJ[j], idbB], [bkB(b)], signal=(j == NJ - 1))
            eng = ev_eng()
            if eng is dve:
                dve.op(lambda e, c=c, pv=pv: e.tensor_scalar(xnT[:, c, :], pv[:, 0:T], vP[:, gidx, c:c + 1], None, op0=ALU.mult),
                       [bkB(b), vPB], [xnTC[c]])
            else:
                act.op(lambda e, c=c, pv=pv: e.activation(out=xnT[:, c, :], in_=pv[:, 0:T], func=AF.Copy, scale=vP[:, gidx, c:c + 1]),
                       [bkB(b), vPB], [xnTC[c]])

    mmq = {"i": 0}
    dgq = {"i": 0}

    def mm_bank():
        mmq["i"] += 1
        return mmq["i"] % 4

    def _sb(srcB, c):
        return [srcB[c]] if isinstance(srcB, list) else [srcB]

    def fm_chunk(wt, wB, m, src, srcB, nkc=16, kc0=0, wcol=128):
        b = mm_bank()
        for c in range(nkc):
            pe.op(lambda e, c=c, b=b: e.matmul(bk(b)[:], wt[:, kc0 + c, m * 128:(m + 1) * 128], src[:, c, :],
                                               start=(c == 0), stop=(c == nkc - 1)),
                  [wB] + _sb(srcB, c), [bkB(b)], signal=(c == nkc - 1))
        return b

    def tm_block(wt, wB, j, src, srcB, nkc=16, b=None, first=True, last=True, c_off=0):
        if b is None:
            b = mm_bank()
        for c in range(nkc):
            pe.op(lambda e, c=c, b=b: e.matmul(bk(b)[:], src[:, c_off + c, j * 128:(j + 1) * 128], wt[:, c, :],
                                               start=(first and c == 0), stop=(last and c == nkc - 1)),
                  [wB] + _sb(srcB, c_off + c), [bkB(b)], signal=(c == nkc - 1))
        return b

    def post_norm_residual(l, gi, hb, hbJ):
        sp.dma(vB, vecB[l, gi], [NOB], [vBB], vBB)
        dve.op(lambda e: e.tensor_reduce(out=ss[:, 4:8], in_=ssq, axis=AX.X, op=ALU.add), [ssqB], [ssB])
        act.op(lambda e: e.activation(out=rstd[:, 4:8], in_=ss[:, 4:8], func=AF.Sqrt, scale=1.0 / D, bias=EPS), [ssB], [rstdB])
        dve.op(lambda e: e.reciprocal(rstd[:, 4:8], rstd[:, 4:8]), [rstdB], [rstdB])
        for j in range(NJ):
            act.op(lambda e, j=j: e.activation(out=hb[:, j, :], in_=hb[:, j, :], func=AF.Copy, scale=rstd[:, 4 + j:5 + j]), [hbJ[j], rstdB], [hbJ[j]])
            dve.op(lambda e, j=j: e.tensor_tensor(hb[:, j, :], hb[:, j, :], vB, ALU.mult), [hbJ[j], vBB], [hbJ[j]])
            pool.op(lambda e, j=j: e.tensor_tensor(xt[:, j, :], xt[:, j, :], hb[:, j, :], ALU.add), [hbJ[j], xtJ[j]], [xtJ[j]])

    def tm_evac(b, j, nb, hb, hbJ):
        act.op(lambda e: e.activation(out=junk[:, 0:512], in_=bk(b)[:], func=AF.Square, accum_out=ssq[:, j, nb:nb + 1]),
               [bkB(b)], [junkB, ssqB])
        dve.op(lambda e: e.tensor_copy(hb[:, j, nb * 512:(nb + 1) * 512], bk(b)[:]), [bkB(b)], [hbJ[j]])

    cnt = {"kv": 0, "pt": 0, "st": 0, "o": 0}

    def attention(l, ti):
        a0 = 2 * ti
        passes = []

        def gate_pre(h):
            def f():
                hs = h % 2
                sel_t, sel_B = sel[hs]
                acc_t, acc_B = acc[hs]
                for j in range(NJ):
                    own = a0 + j // 2
                    if own == 0:
                        continue
                    pe.op(lambda e, j=j: e.matmul(bk(6)[:, j * 64:(j + 1) * 64], qs[:, h, j * 128:(j + 1) * 128], kmean[:, h, :], start=True, stop=True),
                          [qsH[h], kmeanB], [bkB(6)])
                if a0 + 1 > 0:
                    j0 = 0 if a0 > 0 else 2
                    own_max = a0 + 1
                    for j in range(j0, NJ):
                        own = a0 + j // 2
                        dve.op(lambda e, j=j, own=own: e.tensor_copy(Gs[:, j, 0:own], bk(6)[:, j * 64:j * 64 + own]), [bkB(6)], [GsB])
                        if own <= TOPK:
                            pool.op(lambda e, j=j, own=own: e.memset(sel_t[:, j, 0:own], 1.0), [], [sel_B])
                        else:
                            dve.op(lambda e, j=j: e.max(out=top8, in_=Gs[:, j, :]), [GsB], [top8B])
                            dve.op(lambda e, j=j, own=own: e.tensor_scalar(sel_t[:, j, 0:own], Gs[:, j, 0:own], top8[:, TOPK - 1:TOPK], None, op0=ALU.is_ge),
                                   [GsB, top8B], [sel_B])
                pool.op(lambda e: e.memset(acc_t, 0.0), [], [acc_B])
            return f

        def head_post(h):
            def f():
                hs = h % 2
                acc_t, acc_B = acc[hs]
                atk_t, atk_B = atk[hs]
                dve.op(lambda e: e.reciprocal(rinv, acc_t[:, :, HD]), [acc_B], [rinvB])
                for j in range(NJ):
                    eng = pool if j % 2 else dve
                    eng.op(lambda e, j=j: e.tensor_scalar(atk_t[:, j, :], acc_t[:, j, 0:HD], rinv[:, j:j + 1], None, op0=ALU.mult), [acc_B, rinvB], [atk_B])
                pv = bk(7)[:].bitcast(BF16)
                for j in range(NJ):
                    pe.op(lambda e, j=j: e.transpose(pv[:, j * 128:(j + 1) * 128], atk_t[:, j, :], idb), [atk_B, idbB], [bkB(7)], signal=(j == NJ - 1))
                if h % 2:
                    act.op(lambda e: e.activation(out=atf[:, h, :], in_=pv[:, 0:T], func=AF.Copy), [bkB(7)], [atfB])
                else:
                    dve.op(lambda e: e.tensor_copy(atf[:, h, :], pv[:, 0:T]), [bkB(7)], [atfB])
            return f

        for h in range(NH):
            hp = []
            for g0 in range(0, a0, GRP):
                nblk = min(GRP, a0 - g0)

                def load(h=h, g0=g0, nblk=nblk):
                    bi = cnt["kv"] % NKB
                    cnt["kv"] += 1
                    kb_t, kb_B = kbuf[bi]
                    vb_t, vb_B = vbuf[bi]
                    pool.op(lambda e: e.memset(vb_t[:, :, HD:HD + 1], 1.0), [], [vb_B])
                    sp.dma(kb_t[:, 0:nblk * BLK], kT[l][h, :, g0 * BLK:(g0 + nblk) * BLK], [kTB[l]], [kb_B], kb_B)
                    sp.dma(vb_t[:, 0:nblk * 2, 0:HD], vS[l][g0 * BLK:(g0 + nblk) * BLK, h, :].rearrange("(k p) d -> p k d", p=128), [vSB[l]], [vb_B], vb_B)
                    return kb_t, kb_B, vb_t, vb_B
                holder = {}
                for n in range(nblk):
                    def kts(holder=holder, n=n, load=load):
                        if "b" not in holder:
                            holder["b"] = load()
                        kb_t, kb_B, vb_t, vb_B = holder["b"]
                        return [(kb_t[:, (n * 2 + kk) * 128:(n * 2 + kk + 1) * 128], kb_B, vb_t[:, n * 2 + kk, :], vb_B) for kk in range(2)]
                    hp.append(dict(h=h, kts=kts, jl=[0, 1, 2, 3], selcol=g0 + n, diag=None))
            hp.append(dict(h=h, kts=(lambda h=h: [(ks[:, h, kk * 128:(kk + 1) * 128], ksH[h], vs[:, kk, h, :], vsJ[kk]) for kk in range(2)]),
                           jl=[2, 3], selcol=a0, diag=None))
            for mb in range(2):
                hp.append(dict(h=h, kts=(lambda h=h, mb=mb: [(ks[:, h, (2 * mb + kk) * 128:(2 * mb + kk + 1) * 128], ksH[h], vs[:, 2 * mb + kk, h, :], vsJ[2 * mb + kk]) for kk in range(2)]),
                               jl=[2 * mb, 2 * mb + 1], selcol=None, diag={0: 2 * mb, 1: 2 * mb + 1}))
            hp[0]["pre"] = gate_pre(h)
            hp[-1]["post"] = head_post(h)
            passes.extend(hp)

        def stageA(P):
            if "pre" in P:
                P["pre"]()
            h = P["h"]
            kt_list = P["kts"]()
            P["ktl"] = kt_list
            P["pts"] = []
            P["plan"] = []
            for ki, (k_ap, kB_, v_ap, vB_) in enumerate(kt_list):
                js = [j for j in P["jl"] if (P["diag"] is None or j >= P["diag"][ki])]
                P["plan"].append(js)
                q0, q1 = js[0] * 128, (js[-1] + 1) * 128
                sb_ = SBANKS[cnt["st"] % 4]
                cnt["st"] += 1
                pe.op(lambda e, sb_=sb_, k_ap=k_ap, q0=q0, q1=q1: e.matmul(bk(sb_)[:, q0:q1], k_ap, qs[:, h, q0:q1], start=True, stop=True),
                      [kB_, qsH[h]], [bkB(sb_)])
                pi = cnt["pt"] % NPT
                cnt["pt"] += 1
                pt_t, pt_B = ptb[pi]
                act.op(lambda e, sb_=sb_, pt_t=pt_t, q0=q0, q1=q1: e.activation(out=pt_t[:, q0:q1], in_=bk(sb_)[:, q0:q1], func=AF.Exp),
                       [bkB(sb_)], [pt_B])
                if P["diag"] is not None:
                    jd = P["diag"][ki]
                    pool.op(lambda e, pt_t=pt_t, jd=jd: e.tensor_tensor(pt_t[:, jd * 128:(jd + 1) * 128], pt_t[:, jd * 128:(jd + 1) * 128], trib, ALU.mult),
                            [pt_B, tribB], [pt_B])
                P["pts"].append((pt_t, pt_B))

        def stageB(P):
            h = P["h"]
            hs = h % 2
            sel_t, sel_B = sel[hs]
            acc_t, acc_B = acc[hs]
            oset = cnt["o"] % 2
            cnt["o"] += 1
            jl = P["jl"]
            ob = {j: (2 + oset * 2 + (j // 2), (j % 2) * (HD + 1)) for j in jl}
            for j in jl:
                kis = [ki for ki in range(len(P["ktl"])) if j in P["plan"][ki]]
                b_, oo = ob[j]
                lastj = (j == jl[-1]) or (ob[jl[jl.index(j) + 1]][0] != b_)
                for idx, ki in enumerate(kis):
                    pt_t, pt_B = P["pts"][ki]
                    _, _, v_ap, vB_ = P["ktl"][ki]
                    pe.op(lambda e, j=j, b_=b_, oo=oo, pt_t=pt_t, v_ap=v_ap, idx=idx, kis=kis: e.matmul(bk(b_)[:, oo:oo + HD + 1], pt_t[:, j * 128:(j + 1) * 128], v_ap,
                                                                                               start=(idx == 0), stop=(idx == len(kis) - 1)),
                          [pt_B, vB_], [bkB(b_)], signal=(lastj and idx == len(kis) - 1))
            for j in jl:
                b_, oo = ob[j]
                if P["selcol"] is None:
                    dve.op(lambda e, j=j, b_=b_, oo=oo: e.tensor_tensor(acc_t[:, j, :], acc_t[:, j, :], bk(b_)[:, oo:oo + HD + 1], ALU.add),
                           [bkB(b_), acc_B], [acc_B])
                else:
                    sc = P["selcol"]
                    dve.op(lambda e, j=j, b_=b_, oo=oo, sc=sc: e.scalar_tensor_tensor(acc_t[:, j, :], bk(b_)[:, oo:oo + HD + 1], sel_t[:, j, sc:sc + 1], acc_t[:, j, :], ALU.mult, ALU.add),
                           [bkB(b_), sel_B, acc_B], [acc_B])
            if "post" in P:
                P["post"]()

        stageA(passes[0])
        for i in range(len(passes)):
            if i + 1 < len(passes):
                stageA(passes[i + 1])
            stageB(passes[i])

    for l in range(depth):
        x_src, x_srcB = (x_in, NOB) if l == 0 else (xs[l - 1], xsB[l - 1])
        x_dst, x_dstB = (out, outB) if l == depth - 1 else (xs[l], xsB[l])
        sp.dma(vP, vecP[l], [NOB], [vPB], vPB)
        sp.dma(cw, convw[l], [NOB], [cwB], cwB)
        sp.dma(sgB, sguB[l].rearrange("a p n -> p a n"), [NOB], [sgBB], sgBB)
        sp.dma(wmf, sguwT[l], [NOB], [wmfB], wmfB)
        sp.dma(sbr, sgub[l], [NOB], [sbrB], sbrB)
        sp.dma(fw, ffnw[l], [NOB], [fwB], fwB)
        for g in range(4):
            dve.op(lambda e, g=g: e.tensor_tensor(wmT[:, g, :], wmf[:, g, :], cstf[:, 1, :], ALU.mult), [wmfB, cstfB], [wmTB])
        dve.op(lambda e: e.tensor_copy(sbrb, sbr), [sbrB], [sbrbB])
        pool.op(lambda e: e.memset(ahalo, 0.0), [], [ahaloB])
        pool.op(lambda e: e.memset(fhalo, 0.0), [], [fhaloB])
        pool.op(lambda e: e.memset(Gs, NEG), [], [GsB])

        for ti in range(NT):
            t0 = ti * T
            declare()
            declare_banks()
            sp.dma(xt, x_src[t0:t0 + T, :].rearrange("(j p) d -> p j d", p=128), [x_srcB], [xtB], xtB)
            rmsnorm_to_xnT(0)
            sp.dma(rp, ropet[:, :, t0:t0 + T], [NOB], [rpB], rpB)

            for qi in range(4):
                wt, wB = w_next("P%d" % qi)
                isq = qi < 2
                for m in range(4):
                    h = (qi % 2) * 4 + m
                    dst, dstH = (qs, qsH) if isq else (ks, ksH)
                    b = fm_chunk(wt, wB, m, xnT, xnTC)
                    sc = HD ** -0.5 if isq else 1.0
                    act.op(lambda e, b=b, h=h, dst=dst, sc=sc: e.activation(out=dst[:, h, :], in_=bk(b)[:], func=AF.Copy, scale=sc),
                           [bkB(b)], [dstH[h]])
                    pe.op(lambda e, h=h, dst=dst: e.matmul(bk(5)[0:32, :], pmb, dst[0:32, h, :], start=True, stop=True),
                          [pmbB, dstH[h]], [bkB(5)])
                    dve.op(lambda e: e.tensor_tensor(rt2, bk(5)[0:32, :], rp[:, 1, :], ALU.mult), [bkB(5), rpB], [rt2B])
                    pool.op(lambda e, h=h, dst=dst: e.tensor_tensor(rt1, dst[0:32, h, :], rp[:, 0, :], ALU.mult), [dstH[h], rpB], [rt1B])
                    dve.op(lambda e, h=h, dst=dst: e.tensor_tensor(dst[0:32, h, :], rt1, rt2, ALU.add), [rt1B, rt2B], [dstH[h]])
                    if not isq:
                        for half in range(2):
                            nblk = 2 * ti + half
                            dve.op(lambda e, h=h, half=half: e.tensor_reduce(out=kms[:, half:half + 1], in_=ks[:, h, half * BLK:(half + 1) * BLK],
                                                                             axis=AX.X, op=ALU.add), [ksH[h]], [kmsB])
                            dve.op(lambda e, h=h, half=half, nblk=nblk: e.tensor_scalar(kmean[:, h, nblk:nblk + 1], kms[:, half:half + 1], 1.0 / BLK, None, op0=ALU.mult),
                                   [kmsB], [kmeanB])
            act.dma(kT[l][:, :, t0:t0 + T].rearrange("h d t -> d h t"), ks, [ksB], [kTB[l]], ksB)
            pool.op(lambda e: e.memset(vs[:, :, :, HD:HD + 1], 1.0), [], [vsB])
            for vi in range(2):
                wt, wB = w_next("P%d" % (4 + vi))
                for j in range(NJ):
                    b = tm_block(wt, wB, j, xnT, xnTC)
                    eng = ev_eng()
                    src_v = bk(b)[:].rearrange("p (h d) -> p h d", h=4)
                    if eng is dve:
                        dve.op(lambda e, j=j, vi=vi, src_v=src_v: e.tensor_copy(vs[:, j, vi * 4:(vi + 1) * 4, 0:HD], src_v), [bkB(b)], [vsJ[j]])
                    else:
                        act.op(lambda e, j=j, vi=vi, src_v=src_v: e.activation(out=vs[:, j, vi * 4:(vi + 1) * 4, 0:HD], in_=src_v, func=AF.Copy), [bkB(b)], [vsJ[j]])
            for j in range(NJ):
                act.dma(vS[l][t0 + j * 128:t0 + (j + 1) * 128, :, :], vs[:, j, :, 0:HD], [vsB], [vSB[l]], vsB)
            wt, wB = w_next("P6")
            for m in range(4):
                b = fm_chunk(wt, wB, m, xnT, xnTC)
                act.op(lambda e, b=b, m=m: e.activation(out=sgc[:, m, :], in_=bk(b)[:], func=AF.Sigmoid), [bkB(b)], [sgcB])
            pool.op(lambda e: e.tensor_copy(aext[:, :, 0:CK - 1], ahalo), [ahaloB], [aextB])
            wt, wB = w_next("P7")
            for m in range(4):
                b = fm_chunk(wt, wB, m, xnT, xnTC)
                dve.op(lambda e, b=b, m=m: e.tensor_tensor(aext[:, m, CK - 1:CK - 1 + T], bk(b)[:], sgc[:, m, :], ALU.mult), [bkB(b), sgcB], [aextB])
            pool.op(lambda e: e.tensor_copy(ahalo, aext[:, :, T:T + CK - 1]), [aextB], [ahaloB])
            wt, wB = w_next("P8")
            for m in range(4):
                b = fm_chunk(wt, wB, m, xnT, xnTC)
                act.op(lambda e, b=b, m=m: e.activation(out=sus[:, m, :], in_=bk(b)[:], func=AF.Gelu_apprx_tanh), [bkB(b)], [susB])
            wt, wB = w_next("P9")
            for j in range(NJ):
                b = tm_block(wt, wB, j, xnT, xnTC)
                act.op(lambda e, b=b: e.activation(out=svg, in_=bk(b)[:], func=AF.Gelu_apprx_tanh), [bkB(b)], [svgB])
                dve.op(lambda e: e.bn_stats(bst, svg), [svgB], [bstB])
                dve.op(lambda e: e.bn_aggr(bag, bst), [bstB], [bagB])
                act.op(lambda e: e.activation(out=bag[:, 1:2], in_=bag[:, 1:2], func=AF.Sqrt, bias=EPS), [bagB], [bagB])
                dve.op(lambda e: e.reciprocal(bag[:, 1:2], bag[:, 1:2]), [bagB], [bagB])
                dve.op(lambda e: e.tensor_scalar(svg, svg, bag[:, 0:1], bag[:, 1:2], op0=ALU.subtract, op1=ALU.mult), [svgB, bagB], [svgB])
                dve.op(lambda e: e.tensor_tensor(svg, svg, sgB[:, 0, :], ALU.mult), [svgB, sgBB], [svgB])
                dve.op(lambda e, j=j: e.tensor_tensor(svn[:, j, :], svg, sgB[:, 1, :], ALU.add), [svgB, sgBB], [svnB])

            for m in range(4):
                b = mm_bank()
                for k in range(CK):
                    dg_t, dg_B = dg[dgq["i"] % 8]
                    dgq["i"] += 1
                    dve.op(lambda e, m=m, k=k, dg_t=dg_t: e.tensor_scalar(dg_t, idb, cw[:, m, k:k + 1], None, op0=ALU.mult), [idbB, cwB], [dg_B])
                    pe.op(lambda e, m=m, k=k, b=b, dg_t=dg_t: e.matmul(bk(b)[:], dg_t, aext[:, m, k:k + T], start=(k == 0), stop=(k == CK - 1)),
                          [dg_B, aextB], [bkB(b)], signal=True)
                act.op(lambda e, m=m, b=b: e.activation(out=cacc[:, m, :], in_=bk(b)[:], func=AF.Identity, bias=cw[:, m, CK:CK + 1]),
                       [bkB(b), cwB], [caccM[m]])

            for g in range(4):
                for j in range(NJ):
                    pe.op(lambda e, g=g, j=j: e.matmul(bk(5)[:, j * 128:(j + 1) * 128], svn[:, j, g * 128:(g + 1) * 128], wmT[:, g, :], start=True, stop=False),
                          [svnB, wmTB], [bkB(5)], signal=False)
                    pe.op(lambda e, g=g, j=j: e.matmul(bk(5)[:, j * 128:(j + 1) * 128], onesr, sbrb[:, g * 128:(g + 1) * 128], start=False, stop=True),
                          [onesrB, sbrbB], [bkB(5)], signal=(j == NJ - 1))
                dve.op(lambda e, g=g: e.tensor_tensor(ybin[:, g, :], bk(5)[:], sus[:, g, :], ALU.mult), [bkB(5), susB], [ybinB])

            attention(l, ti)

            for m in range(4):
                act.op(lambda e, m=m: e.activation(out=csq, in_=cacc[:, m, :], func=AF.Square), [caccB], [csqB])
                pe.op(lambda e, m=m: e.matmul(bk(4)[:], onesf, cacc[:, m, :], start=(m == 0), stop=(m == 3)), [onesfB, caccB], [bkB(4)], signal=(m == 3))
                pe.op(lambda e, m=m: e.matmul(bk(5)[:], onesf, csq, start=(m == 0), stop=(m == 3)), [onesfB, csqB], [bkB(5)], signal=True)
            dve.op(lambda e: e.tensor_scalar(lnm, bk(4)[:], 1.0 / CC, None, op0=ALU.mult), [bkB(4)], [lnmB])
            dve.op(lambda e: e.tensor_tensor(csq, lnm, lnm, ALU.mult), [lnmB], [csqB])
            dve.op(lambda e: e.scalar_tensor_tensor(lnr, bk(5)[:], 1.0 / CC, csq, ALU.mult, ALU.subtract), [bkB(5), csqB], [lnrB])
            act.op(lambda e: e.activation(out=lnr, in_=lnr, func=AF.Sqrt, bias=EPS), [lnrB], [lnrB])
            dve.op(lambda e: e.reciprocal(lnr, lnr), [lnrB], [lnrB])
            for m in range(4):
                dve.op(lambda e, m=m: e.tensor_tensor(cacc[:, m, :], cacc[:, m, :], lnm, ALU.subtract), [caccB, lnmB], [caccB])
                dve.op(lambda e, m=m: e.tensor_tensor(cacc[:, m, :], cacc[:, m, :], lnr, ALU.mult), [caccB, lnrB], [caccB])
                act.op(lambda e, m=m: e.activation(out=acta[:, m, :], in_=cacc[:, m, :], func=AF.Silu, scale=cw[:, m, CK + 1:CK + 2], bias=cw[:, m, CK + 2:CK + 3]),
                       [caccB, cwB], [actaB])

            srcs = [(acta, actaB, 4, 0), (ybin, ybinB, 4, 4), (atf, atfB, 8, 8)]
            for c in range(16):
                wy, wyB = w_next("YG%d" % c)
                for g in range(3):
                    bg = fm_chunk(wy, wyB, 0, xnT, xnTC, nkc=16, kc0=16 + 16 * g)
                    act.op(lambda e, bg=bg: e.activation(out=sga, in_=bk(bg)[:], func=AF.Sigmoid), [bkB(bg)], [sgaB])
                    src, srcB, nkc, kc0 = srcs[g]
                    by = fm_chunk(wy, wyB, 0, src, srcB, nkc=nkc, kc0=kc0)
                    if g == 0:
                        dve.op(lambda e, by=by: e.tensor_tensor(mt1, bk(by)[:], sga, ALU.mult), [bkB(by), sgaB], [mt1B])
                    elif g == 1:
                        dve.op(lambda e, by=by: e.tensor_tensor(mt2, bk(by)[:], sga, ALU.mult), [bkB(by), sgaB], [mt2B])
                        pool.op(lambda e: e.tensor_tensor(mt1, mt1, mt2, ALU.add), [mt1B, mt2B], [mt1B])
                    else:
                        dve.op(lambda e, by=by: e.tensor_tensor(mt2, bk(by)[:], sga, ALU.mult), [bkB(by), sgaB], [mt2B])
                        pool.op(lambda e, c=c: e.tensor_tensor(mrg[:, c, :], mt1, mt2, ALU.add), [mt1B, mt2B], [mrgC[c]])

            for nb in range(4):
                wt, wB = w_next("O%d" % nb)
                for j in range(NJ):
                    b = tm_block(wt, wB, j, mrg, mrgC)
                    tm_evac(b, j, nb, hbm, hbmJ)
            post_norm_residual(l, 0, hbm, hbmJ)

            rmsnorm_to_xnT(1)
            for i in range(11):
                wt, wB = w_next("FA%d" % i)
                for m in range(4):
                    c = i * 4 + m
                    b = fm_chunk(wt, wB, m, xnT, xnTC)
                    act.op(lambda e, b=b, m=m: e.activation(out=ae[:, m, 2:2 + T], in_=bk(b)[:], func=AF.Copy), [bkB(b)], [aeM[m]])
                    pool.op(lambda e, m=m, c=c: e.tensor_copy(ae[:, m, 0:2], fhalo[:, c, :]), [fhaloB], [aeM[m]])
                    pool.op(lambda e, m=m, c=c: e.tensor_copy(fhalo[:, c, :], ae[:, m, T:T + 2]), [aeM[m]], [fhaloB])
                    b2 = mm_bank()
                    for k in range(3):
                        dg_t, dg_B = dg[dgq["i"] % 8]
                        dgq["i"] += 1
                        dve.op(lambda e, c=c, k=k, dg_t=dg_t: e.tensor_scalar(dg_t, idb, fw[:, c, k:k + 1], None, op0=ALU.mult), [idbB, fwB], [dg_B])
                        pe.op(lambda e, m=m, k=k, b2=b2, dg_t=dg_t: e.matmul(bk(b2)[:], dg_t, ae[:, m, k:k + T], start=(k == 0), stop=(k == 2)),
                              [dg_B, aeM[m]], [bkB(b2)], signal=True)
                    act.op(lambda e, m=m, c=c, b2=b2: e.activation(out=fga[:, m, :], in_=bk(b2)[:], func=AF.Gelu_apprx_tanh, bias=fw[:, c, 3:4]),
                           [bkB(b2), fwB], [fgaM[m]])
                wt, wB = w_next("FB%d" % i)
                for m in range(4):
                    c = i * 4 + m
                    b = fm_chunk(wt, wB, m, xnT, xnTC)
                    dve.op(lambda e, b=b, m=m, c=c: e.tensor_tensor(gff[:, c, :], bk(b)[:], fga[:, m, :], ALU.mult), [bkB(b), fgaM[m]], [gffC[c]])
            for nb in range(4):
                for kg in range(4):
                    wt, wB = w_next("FO%d_%d" % (nb, kg))
                    for j in range(NJ):
                        tm_block(wt, wB, j, gff, gffC, nkc=11, b=j, first=(kg == 0), last=(kg == 3), c_off=kg * 11)
                for j in range(NJ):
                    tm_evac(j, j, nb, hbf, hbfJ)
            post_norm_residual(l, 1, hbf, hbfJ)

            rmsnorm_to_xnT(2)
            sp.dma(pt, p_in[l, t0:t0 + T, :].rearrange("(j p) d -> p j d", p=128), [NOB], [ptB], ptB)
            dve.op(lambda e: e.tensor_copy(ptb16, pt), [ptB], [ptb16B])
            pv = bk(5)[:].bitcast(BF16)
            for c in range(2):
                for j in range(NJ):
                    pe.op(lambda e, j=j, c=c: e.transpose(pv[:, j * 128:(j + 1) * 128], ptb16[:, j, c * 128:(c + 1) * 128], idb),
                          [ptb16B, idbB], [bkB(5)], signal=(j == NJ - 1))
                dve.op(lambda e, c=c: e.tensor_copy(pT[:, c, :], pv[:, 0:T]), [bkB(5)], [pTB])
            for nb in range(4):
                wt, wB = w_next("PG%d" % nb)
                gbanks = []
                for j in range(NJ):
                    gbanks.append(tm_block(wt, wB, j, xnT, xnTC, b=j))
                wp, wpB = w_next("PP%d" % nb)
                for j in range(NJ):
                    b = gbanks[j]
                    act.op(lambda e, b=b: e.activation(out=gsb, in_=bk(b)[:], func=AF.Sigmoid), [bkB(b)], [gsbB])
                    b2 = 4 + (j % 2)
                    for c in range(2):
                        pe.op(lambda e, c=c, j=j, b2=b2: e.matmul(bk(b2)[:], pT[:, c, j * 128:(j + 1) * 128], wp[:, c, :], start=(c == 0), stop=(c == 1)),
                              [pTB, wpB], [bkB(b2)], signal=(c == 1))
                    dve.op(lambda e, b2=b2: e.tensor_tensor(gsb, gsb, bk(b2)[:], ALU.mult), [gsbB, bkB(b2)], [gsbB])
                    pool.op(lambda e, j=j, nb=nb: e.tensor_tensor(xt[:, j, nb * 512:(nb + 1) * 512], xt[:, j, nb * 512:(nb + 1) * 512], gsb, ALU.add),
                            [gsbB, xtJ[j]], [xtJ[j]])
            act.dma(x_dst[t0:t0 + T, :].rearrange("(j p) d -> p j d", p=128), xt, [xtB], [x_dstB], xtB)

    sp.wait_all([xtB, ksB, vsB, outB] + kTB + vSB)
    act.wait_all([xtB, ksB, vsB, outB])
    pes[0].close()
    K.es.close()
    global _LAST_KB
    _LAST_KB = K
    return nc


def host_consts(S):
    cst = np.zeros((128, 3, 128), np.float32)
    cst[:, 0, :] = np.eye(128, dtype=np.float32)
    cst[:, 1, :] = np.triu(np.ones((128, 128), np.float32))
    for m in range(32):
        cst[(m + 16) % 32, 2, m] = 1.0
    half = ROPE // 2
    inv = np.float32(500000.0) ** (-np.arange(0, ROPE, 2, dtype=np.float32) / np.float32(ROPE))
    ang = np.arange(S, dtype=np.float32)[:, None] * inv[None, :].astype(np.float32)
    cos = np.cos(ang.astype(np.float64)).astype(np.float32).T
    sin = np.sin(ang.astype(np.float64)).astype(np.float32).T
    ropet = np.zeros((ROPE, 2, S), np.float32)
    ropet[0:half, 0] = cos
    ropet[half:, 0] = cos
    ropet[0:half, 1] = -sin
    ropet[half:, 1] = sin
    return cst, ropet


def layout_params(inp, depth):
    f = lambda a: np.ascontiguousarray(np.asarray(a, dtype=np.float32))
    L = depth
    toP = lambda v: v.reshape(L, -1, 128).transpose(0, 2, 1)
    vecP = np.stack([toP(f(inp["mix_norm_pre"])), toP(f(inp["ffn_norm_pre"])), toP(f(inp["ple_norm"]))], axis=2)
    vecB = np.stack([np.broadcast_to(f(inp["mix_norm_post"])[:, None, :], (L, 128, D)),
                     np.broadcast_to(f(inp["ffn_norm_post"])[:, None, :], (L, 128, D))], axis=1)
    cw = f(inp["conv_dw_w"]).reshape(L, CK, 4, 128).transpose(0, 3, 2, 1)
    extra = np.stack([toP(f(inp["conv_dw_b"])), toP(f(inp["conv_norm_g"])), toP(f(inp["conv_norm_b"]))], axis=3)
    convw = np.concatenate([cw, extra], axis=3)
    sguB = np.stack([np.broadcast_to(f(inp["sgu_norm_g"])[:, None, :], (L, 128, SW)),
                     np.broadcast_to(f(inp["sgu_norm_b"])[:, None, :], (L, 128, SW))], axis=1)
    sguwT = f(inp["sgu_w"]).transpose(0, 3, 1, 2)
    sgub = f(inp["sgu_b"]).reshape(L, 1, 4 * 128)
    fwt = f(inp["ffn_dw_w"]).reshape(L, 3, 44, 128).transpose(0, 3, 2, 1)
    fwb = toP(f(inp["ffn_dw_b"]))[..., None]
    ffnw = np.concatenate([fwt, fwb], axis=3)
    c = np.ascontiguousarray
    return dict(vecP=c(vecP), vecB=c(vecB), convw=c(convw), sguB=c(sguB), sguwT=c(sguwT), sgub=c(sgub), ffnw=c(ffnw))


_CACHE = {}


def run(inp, S, depth, nseq):
    key = (S, depth)
    if key not in _CACHE:
        _CACHE[key] = build(S, depth)
    nc = _CACHE[key]
    cst, ropet = host_consts(S)
    prm = layout_params(inp, depth)
    f = lambda a: np.ascontiguousarray(np.asarray(a, dtype=np.float32))
    shared = dict(prm)
    shared.update(cst=cst, ropet=ropet)
    for k in ("w_in", "conv_out", "sgu_out", "attn_out", "w_o", "ffn_in", "ffn_out", "ple_gate", "ple_proj"):
        shared[k] = f(inp[k])
    x = f(inp["x"])
    p = f(inp["p"])
    in_maps = []
    for b in range(nseq):
        m = dict(shared)
        m["x"] = np.ascontiguousarray(x[b])
        m["p"] = np.ascontiguousarray(p[:, b])
        in_maps.append(m)
    res = run_bass_kernel_spmd(nc, in_maps, core_ids=list(range(nseq)))
    return np.stack([np.asarray(res.results[b]["out"], dtype=np.float32) for b in range(nseq)], axis=0)


def kernel(**inputs):
    x = np.asarray(inputs["x"])
    B, S, _ = x.shape
    depth = np.asarray(inputs["w_in"]).shape[0]
    return run(inputs, S, depth, B)
```

```python
import numpy as np
from contextlib import ExitStack
import concourse.bass as bass
import concourse.mybir as mybir
from concourse.bass_utils import run_bass_kernel_spmd

F32 = mybir.dt.float32
BF16 = mybir.dt.bfloat16
AF = mybir.ActivationFunctionType
ALU = mybir.AluOpType
AX = mybir.AxisListType

D = 2048
NH = 8
HD = 128
ROPE = 32
BLK = 256
TOPK = 3
CC = 512
CK = 31
SW = 512
DFF = 5632
PLE = 256
INW = 11264
EPS = 1e-6
T = 512
NJ = 4
NEG = -1.0e30
SEM_ROT = 30000


class Buf:
    def __init__(self, name, parent=None):
        self.name = name
        self.parent = parent
        self.children = []
        if parent is not None:
            parent.children.append(self)
        self.w = {}
        self.r = {}
        self.dsem = None
        self.dtotal = 0
        self.excl = False

    def family(self):
        out = [self]
        p = self.parent
        while p is not None:
            out.append(p)
            p = p.parent
        stack = list(self.children)
        while stack:
            c = stack.pop()
            out.append(c)
            stack.extend(c.children)
        return out


class Eng:
    def __init__(self, K, name, eng, is_pe=False):
        self.K = K
        self.name = name
        self.eng = eng
        self.is_pe = is_pe
        self.sem = K.new_sem("e_" + name)
        self.own = [self.sem]
        self.cnt = 0
        self.known = {}
        self.pend_r = []
        self.pend_w = []

    def _wait(self, deps):
        for sem, val in deps.items():
            if self.is_pe and any(sem is o for o in self.own):
                continue
            if self.known.get(sem, 0) >= val:
                continue
            self.eng.wait_ge(sem, val)
            self.known[sem] = val

    def _deps(self, reads, writes):
        deps = {}
        for b in reads:
            for f in b.family():
                for s, v in f.w.items():
                    if deps.get(s, 0) < v:
                        deps[s] = v
        for b in writes:
            for f in b.family():
                for s, v in f.w.items():
                    if deps.get(s, 0) < v:
                        deps[s] = v
                for s, v in f.r.items():
                    if deps.get(s, 0) < v:
                        deps[s] = v
        return deps

    def op(self, fn, reads=(), writes=(), signal=True):
        ex = [b for b in reads if b.excl]
        if ex:
            reads = [b for b in reads if not b.excl]
            writes = list(writes) + ex
        self._wait(self._deps(reads, writes))
        ins = fn(self.eng)
        self.pend_r.extend(reads)
        self.pend_w.extend(writes)
        if signal:
            if self.cnt >= SEM_ROT:
                self.sem = self.K.new_sem("e_" + self.name)
                self.own.append(self.sem)
                self.cnt = 0
            self.cnt += 1
            ins.then_inc(self.sem, 1)
            tok = (self.sem, self.cnt)
            for b in self.pend_r:
                if b.r.get(tok[0], 0) < tok[1]:
                    b.r[tok[0]] = tok[1]
            for b in self.pend_w:
                b.w = {tok[0]: tok[1]}
                b.r = {}
            self.pend_r = []
            self.pend_w = []
        return ins

    def dma(self, out, in_, reads, writes, sbuf, **kw):
        self._wait(self._deps(reads, writes))
        if sbuf.dsem is None or sbuf.dtotal >= SEM_ROT:
            sbuf.dsem = self.K.new_sem("d_" + sbuf.name)
            sbuf.dtotal = 0
        ins = self.eng.dma_start(out=out, in_=in_, **kw)
        sbuf.dtotal += 16
        ins.then_inc(sbuf.dsem, 16)
        tok = (sbuf.dsem, sbuf.dtotal)
        for b in reads:
            if b.r.get(tok[0], 0) < tok[1]:
                b.r[tok[0]] = tok[1]
        for b in writes:
            b.w = {tok[0]: tok[1]}
            b.r = {}
        return ins

    def wait_all(self, bufs):
        deps = {}
        for b in bufs:
            for f in b.family():
                for dct in (f.w, f.r):
                    for s, v in dct.items():
                        if deps.get(s, 0) < v:
                            deps[s] = v
        self._wait(deps)


class Kb:
    def __init__(self):
        self.nc = bass.Bass("TRN2", target_bir_lowering=False)
        self.es = ExitStack()
        self.nsem = 0
        nc = self.nc
        self.pe = Eng(self, "pe", nc.tensor, is_pe=True)
        self.act = Eng(self, "act", nc.scalar)
        self.dve = Eng(self, "dve", nc.vector)
        self.pool = Eng(self, "pool", nc.gpsimd)
        self.sp = Eng(self, "sp", nc.sync)

    def new_sem(self, name):
        self.nsem += 1
        return self.es.enter_context(self.nc.semaphore(name + "_%d" % self.nsem))

    def sb(self, name, shape, dt):
        t = self.es.enter_context(self.nc.sbuf_tensor(name, shape, dt))
        return t, Buf(name)

    def ps(self, name, shape, dt):
        t = self.es.enter_context(self.nc.psum_tensor(name, shape, dt))
        b = Buf(name)
        b.excl = True
        return t, b

    def din(self, name, shape, dt=F32):
        return self.nc.dram_tensor(name, list(shape), dt, kind="ExternalInput").ap()

    def dout(self, name, shape, dt=F32):
        return self.nc.dram_tensor(name, list(shape), dt, kind="ExternalOutput").ap()

    def dint(self, name, shape, dt):
        return self.nc.dram_tensor(name, list(shape), dt).ap()


def weight_blocks():
    blks = []
    for i in range(10):
        col = {0: 0, 1: 512, 2: 1024, 3: 1536, 4: 2048, 5: 2560, 6: 3584, 7: 3072, 8: 4096, 9: 4608}[i]
        blks.append(("P%d" % i, 512, [("w_in", 0, 2048, col)]))
    for c in range(16):
        blks.append(("YG%d" % c, 128, [("conv_out", 0, 512, c * 128), ("sgu_out", 0, 512, c * 128),
                                       ("attn_out", 0, 1024, c * 128), ("w_in", 0, 2048, 5120 + c * 128),
                                       ("w_in", 0, 2048, 7168 + c * 128), ("w_in", 0, 2048, 9216 + c * 128)]))
    for nb in range(4):
        blks.append(("O%d" % nb, 512, [("w_o", 0, 2048, nb * 512)]))
    for i in range(11):
        blks.append(("FA%d" % i, 512, [("ffn_in", 0, 2048, i * 512)]))
        blks.append(("FB%d" % i, 512, [("ffn_in", 0, 2048, DFF + i * 512)]))
    for nb in range(4):
        for kg in range(4):
            blks.append(("FO%d_%d" % (nb, kg), 512, [("ffn_out", kg * 1408, 1408, nb * 512)]))
    for nb in range(4):
        blks.append(("PG%d" % nb, 512, [("ple_gate", 0, 2048, nb * 512)]))
        blks.append(("PP%d" % nb, 512, [("ple_proj", 0, 256, nb * 512)]))
    return blks


WBLKS = weight_blocks()
NWB = len(WBLKS)
GRP = 4
NKB = 3
NPT = 4
SBANKS = (0, 1, 6, 7)


class Arena:
    def __init__(self, K, nbytes):
        self.K = K
        self.n = nbytes
        slab = K.es.enter_context(K.nc.sbuf_tensor("arena", [128, nbytes // 2], BF16))
        self.base = K.nc.lookup_mloc(slab).addr
        self.bufs = {}
        self.top = 0
        self.gen = 0

    def begin(self):
        self.top = 0
        self.gen += 1

    def carve(self, name, shape, dt, at=None):
        esz = 4 if dt == F32 else 2
        n = 1
        for d in shape[1:]:
            n *= d
        nb = n * esz
        nb_al = (nb + 31) // 32 * 32
        if at is None:
            at = self.top
            self.top += nb_al
        assert at % 4 == 0 and at + nb <= self.n, (name, at, nb, self.n)
        h = self.K.nc.alloc_sbuf_tensor_at("%s_g%d" % (name, self.gen), list(shape), dt, offset=self.base + at)
        ap = h.ap()
        b = self.bufs.get(name)
        if b is None:
            b = Buf(name)
            b.lo, b.hi = at, at + nb
            b.fam = None
            self.bufs[name] = b
        else:
            assert (b.lo, b.hi) == (at, at + nb), name
        return ap, b

    def sub(self, parent, name, lo_off, nbytes):
        b = self.bufs.get(name)
        if b is None:
            b = Buf(name)
            b.lo, b.hi = parent.lo + lo_off, parent.lo + lo_off + nbytes
            b.fam = None
            self.bufs[name] = b
        return b

    def finalize(self):
        bl = list(self.bufs.values())
        for b in bl:
            b.fam = [o for o in bl if o.lo < b.hi and b.lo < o.hi]


def _family(self):
    f = getattr(self, "fam", None)
    if f is not None:
        return f
    return [self]


Buf.family = _family


def build(S, depth):
    NT = S // T
    NB = S // BLK
    assert NB <= 64
    K = Kb()
    nc = K.nc
    pe, act, dve, pool, sp = K.pe, K.act, K.dve, K.pool, K.sp

    x_in = K.din("x", [S, D])
    p_in = K.din("p", [depth, S, PLE])
    win = {"w_in": K.din("w_in", [depth, D, INW]), "conv_out": K.din("conv_out", [depth, CC, D]),
           "sgu_out": K.din("sgu_out", [depth, SW, D]), "attn_out": K.din("attn_out", [depth, NH * HD, D]),
           "w_o": K.din("w_o", [depth, D, D]), "ffn_in": K.din("ffn_in", [depth, D, 2 * DFF]),
           "ffn_out": K.din("ffn_out", [depth, DFF, D]), "ple_gate": K.din("ple_gate", [depth, D, D]),
           "ple_proj": K.din("ple_proj", [depth, PLE, D])}
    vecP = K.din("vecP", [depth, 128, 3, 16])
    vecB = K.din("vecB", [depth, 2, 128, D])
    convw = K.din("convw", [depth, 128, 4, CK + 3])
    sguB = K.din("sguB", [depth, 2, 128, SW])
    sguwT = K.din("sguwT", [depth, 128, 4, 128])
    sgub = K.din("sgub", [depth, 1, 4 * 128])
    ffnw = K.din("ffnw", [depth, 128, 44, 4])
    cst = K.din("cst", [128, 3, 128])
    ropet = K.din("ropet", [ROPE, 2, S])
    out = K.dout("out", [S, D])

    wb = [K.dint("wb%d" % l, [NWB, 128, 16 * 512], BF16) for l in range(depth)]
    wbB = [Buf("wbB%d" % l) for l in range(depth)]
    kT = [K.dint("kT%d" % l, [NH, HD, S], BF16) for l in range(depth)]
    vS = [K.dint("vS%d" % l, [S, NH, HD], BF16) for l in range(depth)]
    kTB = [Buf("kT%d" % l) for l in range(depth)]
    vSB = [Buf("vS%d" % l) for l in range(depth)]
    xs = [K.dint("xs%d" % l, [S, D], F32) for l in range(max(depth - 1, 1))]
    xsB = [Buf("xs%d" % l) for l in range(max(depth - 1, 1))]
    outB = Buf("outB")
    NOB = Buf("extern")

    for l in range(depth):
        for bi, (tag, ncol, parts) in enumerate(WBLKS):
            kc0 = 0
            for (src, r0, nr, c0) in parts:
                nkc = nr // 128
                src_ap = win[src][l, r0:r0 + nr, c0:c0 + ncol].rearrange("(c p) n -> p c n", p=128)
                dst_ap = wb[l][bi, :, kc0 * ncol:(kc0 + nkc) * ncol].rearrange("p (c n) -> p c n", n=ncol)
                pool.dma(dst_ap, src_ap, [NOB], [], wbB[l])
                kc0 += nkc
        wbB[l].w = {wbB[l].dsem: wbB[l].dtotal}

    A = Arena(K, 207 * 1024)
    cv = A.carve
    dg = wsl = xt = xtB = xtJ = vB = vBB = sgB = sgBB = vP = vPB = cw = cwB = wmT = wmTB = sbrb = sbrbB = onesr = onesrB = fw = fwB = idb = idbB = trib = tribB = pmb = pmbB = onesf = onesfB = cstf = cstfB = kmean = kmeanB = Gs = GsB = ahalo = ahaloB = fhalo = fhaloB = ss = ssB = rstd = rstdB = ssq = ssqB = top8 = top8B = bst = bstB = bag = bagB = kms = kmsB = rinv = rinvB = junk = junkB = ptmp = ptmpB = R0 = xnT = xnTB = xnTC = o = qs = qsB = qsH = ks = ksB = ksH = vs = vsB = vsJ = o2 = aext = aextB = o3 = cacc = caccB = caccM = sgc = sgcB = o4 = sus = susB = svn = svnB = acta = actaB = ybin = ybinB = o5 = lnm = lnmB = lnr = lnrB = csq = csqB = svg = svgB = o6 = atf = atfB = o7 = sel = acc = atk = o8 = kbuf = vbuf = ptb = o9 = xnb = xnbB = xnbJ = rp = rpB = rt1 = rt1B = rt2 = rt2B = mrg = mrgB = mrgC = sga = sgaB = mt1 = mt1B = mt2 = mt2B = hbm = hbmB = hbmJ = ae = aeB = aeM = fca = fcaB = fcaM = fga = fgaB = fgaM = gff = gffB = gffC = hbf = hbfB = hbfJ = pt = ptB = ptb16 = ptb16B = pT = pTB = gsb = gsbB = wmf = wmfB = sbr = sbrB = None

    def declare():
        nonlocal dg, wsl, xt, xtB, xtJ, vB, vBB, sgB, sgBB, vP, vPB, cw, cwB, wmT, wmTB, sbrb, sbrbB, onesr, onesrB, fw, fwB, idb, idbB, trib, tribB, pmb, pmbB, onesf, onesfB, cstf, cstfB, kmean, kmeanB, Gs, GsB, ahalo, ahaloB, fhalo, fhaloB, ss, ssB, rstd, rstdB, ssq, ssqB, top8, top8B, bst, bstB, bag, bagB, kms, kmsB, rinv, rinvB, junk, junkB, ptmp, ptmpB, R0, xnT, xnTB, xnTC, o, qs, qsB, qsH, ks, ksB, ksH, vs, vsB, vsJ, o2, aext, aextB, o3, cacc, caccB, caccM, sgc, sgcB, o4, sus, susB, svn, svnB, acta, actaB, ybin, ybinB, o5, lnm, lnmB, lnr, lnrB, csq, csqB, svg, svgB, o6, atf, atfB, o7, sel, acc, atk, o8, kbuf, vbuf, ptb, o9, xnb, xnbB, xnbJ, rp, rpB, rt1, rt1B, rt2, rt2B, mrg, mrgB, mrgC, sga, sgaB, mt1, mt1B, mt2, mt2B, hbm, hbmB, hbmJ, ae, aeB, aeM, fca, fcaB, fcaM, fga, fgaB, fgaM, gff, gffB, gffC, hbf, hbfB, hbfJ, pt, ptB, ptb16, ptb16B, pT, pTB, gsb, gsbB, wmf, wmfB, sbr, sbrB
        A.begin()
        wsl = [cv("wsl%d" % i, [128, 16 * 512], BF16) for i in range(2)]
        xt, xtB = cv("xt", [128, NJ, D], F32)
        xtJ = [A.sub(xtB, "xt%d" % j, j * D * 4, D * 4) for j in range(NJ)]
        vB, vBB = cv("vB", [128, D], F32)
        sgB, sgBB = cv("sgB", [128, 2, SW], F32)
        vP, vPB = cv("vP", [128, 3, 16], F32)
        cw, cwB = cv("cw", [128, 4, CK + 3], F32)
        wmT, wmTB = cv("wmT", [128, 4, 128], BF16)
        sbrb, sbrbB = cv("sbrb", [1, 4 * 128], BF16)
        onesr, onesrB = cv("onesr", [1, 128], BF16)
        fw, fwB = cv("fw", [128, 44, 4], F32)
        idb, idbB = cv("idb", [128, 128], BF16)
        trib, tribB = cv("trib", [128, 128], BF16)
        pmb, pmbB = cv("pmb", [32, 32], BF16)
        onesf, onesfB = cv("onesf", [128, 128], F32)
        cstf, cstfB = cv("cstf", [128, 3, 128], F32)
        kmean, kmeanB = cv("kmean", [128, NH, 64], BF16)
        Gs, GsB = cv("Gs", [128, NJ, 64], F32)
        ahalo, ahaloB = cv("ahalo", [128, 4, CK - 1], BF16)
        fhalo, fhaloB = cv("fhalo", [128, 44, 2], BF16)
        dg = [cv("dg%d" % i, [128, 128], BF16) for i in range(8)]
        ss, ssB = cv("ss", [128, 8], F32)
        rstd, rstdB = cv("rstd", [128, 8], F32)
        ssq, ssqB = cv("ssq", [128, NJ, 4], F32)
        top8, top8B = cv("top8", [128, 8], F32)
        bst, bstB = cv("bst", [128, 6], F32)
        bag, bagB = cv("bag", [128, 2], F32)
        kms, kmsB = cv("kms", [128, 2], F32)
        rinv, rinvB = cv("rinv", [128, NJ], F32)
        junk, junkB = cv("junk", [128, D], BF16)
        ptmp, ptmpB = cv("ptmp", [128, T], F32)
        R0 = A.top
        xnT, xnTB = cv("xnT", [128, 16, T], BF16, at=R0)
        xnTC = [A.sub(xnTB, "xnT%d" % c, c * T * 2, T * 2) for c in range(16)]
        o = R0 + 16384
        qs, qsB = cv("qs", [128, NH, T], BF16, at=o)
        qsH = [A.sub(qsB, "qs%d" % h, h * T * 2, T * 2) for h in range(NH)]
        ks, ksB = cv("ks", [128, NH, T], BF16, at=o + 8192)
        ksH = [A.sub(ksB, "ks%d" % h, h * T * 2, T * 2) for h in range(NH)]
        vs, vsB = cv("vs", [128, NJ, NH, HD + 1], BF16, at=o + 16384)
        vsJ = [A.sub(vsB, "vs%d" % j, j * NH * (HD + 1) * 2, NH * (HD + 1) * 2) for j in range(NJ)]
        o2 = o + 16384 + 8256
        aext, aextB = cv("aext", [128, 4, CK - 1 + T], BF16, at=o2)
        o3 = o2 + 8672
        cacc, caccB = cv("cacc", [128, 4, T], F32, at=o3)
        caccM = [A.sub(caccB, "cacc%d" % m, m * T * 4, T * 4) for m in range(4)]
        sgc, sgcB = cv("sgc", [128, 4, T], F32, at=o3)
        o4 = o3 + 8192
        sus, susB = cv("sus", [128, 4, T], BF16, at=o4)
        svn, svnB = cv("svn", [128, NJ, SW], BF16, at=o4 + 4096)
        acta, actaB = cv("acta", [128, 4, T], BF16, at=o4 + 8192)
        ybin, ybinB = cv("ybin", [128, 4, T], BF16, at=o4 + 12288)
        o5 = o4 + 16384
        lnm, lnmB = cv("lnm", [128, T], F32, at=o5)
        lnr, lnrB = cv("lnr", [128, T], F32, at=o5 + 2048)
        csq, csqB = cv("csq", [128, T], F32, at=o5 + 4096)
        svg, svgB = cv("svg", [128, SW], F32, at=o5 + 4096)
        o6 = o5 + 6144
        atf, atfB = cv("atf", [128, NH, T], BF16, at=o6)
        o7 = o6 + 8192
        sel = [cv("sel%d" % i, [128, NJ, 64], F32, at=o7 + i * 1024) for i in range(2)]
        acc = [cv("acc%d" % i, [128, NJ, HD + 1], F32, at=o7 + 2048 + i * 2080) for i in range(2)]
        atk = [cv("atk%d" % i, [128, NJ, HD], BF16, at=o7 + 6208 + i * 1024) for i in range(2)]
        o8 = o7 + 8256
        kbuf = [cv("kbuf%d" % i, [128, GRP * BLK], BF16, at=o8 + i * 2048) for i in range(NKB)]
        vbuf = [cv("vbuf%d" % i, [128, GRP * 2, HD + 1], BF16, at=o8 + 6144 + i * 2080) for i in range(NKB)]
        ptb = [cv("ptb%d" % i, [128, T], BF16, at=o8 + 12384 + i * 1024) for i in range(NPT)]
        o9 = o8 + 12384 + 4096
        xnb, xnbB = cv("xnb", [128, NJ, D], BF16, at=o7)
        xnbJ = [A.sub(xnbB, "xnb%d" % j, j * D * 2, D * 2) for j in range(NJ)]
        rp, rpB = cv("rp", [ROPE, 2, T], F32, at=o7 + 16384)
        rt1, rt1B = cv("rt1", [ROPE, T], F32, at=o7 + 16384 + 4096)
        rt2, rt2B = cv("rt2", [ROPE, T], F32, at=o7 + 16384 + 6144)
        assert o7 + 16384 + 8192 <= o9, (o7, o9)
        mrg, mrgB = cv("mrg", [128, 16, T], BF16, at=o)
        mrgC = [A.sub(mrgB, "mrg%d" % c, c * T * 2, T * 2) for c in range(16)]
        sga, sgaB = cv("sga", [128, T], F32, at=o + 16384)
        mt1, mt1B = cv("mt1", [128, T], F32, at=o + 16384 + 2048)
        mt2, mt2B = cv("mt2", [128, T], F32, at=o + 16384 + 4096)
        hbm, hbmB = cv("hbm", [128, NJ, D], F32, at=o2)
        hbmJ = [A.sub(hbmB, "hbm%d" % j, j * D * 4, D * 4) for j in range(NJ)]
        assert o2 + 32768 <= o9
        ae, aeB = cv("ae", [128, 4, T + 2], BF16, at=o)
        aeM = [A.sub(aeB, "ae%d" % m, m * (T + 2) * 2, (T + 2) * 2) for m in range(4)]
        fca, fcaB = cv("fca", [128, 4, T], F32, at=o + 8224)
        fcaM = [A.sub(fcaB, "fca%d" % m, m * T * 4, T * 4) for m in range(4)]
        fga, fgaB = cv("fga", [128, 4, T], BF16, at=o + 16416)
        fgaM = [A.sub(fgaB, "fga%d" % m, m * T * 2, T * 2) for m in range(4)]
        gff, gffB = cv("gff", [128, 44, T], BF16, at=o + 20512)
        gffC = [A.sub(gffB, "gff%d" % c, c * T * 2, T * 2) for c in range(44)]
        assert o + 20512 + 45056 <= o7, (o + 20512 + 45056, o7)
        hbf, hbfB = cv("hbf", [128, NJ, D], F32, at=R0)
        hbfJ = [A.sub(hbfB, "hbf%d" % j, j * D * 4, D * 4) for j in range(NJ)]
        assert R0 + 32768 <= o + 20512
        pt, ptB = cv("pt", [128, NJ, PLE], F32, at=o)
        ptb16, ptb16B = cv("ptb16", [128, NJ, PLE], BF16, at=o + 4096)
        pT, pTB = cv("pT", [128, 2, T], BF16, at=o + 6144)
        gsb, gsbB = cv("gsb", [128, T], F32, at=o + 8192)
        wmf, wmfB = cv("wmf", [128, 4, 128], F32, at=o9 - 4096)
        sbr, sbrB = cv("sbr", [1, 4 * 128], F32, at=o9 - 2048)
        assert o9 <= A.n, (o9, A.n)

    declare()
    A.finalize()

    bankB = []
    for i in range(8):
        bb = Buf("bank%d" % i)
        bb.excl = True
        bankB.append(bb)
    bank = None
    pes = [None]

    def declare_banks():
        nonlocal bank
        if pes[0] is not None:
            pes[0].close()
        pes[0] = ExitStack()
        bank = [(pes[0].enter_context(nc.psum_tensor("bank%d_g%d" % (i, A.gen), [128, 512], F32)), bankB[i]) for i in range(8)]
    declare_banks()

    def bk(i):
        return bank[i][0]

    def bkB(i):
        return bank[i][1]

    sp.dma(cstf, cst[:, :, :], [NOB], [cstfB], cstfB)
    dve.op(lambda e: e.tensor_copy(idb, cstf[:, 0, :]), [cstfB], [idbB])
    dve.op(lambda e: e.tensor_copy(trib, cstf[:, 1, :]), [cstfB], [tribB])
    dve.op(lambda e: e.tensor_copy(pmb, cstf[0:32, 2, 0:32]), [cstfB], [pmbB])
    pool.op(lambda e: e.memset(onesf, 1.0), [], [onesfB])
    pool.op(lambda e: e.memset(onesr, 1.0), [], [onesrB])

    wstate = {"g": 0, "issued": 0}
    NSLOT = 2
    wseq = [(l, bi) for l in range(depth) for ti in range(NT) for bi in range(NWB)]

    def w_issue():
        g = wstate["issued"]
        l, bi = wseq[g]
        s = g % NSLOT
        ncol = WBLKS[bi][1]
        n = sum(p[2] for p in WBLKS[bi][2]) // 128 * ncol
        sp.dma(wsl[s][0][:, 0:n], wb[l][bi, :, 0:n], [wbB[l]], [wsl[s][1]], wsl[s][1])
        wstate["issued"] += 1

    def w_next(expect_tag):
        g = wstate["g"]
        l, bi = wseq[g]
        assert WBLKS[bi][0] == expect_tag, (WBLKS[bi][0], expect_tag)
        while wstate["issued"] < min(g + NSLOT, len(wseq)):
            w_issue()
        wstate["g"] += 1
        t, b = wsl[g % NSLOT]
        ncol = WBLKS[bi][1]
        return t.rearrange("p (c n) -> p c n", n=ncol), b

    evq = {"i": 0}

    def ev_eng():
        evq["i"] += 1
        return act if evq["i"] % 2 else dve

    def rmsnorm_to_xnT(gidx):
        for j in range(NJ):
            act.op(lambda e, j=j: e.activation(out=junk, in_=xt[:, j, :], func=AF.Square, accum_out=ss[:, j:j + 1]),
                   [xtJ[j]], [junkB, ssB])
        act.op(lambda e: e.activation(out=rstd[:, 0:NJ], in_=ss[:, 0:NJ], func=AF.Sqrt, scale=1.0 / D, bias=EPS),
               [ssB], [rstdB])
        dve.op(lambda e: e.reciprocal(rstd[:, 0:NJ], rstd[:, 0:NJ]), [rstdB], [rstdB])
        for j in range(NJ):
            if j % 2 == 0:
                dve.op(lambda e, j=j: e.tensor_scalar(xnb[:, j, :], xt[:, j, :], rstd[:, j:j + 1], None, op0=ALU.mult),
                       [xtJ[j], rstdB], [xnbJ[j]])
            else:
                act.op(lambda e, j=j: e.activation(out=xnb[:, j, :], in_=xt[:, j, :], func=AF.Copy, scale=rstd[:, j:j + 1]),
                       [xtJ[j], rstdB], [xnbJ[j]])
        for c in range(16):
            b = 4 + (c % 4)
            pv = bk(b)[:].bitcast(BF16)
            for j in range(NJ):
                pe.op(lambda e, j=j, c=c, pv=pv: e.transpose(pv[:, j * 128:(j + 1) * 128], xnb[:, j, c * 128:(c + 1) * 128], idb),
                      [xnbJ[j], idbB], [bkB(b)], signal=(j == NJ - 1))
            eng = ev_eng()
            if eng is dve:
                dve.op(lambda e, c=c, pv=pv: e.tensor_scalar(xnT[:, c, :], pv[:, 0:T], vP[:, gidx, c:c + 1], None, op0=ALU.mult),
                       [bkB(b), vPB], [xnTC[c]])
            else:
                act.op(lambda e, c=c, pv=pv: e.activation(out=xnT[:, c, :], in_=pv[:, 0:T], func=AF.Copy, scale=vP[:, gidx, c:c + 1]),
                       [bkB(b), vPB], [xnTC[c]])

    mmq = {"i": 0}
    dgq = {"i": 0}

    def mm_bank():
        mmq["i"] += 1
        return mmq["i"] % 4

    def _sb(srcB, c):
        return [srcB[c]] if isinstance(srcB, list) else [srcB]

    def fm_chunk(wt, wB, m, src, srcB, nkc=16, kc0=0, wcol=128):
        b = mm_bank()
        for c in range(nkc):
            pe.op(lambda e, c=c, b=b: e.matmul(bk(b)[:], wt[:, kc0 + c, m * 128:(m + 1) * 128], src[:, c, :],
                                               start=(c == 0), stop=(c == nkc - 1)),
                  [wB] + _sb(srcB, c), [bkB(b)], signal=(c == nkc - 1))
        return b

    def tm_block(wt, wB, j, src, srcB, nkc=16, b=None, first=True, last=True, c_off=0):
        if b is None:
            b = mm_bank()
        for c in range(nkc):
            pe.op(lambda e, c=c, b=b: e.matmul(bk(b)[:], src[:, c_off + c, j * 128:(j + 1) * 128], wt[:, c, :],
                                               start=(first and c == 0), stop=(last and c == nkc - 1)),
                  [wB] + _sb(srcB, c_off + c), [bkB(b)], signal=(c == nkc - 1))
        return b

    def post_norm_residual(l, gi, hb, hbJ):
        sp.dma(vB, vecB[l, gi], [NOB], [vBB], vBB)
        dve.op(lambda e: e.tensor_reduce(out=ss[:, 4:8], in_=ssq, axis=AX.X, op=ALU.add), [ssqB], [ssB])
        act.op(lambda e: e.activation(out=rstd[:, 4:8], in_=ss[:, 4:8], func=AF.Sqrt, scale=1.0 / D, bias=EPS), [ssB], [rstdB])
        dve.op(lambda e: e.reciprocal(rstd[:, 4:8], rstd[:, 4:8]), [rstdB], [rstdB])
        for j in range(NJ):
            act.op(lambda e, j=j: e.activation(out=hb[:, j, :], in_=hb[:, j, :], func=AF.Copy, scale=rstd[:, 4 + j:5 + j]), [hbJ[j], rstdB], [hbJ[j]])
            dve.op(lambda e, j=j: e.tensor_tensor(hb[:, j, :], hb[:, j, :], vB, ALU.mult), [hbJ[j], vBB], [hbJ[j]])
            pool.op(lambda e, j=j: e.tensor_tensor(xt[:, j, :], xt[:, j, :], hb[:, j, :], ALU.add), [hbJ[j], xtJ[j]], [xtJ[j]])

    def tm_evac(b, j, nb, hb, hbJ):
        act.op(lambda e: e.activation(out=junk[:, 0:512], in_=bk(b)[:], func=AF.Square, accum_out=ssq[:, j, nb:nb + 1]),
               [bkB(b)], [junkB, ssqB])
        dve.op(lambda e: e.tensor_copy(hb[:, j, nb * 512:(nb + 1) * 512], bk(b)[:]), [bkB(b)], [hbJ[j]])

    cnt = {"kv": 0, "pt": 0, "st": 0, "o": 0}

    def attention(l, ti):
        a0 = 2 * ti
        passes = []

        def gate_pre(h):
            def f():
                hs = h % 2
                sel_t, sel_B = sel[hs]
                acc_t, acc_B = acc[hs]
                for j in range(NJ):
                    own = a0 + j // 2
                    if own == 0:
                        continue
                    pe.op(lambda e, j=j: e.matmul(bk(6)[:, j * 64:(j + 1) * 64], qs[:, h, j * 128:(j + 1) * 128], kmean[:, h, :], start=True, stop=True),
                          [qsH[h], kmeanB], [bkB(6)])
                if a0 + 1 > 0:
                    j0 = 0 if a0 > 0 else 2
                    own_max = a0 + 1
                    for j in range(j0, NJ):
                        own = a0 + j // 2
                        dve.op(lambda e, j=j, own=own: e.tensor_copy(Gs[:, j, 0:own], bk(6)[:, j * 64:j * 64 + own]), [bkB(6)], [GsB])
                        if own <= TOPK:
                            pool.op(lambda e, j=j, own=own: e.memset(sel_t[:, j, 0:own], 1.0), [], [sel_B])
                        else:
                            dve.op(lambda e, j=j: e.max(out=top8, in_=Gs[:, j, :]), [GsB], [top8B])
                            dve.op(lambda e, j=j, own=own: e.tensor_scalar(sel_t[:, j, 0:own], Gs[:, j, 0:own], top8[:, TOPK - 1:TOPK], None, op0=ALU.is_ge),
                                   [GsB, top8B], [sel_B])
                pool.op(lambda e: e.memset(acc_t, 0.0), [], [acc_B])
            return f

        def head_post(h):
            def f():
                hs = h % 2
                acc_t, acc_B = acc[hs]
                atk_t, atk_B = atk[hs]
                dve.op(lambda e: e.reciprocal(rinv, acc_t[:, :, HD]), [acc_B], [rinvB])
                for j in range(NJ):
                    eng = pool if j % 2 else dve
                    eng.op(lambda e, j=j: e.tensor_scalar(atk_t[:, j, :], acc_t[:, j, 0:HD], rinv[:, j:j + 1], None, op0=ALU.mult), [acc_B, rinvB], [atk_B])
                pv = bk(7)[:].bitcast(BF16)
                for j in range(NJ):
                    pe.op(lambda e, j=j: e.transpose(pv[:, j * 128:(j + 1) * 128], atk_t[:, j, :], idb), [atk_B, idbB], [bkB(7)], signal=(j == NJ - 1))
                if h % 2:
                    act.op(lambda e: e.activation(out=atf[:, h, :], in_=pv[:, 0:T], func=AF.Copy), [bkB(7)], [atfB])
                else:
                    dve.op(lambda e: e.tensor_copy(atf[:, h, :], pv[:, 0:T]), [bkB(7)], [atfB])
            return f

        for h in range(NH):
            hp = []
            for g0 in range(0, a0, GRP):
                nblk = min(GRP, a0 - g0)

                def load(h=h, g0=g0, nblk=nblk):
                    bi = cnt["kv"] % NKB
                    cnt["kv"] += 1
                    kb_t, kb_B = kbuf[bi]
                    vb_t, vb_B = vbuf[bi]
                    pool.op(lambda e: e.memset(vb_t[:, :, HD:HD + 1], 1.0), [], [vb_B])
                    sp.dma(kb_t[:, 0:nblk * BLK], kT[l][h, :, g0 * BLK:(g0 + nblk) * BLK], [kTB[l]], [kb_B], kb_B)
                    sp.dma(vb_t[:, 0:nblk * 2, 0:HD], vS[l][g0 * BLK:(g0 + nblk) * BLK, h, :].rearrange("(k p) d -> p k d", p=128), [vSB[l]], [vb_B], vb_B)
                    return kb_t, kb_B, vb_t, vb_B
                holder = {}
                for n in range(nblk):
                    def kts(holder=holder, n=n, load=load):
                        if "b" not in holder:
                            holder["b"] = load()
                        kb_t, kb_B, vb_t, vb_B = holder["b"]
                        return [(kb_t[:, (n * 2 + kk) * 128:(n * 2 + kk + 1) * 128], kb_B, vb_t[:, n * 2 + kk, :], vb_B) for kk in range(2)]
                    hp.append(dict(h=h, kts=kts, jl=[0, 1, 2, 3], selcol=g0 + n, diag=None))
            hp.append(dict(h=h, kts=(lambda h=h: [(ks[:, h, kk * 128:(kk + 1) * 128], ksH[h], vs[:, kk, h, :], vsJ[kk]) for kk in range(2)]),
                           jl=[2, 3], selcol=a0, diag=None))
            for mb in range(2):
                hp.append(dict(h=h, kts=(lambda h=h, mb=mb: [(ks[:, h, (2 * mb + kk) * 128:(2 * mb + kk + 1) * 128], ksH[h], vs[:, 2 * mb + kk, h, :], vsJ[2 * mb + kk]) for kk in range(2)]),
                               jl=[2 * mb, 2 * mb + 1], selcol=None, diag={0: 2 * mb, 1: 2 * mb + 1}))
            hp[0]["pre"] = gate_pre(h)
            hp[-1]["post"] = head_post(h)
            passes.extend(hp)

        def stageA(P):
            if "pre" in P:
                P["pre"]()
            h = P["h"]
            kt_list = P["kts"]()
            P["ktl"] = kt_list
            P["pts"] = []
            P["plan"] = []
            for ki, (k_ap, kB_, v_ap, vB_) in enumerate(kt_list):
                js = [j for j in P["jl"] if (P["diag"] is None or j >= P["diag"][ki])]
                P["plan"].append(js)
                q0, q1 = js[0] * 128, (js[-1] + 1) * 128
                sb_ = SBANKS[cnt["st"] % 4]
                cnt["st"] += 1
                pe.op(lambda e, sb_=sb_, k_ap=k_ap, q0=q0, q1=q1: e.matmul(bk(sb_)[:, q0:q1], k_ap, qs[:, h, q0:q1], start=True, stop=True),
                      [kB_, qsH[h]], [bkB(sb_)])
                pi = cnt["pt"] % NPT
                cnt["pt"] += 1
                pt_t, pt_B = ptb[pi]
                act.op(lambda e, sb_=sb_, pt_t=pt_t, q0=q0, q1=q1: e.activation(out=pt_t[:, q0:q1], in_=bk(sb_)[:, q0:q1], func=AF.Exp),
                       [bkB(sb_)], [pt_B])
                if P["diag"] is not None:
                    jd = P["diag"][ki]
                    pool.op(lambda e, pt_t=pt_t, jd=jd: e.tensor_tensor(pt_t[:, jd * 128:(jd + 1) * 128], pt_t[:, jd * 128:(jd + 1) * 128], trib, ALU.mult),
                            [pt_B, tribB], [pt_B])
                P["pts"].append((pt_t, pt_B))

        def stageB(P):
            h = P["h"]
            hs = h % 2
            sel_t, sel_B = sel[hs]
            acc_t, acc_B = acc[hs]
            oset = cnt["o"] % 2
            cnt["o"] += 1
            jl = P["jl"]
            ob = {j: (2 + oset * 2 + (j // 2), (j % 2) * (HD + 1)) for j in jl}
            for j in jl:
                kis = [ki for ki in range(len(P["ktl"])) if j in P["plan"][ki]]
                b_, oo = ob[j]
                lastj = (j == jl[-1]) or (ob[jl[jl.index(j) + 1]][0] != b_)
                for idx, ki in enumerate(kis):
                    pt_t, pt_B = P["pts"][ki]
                    _, _, v_ap, vB_ = P["ktl"][ki]
                    pe.op(lambda e, j=j, b_=b_, oo=oo, pt_t=pt_t, v_ap=v_ap, idx=idx, kis=kis: e.matmul(bk(b_)[:, oo:oo + HD + 1], pt_t[:, j * 128:(j + 1) * 128], v_ap,
                                                                                               start=(idx == 0), stop=(idx == len(kis) - 1)),
                          [pt_B, vB_], [bkB(b_)], signal=(lastj and idx == len(kis) - 1))
            for j in jl:
                b_, oo = ob[j]
                if P["selcol"] is None:
                    dve.op(lambda e, j=j, b_=b_, oo=oo: e.tensor_tensor(acc_t[:, j, :], acc_t[:, j, :], bk(b_)[:, oo:oo + HD + 1], ALU.add),
                           [bkB(b_), acc_B], [acc_B])
                else:
                    sc = P["selcol"]
                    dve.op(lambda e, j=j, b_=b_, oo=oo, sc=sc: e.scalar_tensor_tensor(acc_t[:, j, :], bk(b_)[:, oo:oo + HD + 1], sel_t[:, j, sc:sc + 1], acc_t[:, j, :], ALU.mult, ALU.add),
                           [bkB(b_), sel_B, acc_B], [acc_B])

        stageA(passes[0])
        pending = []
        for i in range(len(passes)):
            if i + 1 < len(passes):
                stageA(passes[i + 1])
            stageB(passes[i])
            if "post" in passes[i]:
                pending.append((i + 2, passes[i]["post"]))
            while pending and pending[0][0] <= i:
                pending.pop(0)[1]()
        for _, fn in pending:
            fn()

    for l in range(depth):
        x_src, x_srcB = (x_in, NOB) if l == 0 else (xs[l - 1], xsB[l - 1])
        x_dst, x_dstB = (out, outB) if l == depth - 1 else (xs[l], xsB[l])
        sp.dma(vP, vecP[l], [NOB], [vPB], vPB)
        sp.dma(cw, convw[l], [NOB], [cwB], cwB)
        sp.dma(sgB, sguB[l].rearrange("a p n -> p a n"), [NOB], [sgBB], sgBB)
        sp.dma(wmf, sguwT[l], [NOB], [wmfB], wmfB)
        sp.dma(sbr, sgub[l], [NOB], [sbrB], sbrB)
        sp.dma(fw, ffnw[l], [NOB], [fwB], fwB)
        for g in range(4):
            dve.op(lambda e, g=g: e.tensor_tensor(wmT[:, g, :], wmf[:, g, :], cstf[:, 1, :], ALU.mult), [wmfB, cstfB], [wmTB])
        dve.op(lambda e: e.tensor_copy(sbrb, sbr), [sbrB], [sbrbB])
        pool.op(lambda e: e.memset(ahalo, 0.0), [], [ahaloB])
        pool.op(lambda e: e.memset(fhalo, 0.0), [], [fhaloB])
        pool.op(lambda e: e.memset(Gs, NEG), [], [GsB])

        for ti in range(NT):
            t0 = ti * T
            declare()
            declare_banks()
            sp.dma(xt, x_src[t0:t0 + T, :].rearrange("(j p) d -> p j d", p=128), [x_srcB], [xtB], xtB)
            rmsnorm_to_xnT(0)
            sp.dma(rp, ropet[:, :, t0:t0 + T], [NOB], [rpB], rpB)

            for qi in range(4):
                wt, wB = w_next("P%d" % qi)
                isq = qi < 2
                for m in range(4):
                    h = (qi % 2) * 4 + m
                    dst, dstH = (qs, qsH) if isq else (ks, ksH)
                    b = fm_chunk(wt, wB, m, xnT, xnTC)
                    sc = HD ** -0.5 if isq else 1.0
                    act.op(lambda e, b=b, h=h, dst=dst, sc=sc: e.activation(out=dst[:, h, :], in_=bk(b)[:], func=AF.Copy, scale=sc),
                           [bkB(b)], [dstH[h]])
                    pe.op(lambda e, h=h, dst=dst: e.matmul(bk(5)[0:32, :], pmb, dst[0:32, h, :], start=True, stop=True),
                          [pmbB, dstH[h]], [bkB(5)])
                    dve.op(lambda e: e.tensor_tensor(rt2, bk(5)[0:32, :], rp[:, 1, :], ALU.mult), [bkB(5), rpB], [rt2B])
                    pool.op(lambda e, h=h, dst=dst: e.tensor_tensor(rt1, dst[0:32, h, :], rp[:, 0, :], ALU.mult), [dstH[h], rpB], [rt1B])
                    dve.op(lambda e, h=h, dst=dst: e.tensor_tensor(dst[0:32, h, :], rt1, rt2, ALU.add), [rt1B, rt2B], [dstH[h]])
                    if not isq:
                        for half in range(2):
                            nblk = 2 * ti + half
                            dve.op(lambda e, h=h, half=half: e.tensor_reduce(out=kms[:, half:half + 1], in_=ks[:, h, half * BLK:(half + 1) * BLK],
                                                                             axis=AX.X, op=ALU.add), [ksH[h]], [kmsB])
                            dve.op(lambda e, h=h, half=half, nblk=nblk: e.tensor_scalar(kmean[:, h, nblk:nblk + 1], kms[:, half:half + 1], 1.0 / BLK, None, op0=ALU.mult),
                                   [kmsB], [kmeanB])
            act.dma(kT[l][:, :, t0:t0 + T].rearrange("h d t -> d h t"), ks, [ksB], [kTB[l]], ksB)
            pool.op(lambda e: e.memset(vs[:, :, :, HD:HD + 1], 1.0), [], [vsB])
            for vi in range(2):
                wt, wB = w_next("P%d" % (4 + vi))
                for j in range(NJ):
                    b = tm_block(wt, wB, j, xnT, xnTC)
                    eng = ev_eng()
                    src_v = bk(b)[:].rearrange("p (h d) -> p h d", h=4)
                    if eng is dve:
                        dve.op(lambda e, j=j, vi=vi, src_v=src_v: e.tensor_copy(vs[:, j, vi * 4:(vi + 1) * 4, 0:HD], src_v), [bkB(b)], [vsJ[j]])
                    else:
                        act.op(lambda e, j=j, vi=vi, src_v=src_v: e.activation(out=vs[:, j, vi * 4:(vi + 1) * 4, 0:HD], in_=src_v, func=AF.Copy), [bkB(b)], [vsJ[j]])
            for j in range(NJ):
                act.dma(vS[l][t0 + j * 128:t0 + (j + 1) * 128, :, :], vs[:, j, :, 0:HD], [vsB], [vSB[l]], vsB)
            wt, wB = w_next("P6")
            for m in range(4):
                b = fm_chunk(wt, wB, m, xnT, xnTC)
                act.op(lambda e, b=b, m=m: e.activation(out=sgc[:, m, :], in_=bk(b)[:], func=AF.Sigmoid), [bkB(b)], [sgcB])
            pool.op(lambda e: e.tensor_copy(aext[:, :, 0:CK - 1], ahalo), [ahaloB], [aextB])
            wt, wB = w_next("P7")
            for m in range(4):
                b = fm_chunk(wt, wB, m, xnT, xnTC)
                dve.op(lambda e, b=b, m=m: e.tensor_tensor(aext[:, m, CK - 1:CK - 1 + T], bk(b)[:], sgc[:, m, :], ALU.mult), [bkB(b), sgcB], [aextB])
            pool.op(lambda e: e.tensor_copy(ahalo, aext[:, :, T:T + CK - 1]), [aextB], [ahaloB])
            wt, wB = w_next("P8")
            for m in range(4):
                b = fm_chunk(wt, wB, m, xnT, xnTC)
                act.op(lambda e, b=b, m=m: e.activation(out=sus[:, m, :], in_=bk(b)[:], func=AF.Gelu_apprx_tanh), [bkB(b)], [susB])
            wt, wB = w_next("P9")
            for j in range(NJ):
                b = tm_block(wt, wB, j, xnT, xnTC)
                act.op(lambda e, b=b: e.activation(out=svg, in_=bk(b)[:], func=AF.Gelu_apprx_tanh), [bkB(b)], [svgB])
                dve.op(lambda e: e.bn_stats(bst, svg), [svgB], [bstB])
                dve.op(lambda e: e.bn_aggr(bag, bst), [bstB], [bagB])
                act.op(lambda e: e.activation(out=bag[:, 1:2], in_=bag[:, 1:2], func=AF.Sqrt, bias=EPS), [bagB], [bagB])
                dve.op(lambda e: e.reciprocal(bag[:, 1:2], bag[:, 1:2]), [bagB], [bagB])
                dve.op(lambda e: e.tensor_scalar(svg, svg, bag[:, 0:1], bag[:, 1:2], op0=ALU.subtract, op1=ALU.mult), [svgB, bagB], [svgB])
                dve.op(lambda e: e.tensor_tensor(svg, svg, sgB[:, 0, :], ALU.mult), [svgB, sgBB], [svgB])
                dve.op(lambda e, j=j: e.tensor_tensor(svn[:, j, :], svg, sgB[:, 1, :], ALU.add), [svgB, sgBB], [svnB])

            for m in range(4):
                b = mm_bank()
                for k in range(CK):
                    dg_t, dg_B = dg[dgq["i"] % 8]
                    dgq["i"] += 1
                    dve.op(lambda e, m=m, k=k, dg_t=dg_t: e.tensor_scalar(dg_t, idb, cw[:, m, k:k + 1], None, op0=ALU.mult), [idbB, cwB], [dg_B])
                    pe.op(lambda e, m=m, k=k, b=b, dg_t=dg_t: e.matmul(bk(b)[:], dg_t, aext[:, m, k:k + T], start=(k == 0), stop=(k == CK - 1)),
                          [dg_B, aextB], [bkB(b)], signal=True)
                act.op(lambda e, m=m, b=b: e.activation(out=cacc[:, m, :], in_=bk(b)[:], func=AF.Identity, bias=cw[:, m, CK:CK + 1]),
                       [bkB(b), cwB], [caccM[m]])

            for g in range(4):
                for j in range(NJ):
                    pe.op(lambda e, g=g, j=j: e.matmul(bk(5)[:, j * 128:(j + 1) * 128], svn[:, j, g * 128:(g + 1) * 128], wmT[:, g, :], start=True, stop=False),
                          [svnB, wmTB], [bkB(5)], signal=False)
                    pe.op(lambda e, g=g, j=j: e.matmul(bk(5)[:, j * 128:(j + 1) * 128], onesr, sbrb[:, g * 128:(g + 1) * 128], start=False, stop=True),
                          [onesrB, sbrbB], [bkB(5)], signal=(j == NJ - 1))
                dve.op(lambda e, g=g: e.tensor_tensor(ybin[:, g, :], bk(5)[:], sus[:, g, :], ALU.mult), [bkB(5), susB], [ybinB])

            attention(l, ti)

            for m in range(4):
                act.op(lambda e, m=m: e.activation(out=csq, in_=cacc[:, m, :], func=AF.Square), [caccB], [csqB])
                pe.op(lambda e, m=m: e.matmul(bk(4)[:], onesf, cacc[:, m, :], start=(m == 0), stop=(m == 3)), [onesfB, caccB], [bkB(4)], signal=(m == 3))
                pe.op(lambda e, m=m: e.matmul(bk(5)[:], onesf, csq, start=(m == 0), stop=(m == 3)), [onesfB, csqB], [bkB(5)], signal=True)
            dve.op(lambda e: e.tensor_scalar(lnm, bk(4)[:], 1.0 / CC, None, op0=ALU.mult), [bkB(4)], [lnmB])
            dve.op(lambda e: e.tensor_tensor(csq, lnm, lnm, ALU.mult), [lnmB], [csqB])
            dve.op(lambda e: e.scalar_tensor_tensor(lnr, bk(5)[:], 1.0 / CC, csq, ALU.mult, ALU.subtract), [bkB(5), csqB], [lnrB])
            act.op(lambda e: e.activation(out=lnr, in_=lnr, func=AF.Sqrt, bias=EPS), [lnrB], [lnrB])
            dve.op(lambda e: e.reciprocal(lnr, lnr), [lnrB], [lnrB])
            for m in range(4):
                dve.op(lambda e, m=m: e.tensor_tensor(cacc[:, m, :], cacc[:, m, :], lnm, ALU.subtract), [caccB, lnmB], [caccB])
                dve.op(lambda e, m=m: e.tensor_tensor(cacc[:, m, :], cacc[:, m, :], lnr, ALU.mult), [caccB, lnrB], [caccB])
                act.op(lambda e, m=m: e.activation(out=acta[:, m, :], in_=cacc[:, m, :], func=AF.Silu, scale=cw[:, m, CK + 1:CK + 2], bias=cw[:, m, CK + 2:CK + 3]),
                       [caccB, cwB], [actaB])

            srcs = [(acta, actaB, 4, 0), (ybin, ybinB, 4, 4), (atf, atfB, 8, 8)]
            for c in range(16):
                wy, wyB = w_next("YG%d" % c)
                for g in range(3):
                    bg = fm_chunk(wy, wyB, 0, xnT, xnTC, nkc=16, kc0=16 + 16 * g)
                    act.op(lambda e, bg=bg: e.activation(out=sga, in_=bk(bg)[:], func=AF.Sigmoid), [bkB(bg)], [sgaB])
                    src, srcB, nkc, kc0 = srcs[g]
                    by = fm_chunk(wy, wyB, 0, src, srcB, nkc=nkc, kc0=kc0)
                    if g == 0:
                        dve.op(lambda e, by=by: e.tensor_tensor(mt1, bk(by)[:], sga, ALU.mult), [bkB(by), sgaB], [mt1B])
                    elif g == 1:
                        dve.op(lambda e, by=by: e.tensor_tensor(mt2, bk(by)[:], sga, ALU.mult), [bkB(by), sgaB], [mt2B])
                        pool.op(lambda e: e.tensor_tensor(mt1, mt1, mt2, ALU.add), [mt1B, mt2B], [mt1B])
                    else:
                        dve.op(lambda e, by=by: e.tensor_tensor(mt2, bk(by)[:], sga, ALU.mult), [bkB(by), sgaB], [mt2B])
                        pool.op(lambda e, c=c: e.tensor_tensor(mrg[:, c, :], mt1, mt2, ALU.add), [mt1B, mt2B], [mrgC[c]])

            for nb in range(4):
                wt, wB = w_next("O%d" % nb)
                for j in range(NJ):
                    b = tm_block(wt, wB, j, mrg, mrgC)
                    tm_evac(b, j, nb, hbm, hbmJ)
            post_norm_residual(l, 0, hbm, hbmJ)

            rmsnorm_to_xnT(1)
            for i in range(11):
                wt, wB = w_next("FA%d" % i)
                for m in range(4):
                    c = i * 4 + m
                    b = fm_chunk(wt, wB, m, xnT, xnTC)
                    act.op(lambda e, b=b, m=m: e.activation(out=ae[:, m, 2:2 + T], in_=bk(b)[:], func=AF.Copy), [bkB(b)], [aeM[m]])
                    pool.op(lambda e, m=m, c=c: e.tensor_copy(ae[:, m, 0:2], fhalo[:, c, :]), [fhaloB], [aeM[m]])
                    pool.op(lambda e, m=m, c=c: e.tensor_copy(fhalo[:, c, :], ae[:, m, T:T + 2]), [aeM[m]], [fhaloB])
                    b2 = mm_bank()
                    for k in range(3):
                        dg_t, dg_B = dg[dgq["i"] % 8]
                        dgq["i"] += 1
                        dve.op(lambda e, c=c, k=k, dg_t=dg_t: e.tensor_scalar(dg_t, idb, fw[:, c, k:k + 1], None, op0=ALU.mult), [idbB, fwB], [dg_B])
                        pe.op(lambda e, m=m, k=k, b2=b2, dg_t=dg_t: e.matmul(bk(b2)[:], dg_t, ae[:, m, k:k + T], start=(k == 0), stop=(k == 2)),
                              [dg_B, aeM[m]], [bkB(b2)], signal=True)
                    act.op(lambda e, m=m, c=c, b2=b2: e.activation(out=fga[:, m, :], in_=bk(b2)[:], func=AF.Gelu_apprx_tanh, bias=fw[:, c, 3:4]),
                           [bkB(b2), fwB], [fgaM[m]])
                wt, wB = w_next("FB%d" % i)
                for m in range(4):
                    c = i * 4 + m
                    b = fm_chunk(wt, wB, m, xnT, xnTC)
                    dve.op(lambda e, b=b, m=m, c=c: e.tensor_tensor(gff[:, c, :], bk(b)[:], fga[:, m, :], ALU.mult), [bkB(b), fgaM[m]], [gffC[c]])
            for nb in range(4):
                for kg in range(4):
                    wt, wB = w_next("FO%d_%d" % (nb, kg))
                    for j in range(NJ):
                        tm_block(wt, wB, j, gff, gffC, nkc=11, b=j, first=(kg == 0), last=(kg == 3), c_off=kg * 11)
                for j in range(NJ):
                    tm_evac(j, j, nb, hbf, hbfJ)
            post_norm_residual(l, 1, hbf, hbfJ)

            rmsnorm_to_xnT(2)
            sp.dma(pt, p_in[l, t0:t0 + T, :].rearrange("(j p) d -> p j d", p=128), [NOB], [ptB], ptB)
            dve.op(lambda e: e.tensor_copy(ptb16, pt), [ptB], [ptb16B])
            pv = bk(5)[:].bitcast(BF16)
            for c in range(2):
                for j in range(NJ):
                    pe.op(lambda e, j=j, c=c: e.transpose(pv[:, j * 128:(j + 1) * 128], ptb16[:, j, c * 128:(c + 1) * 128], idb),
                          [ptb16B, idbB], [bkB(5)], signal=(j == NJ - 1))
                dve.op(lambda e, c=c: e.tensor_copy(pT[:, c, :], pv[:, 0:T]), [bkB(5)], [pTB])
            for nb in range(4):
                wt, wB = w_next("PG%d" % nb)
                gbanks = []
                for j in range(NJ):
                    gbanks.append(tm_block(wt, wB, j, xnT, xnTC, b=j))
                wp, wpB = w_next("PP%d" % nb)
                for j in range(NJ):
                    b = gbanks[j]
                    act.op(lambda e, b=b: e.activation(out=gsb, in_=bk(b)[:], func=AF.Sigmoid), [bkB(b)], [gsbB])
                    b2 = 4 + (j % 2)
                    for c in range(2):
                        pe.op(lambda e, c=c, j=j, b2=b2: e.matmul(bk(b2)[:], pT[:, c, j * 128:(j + 1) * 128], wp[:, c, :], start=(c == 0), stop=(c == 1)),
                              [pTB, wpB], [bkB(b2)], signal=(c == 1))
                    dve.op(lambda e, b2=b2: e.tensor_tensor(gsb, gsb, bk(b2)[:], ALU.mult), [gsbB, bkB(b2)], [gsbB])
                    pool.op(lambda e, j=j, nb=nb: e.tensor_tensor(xt[:, j, nb * 512:(nb + 1) * 512], xt[:, j, nb * 512:(nb + 1) * 512], gsb, ALU.add),
                            [gsbB, xtJ[j]], [xtJ[j]])
            act.dma(x_dst[t0:t0 + T, :].rearrange("(j p) d -> p j d", p=128), xt, [xtB], [x_dstB], xtB)

    sp.wait_all([xtB, ksB, vsB, outB] + kTB + vSB)
    act.wait_all([xtB, ksB, vsB, outB])
    pes[0].close()
    K.es.close()
    global _LAST_KB
    _LAST_KB = K
    return nc


def host_consts(S):
    cst = np.zeros((128, 3, 128), np.float32)
    cst[:, 0, :] = np.eye(128, dtype=np.float32)
    cst[:, 1, :] = np.triu(np.ones((128, 128), np.float32))
    for m in range(32):
        cst[(m + 16) % 32, 2, m] = 1.0
    half = ROPE // 2
    inv = np.float32(500000.0) ** (-np.arange(0, ROPE, 2, dtype=np.float32) / np.float32(ROPE))
    ang = np.arange(S, dtype=np.float32)[:, None] * inv[None, :].astype(np.float32)
    cos = np.cos(ang.astype(np.float64)).astype(np.float32).T
    sin = np.sin(ang.astype(np.float64)).astype(np.float32).T
    ropet = np.zeros((ROPE, 2, S), np.float32)
    ropet[0:half, 0] = cos
    ropet[half:, 0] = cos
    ropet[0:half, 1] = -sin
    ropet[half:, 1] = sin
    return cst, ropet


def layout_params(inp, depth):
    f = lambda a: np.ascontiguousarray(np.asarray(a, dtype=np.float32))
    L = depth
    toP = lambda v: v.reshape(L, -1, 128).transpose(0, 2, 1)
    vecP = np.stack([toP(f(inp["mix_norm_pre"])), toP(f(inp["ffn_norm_pre"])), toP(f(inp["ple_norm"]))], axis=2)
    vecB = np.stack([np.broadcast_to(f(inp["mix_norm_post"])[:, None, :], (L, 128, D)),
                     np.broadcast_to(f(inp["ffn_norm_post"])[:, None, :], (L, 128, D))], axis=1)
    cw = f(inp["conv_dw_w"]).reshape(L, CK, 4, 128).transpose(0, 3, 2, 1)
    extra = np.stack([toP(f(inp["conv_dw_b"])), toP(f(inp["conv_norm_g"])), toP(f(inp["conv_norm_b"]))], axis=3)
    convw = np.concatenate([cw, extra], axis=3)
    sguB = np.stack([np.broadcast_to(f(inp["sgu_norm_g"])[:, None, :], (L, 128, SW)),
                     np.broadcast_to(f(inp["sgu_norm_b"])[:, None, :], (L, 128, SW))], axis=1)
    sguwT = f(inp["sgu_w"]).transpose(0, 3, 1, 2)
    sgub = f(inp["sgu_b"]).reshape(L, 1, 4 * 128)
    fwt = f(inp["ffn_dw_w"]).reshape(L, 3, 44, 128).transpose(0, 3, 2, 1)
    fwb = toP(f(inp["ffn_dw_b"]))[..., None]
    ffnw = np.concatenate([fwt, fwb], axis=3)
    c = np.ascontiguousarray
    return dict(vecP=c(vecP), vecB=c(vecB), convw=c(convw), sguB=c(sguB), sguwT=c(sguwT), sgub=c(sgub), ffnw=c(ffnw))


_CACHE = {}


def run(inp, S, depth, nseq):
    key = (S, depth)
    if key not in _CACHE:
        _CACHE[key] = build(S, depth)
    nc = _CACHE[key]
    cst, ropet = host_consts(S)
    prm = layout_params(inp, depth)
    f = lambda a: np.ascontiguousarray(np.asarray(a, dtype=np.float32))
    shared = dict(prm)
    shared.update(cst=cst, ropet=ropet)
    for k in ("w_in", "conv_out", "sgu_out", "attn_out", "w_o", "ffn_in", "ffn_out", "ple_gate", "ple_proj"):
        shared[k] = f(inp[k])
    x = f(inp["x"])
    p = f(inp["p"])
    in_maps = []
    for b in range(nseq):
        m = dict(shared)
        m["x"] = np.ascontiguousarray(x[b])
        m["p"] = np.ascontiguousarray(p[:, b])
        in_maps.append(m)
    res = run_bass_kernel_spmd(nc, in_maps, core_ids=list(range(nseq)))
    return np.stack([np.asarray(res.results[b]["out"], dtype=np.float32) for b in range(nseq)], axis=0)


def kernel(**inputs):
    x = np.asarray(inputs["x"])
    B, S, _ = x.shape
    depth = np.asarray(inputs["w_in"]).shape[0]
    return run(inputs, S, depth, B)
```

```python
import numpy as np
from contextlib import ExitStack
import concourse.bass as bass
import concourse.mybir as mybir
from concourse.bass_utils import run_bass_kernel_spmd

F32 = mybir.dt.float32
BF16 = mybir.dt.bfloat16
AF = mybir.ActivationFunctionType
ALU = mybir.AluOpType
AX = mybir.AxisListType

D = 2048
NH = 8
HD = 128
ROPE = 32
BLK = 256
TOPK = 3
CC = 512
CK = 31
SW = 512
DFF = 5632
PLE = 256
INW = 11264
EPS = 1e-6
T = 512
NJ = 4
NEG = -1.0e30
SEM_ROT = 30000


class Buf:
    def __init__(self, name, parent=None):
        self.name = name
        self.parent = parent
        self.children = []
        if parent is not None:
            parent.children.append(self)
        self.w = {}
        self.r = {}
        self.dsem = None
        self.dtotal = 0
        self.excl = False

    def family(self):
        out = [self]
        p = self.parent
        while p is not None:
            out.append(p)
            p = p.parent
        stack = list(self.children)
        while stack:
            c = stack.pop()
            out.append(c)
            stack.extend(c.children)
        return out


class Eng:
    def __init__(self, K, name, eng, is_pe=False):
        self.K = K
        self.name = name
        self.eng = eng
        self.is_pe = is_pe
        self.sem = K.new_sem("e_" + name)
        self.own = [self.sem]
        self.cnt = 0
        self.known = {}
        self.pend_r = []
        self.pend_w = []

    def _wait(self, deps):
        for sem, val in deps.items():
            if self.is_pe and any(sem is o for o in self.own):
                continue
            if self.known.get(sem, 0) >= val:
                continue
            self.eng.wait_ge(sem, val)
            self.known[sem] = val

    def _deps(self, reads, writes):
        deps = {}
        for b in reads:
            for f in b.family():
                for s, v in f.w.items():
                    if deps.get(s, 0) < v:
                        deps[s] = v
        for b in writes:
            for f in b.family():
                for s, v in f.w.items():
                    if deps.get(s, 0) < v:
                        deps[s] = v
                for s, v in f.r.items():
                    if deps.get(s, 0) < v:
                        deps[s] = v
        return deps

    def op(self, fn, reads=(), writes=(), signal=True):
        ex = [b for b in reads if b.excl]
        if ex:
            reads = [b for b in reads if not b.excl]
            writes = list(writes) + ex
        self._wait(self._deps(reads, writes))
        ins = fn(self.eng)
        self.pend_r.extend(reads)
        self.pend_w.extend(writes)
        if signal:
            if self.cnt >= SEM_ROT:
                self.sem = self.K.new_sem("e_" + self.name)
                self.own.append(self.sem)
                self.cnt = 0
            self.cnt += 1
            ins.then_inc(self.sem, 1)
            tok = (self.sem, self.cnt)
            for b in self.pend_r:
                if b.r.get(tok[0], 0) < tok[1]:
                    b.r[tok[0]] = tok[1]
            for b in self.pend_w:
                b.w = {tok[0]: tok[1]}
                b.r = {}
            self.pend_r = []
            self.pend_w = []
        return ins

    def dma(self, out, in_, reads, writes, sbuf, **kw):
        self._wait(self._deps(reads, writes))
        if sbuf.dsem is None or sbuf.dtotal >= SEM_ROT:
            sbuf.dsem = self.K.new_sem("d_" + sbuf.name)
            sbuf.dtotal = 0
        ins = self.eng.dma_start(out=out, in_=in_, **kw)
        sbuf.dtotal += 16
        ins.then_inc(sbuf.dsem, 16)
        tok = (sbuf.dsem, sbuf.dtotal)
        for b in reads:
            if b.r.get(tok[0], 0) < tok[1]:
                b.r[tok[0]] = tok[1]
        for b in writes:
            b.w = {tok[0]: tok[1]}
            b.r = {}
        return ins

    def wait_all(self, bufs):
        deps = {}
        for b in bufs:
            for f in b.family():
                for dct in (f.w, f.r):
                    for s, v in dct.items():
                        if deps.get(s, 0) < v:
                            deps[s] = v
        self._wait(deps)


class Kb:
    def __init__(self):
        self.nc = bass.Bass("TRN2", target_bir_lowering=False)
        self.es = ExitStack()
        self.nsem = 0
        nc = self.nc
        self.pe = Eng(self, "pe", nc.tensor, is_pe=True)
        self.act = Eng(self, "act", nc.scalar)
        self.dve = Eng(self, "dve", nc.vector)
        self.pool = Eng(self, "pool", nc.gpsimd)
        self.sp = Eng(self, "sp", nc.sync)

    def new_sem(self, name):
        self.nsem += 1
        return self.es.enter_context(self.nc.semaphore(name + "_%d" % self.nsem))

    def sb(self, name, shape, dt):
        t = self.es.enter_context(self.nc.sbuf_tensor(name, shape, dt))
        return t, Buf(name)

    def ps(self, name, shape, dt):
        t = self.es.enter_context(self.nc.psum_tensor(name, shape, dt))
        b = Buf(name)
        b.excl = True
        return t, b

    def din(self, name, shape, dt=F32):
        return self.nc.dram_tensor(name, list(shape), dt, kind="ExternalInput").ap()

    def dout(self, name, shape, dt=F32):
        return self.nc.dram_tensor(name, list(shape), dt, kind="ExternalOutput").ap()

    def dint(self, name, shape, dt):
        return self.nc.dram_tensor(name, list(shape), dt).ap()


def weight_blocks():
    blks = []
    for i in range(10):
        col = {0: 0, 1: 512, 2: 1024, 3: 1536, 4: 2048, 5: 2560, 6: 3584, 7: 3072, 8: 4096, 9: 4608}[i]
        blks.append(("P%d" % i, 512, [("w_in", 0, 2048, col)]))
    for c in range(16):
        blks.append(("YG%d" % c, 128, [("conv_out", 0, 512, c * 128), ("sgu_out", 0, 512, c * 128),
                                       ("attn_out", 0, 1024, c * 128), ("w_in", 0, 2048, 5120 + c * 128),
                                       ("w_in", 0, 2048, 7168 + c * 128), ("w_in", 0, 2048, 9216 + c * 128)]))
    for nb in range(4):
        blks.append(("O%d" % nb, 512, [("w_o", 0, 2048, nb * 512)]))
    for i in range(11):
        blks.append(("FA%d" % i, 512, [("ffn_in", 0, 2048, i * 512)]))
        blks.append(("FB%d" % i, 512, [("ffn_in", 0, 2048, DFF + i * 512)]))
    for nb in range(4):
        for kg in range(4):
            blks.append(("FO%d_%d" % (nb, kg), 512, [("ffn_out", kg * 1408, 1408, nb * 512)]))
    for nb in range(4):
        blks.append(("PG%d" % nb, 512, [("ple_gate", 0, 2048, nb * 512)]))
        blks.append(("PP%d" % nb, 512, [("ple_proj", 0, 256, nb * 512)]))
    return blks


WBLKS = weight_blocks()
NWB = len(WBLKS)
GRP = 4
NKB = 3
NPT = 4
SBANKS = (0, 1, 6, 7)


class Arena:
    def __init__(self, K, nbytes):
        self.K = K
        self.n = nbytes
        slab = K.es.enter_context(K.nc.sbuf_tensor("arena", [128, nbytes // 2], BF16))
        self.base = K.nc.lookup_mloc(slab).addr
        self.bufs = {}
        self.top = 0
        self.gen = 0

    def begin(self):
        self.top = 0
        self.gen += 1

    def carve(self, name, shape, dt, at=None):
        esz = 4 if dt == F32 else 2
        n = 1
        for d in shape[1:]:
            n *= d
        nb = n * esz
        nb_al = (nb + 31) // 32 * 32
        if at is None:
            at = self.top
            self.top += nb_al
        assert at % 4 == 0 and at + nb <= self.n, (name, at, nb, self.n)
        h = self.K.nc.alloc_sbuf_tensor_at("%s_g%d" % (name, self.gen), list(shape), dt, offset=self.base + at)
        ap = h.ap()
        b = self.bufs.get(name)
        if b is None:
            b = Buf(name)
            b.lo, b.hi = at, at + nb
            b.fam = None
            self.bufs[name] = b
        else:
            assert (b.lo, b.hi) == (at, at + nb), name
        return ap, b

    def sub(self, parent, name, lo_off, nbytes):
        b = self.bufs.get(name)
        if b is None:
            b = Buf(name)
            b.lo, b.hi = parent.lo + lo_off, parent.lo + lo_off + nbytes
            b.fam = None
            self.bufs[name] = b
        return b

    def finalize(self):
        bl = list(self.bufs.values())
        for b in bl:
            b.fam = [o for o in bl if o.lo < b.hi and b.lo < o.hi]


def _family(self):
    f = getattr(self, "fam", None)
    if f is not None:
        return f
    return [self]


Buf.family = _family


def build(S, depth):
    NT = S // T
    NB = S // BLK
    assert NB <= 64
    K = Kb()
    nc = K.nc
    pe, act, dve, pool, sp = K.pe, K.act, K.dve, K.pool, K.sp

    x_in = K.din("x", [S, D])
    p_in = K.din("p", [depth, S, PLE])
    win = {"w_in": K.din("w_in", [depth, D, INW]), "conv_out": K.din("conv_out", [depth, CC, D]),
           "sgu_out": K.din("sgu_out", [depth, SW, D]), "attn_out": K.din("attn_out", [depth, NH * HD, D]),
           "w_o": K.din("w_o", [depth, D, D]), "ffn_in": K.din("ffn_in", [depth, D, 2 * DFF]),
           "ffn_out": K.din("ffn_out", [depth, DFF, D]), "ple_gate": K.din("ple_gate", [depth, D, D]),
           "ple_proj": K.din("ple_proj", [depth, PLE, D])}
    vecP = K.din("vecP", [depth, 128, 3, 16])
    vecB = K.din("vecB", [depth, 2, 128, D])
    convw = K.din("convw", [depth, 128, 4, CK + 3])
    sguB = K.din("sguB", [depth, 2, 128, SW])
    sguwT = K.din("sguwT", [depth, 128, 4, 128])
    sgub = K.din("sgub", [depth, 1, 4 * 128])
    ffnw = K.din("ffnw", [depth, 128, 44, 4])
    cst = K.din("cst", [128, 3, 128])
    ropet = K.din("ropet", [ROPE, 2, S])
    out = K.dout("out", [S, D])

    wb = [K.dint("wb%d" % l, [NWB, 128, 16 * 512], BF16) for l in range(depth)]
    wbB = [Buf("wbB%d" % l) for l in range(depth)]
    kT = [K.dint("kT%d" % l, [NH, HD, S], BF16) for l in range(depth)]
    vS = [K.dint("vS%d" % l, [S, NH, HD], BF16) for l in range(depth)]
    kTB = [Buf("kT%d" % l) for l in range(depth)]
    vSB = [Buf("vS%d" % l) for l in range(depth)]
    xs = [K.dint("xs%d" % l, [S, D], F32) for l in range(max(depth - 1, 1))]
    xsB = [Buf("xs%d" % l) for l in range(max(depth - 1, 1))]
    outB = Buf("outB")
    NOB = Buf("extern")

    for l in range(depth):
        for bi, (tag, ncol, parts) in enumerate(WBLKS):
            kc0 = 0
            for (src, r0, nr, c0) in parts:
                nkc = nr // 128
                src_ap = win[src][l, r0:r0 + nr, c0:c0 + ncol].rearrange("(c p) n -> p c n", p=128)
                dst_ap = wb[l][bi, :, kc0 * ncol:(kc0 + nkc) * ncol].rearrange("p (c n) -> p c n", n=ncol)
                pool.dma(dst_ap, src_ap, [NOB], [], wbB[l])
                kc0 += nkc
        wbB[l].w = {wbB[l].dsem: wbB[l].dtotal}

    A = Arena(K, 207 * 1024)
    cv = A.carve
    dg = wsl = xt = xtB = xtJ = vB = vBB = sgB = sgBB = vP = vPB = cw = cwB = wmT = wmTB = sbrb = sbrbB = onesr = onesrB = fw = fwB = idb = idbB = trib = tribB = pmb = pmbB = onesf = onesfB = cstf = cstfB = kmean = kmeanB = Gs = GsB = ahalo = ahaloB = fhalo = fhaloB = ss = ssB = rstd = rstdB = ssq = ssqB = top8 = top8B = bst = bstB = bag = bagB = kms = kmsB = rinv = rinvB = junk = junkB = ptmp = ptmpB = R0 = xnT = xnTB = xnTC = o = qs = qsB = qsH = ks = ksB = ksH = vs = vsB = vsJ = o2 = aext = aextB = o3 = cacc = caccB = caccM = sgc = sgcB = o4 = sus = susB = svn = svnB = acta = actaB = ybin = ybinB = o5 = lnm = lnmB = lnr = lnrB = csq = csqB = svg = svgB = o6 = atf = atfB = o7 = sel = acc = atk = o8 = kbuf = vbuf = ptb = o9 = xnb = xnbB = xnbJ = rp = rpB = rt1 = rt1B = rt2 = rt2B = mrg = mrgB = mrgC = sga = sgaB = mt1 = mt1B = mt2 = mt2B = hbm = hbmB = hbmJ = ae = aeB = aeM = fca = fcaB = fcaM = fga = fgaB = fgaM = gff = gffB = gffC = hbf = hbfB = hbfJ = pt = ptB = ptb16 = ptb16B = pT = pTB = gsb = gsbB = wmf = wmfB = sbr = sbrB = None

    def declare():
        nonlocal dg, wsl, xt, xtB, xtJ, vB, vBB, sgB, sgBB, vP, vPB, cw, cwB, wmT, wmTB, sbrb, sbrbB, onesr, onesrB, fw, fwB, idb, idbB, trib, tribB, pmb, pmbB, onesf, onesfB, cstf, cstfB, kmean, kmeanB, Gs, GsB, ahalo, ahaloB, fhalo, fhaloB, ss, ssB, rstd, rstdB, ssq, ssqB, top8, top8B, bst, bstB, bag, bagB, kms, kmsB, rinv, rinvB, junk, junkB, ptmp, ptmpB, R0, xnT, xnTB, xnTC, o, qs, qsB, qsH, ks, ksB, ksH, vs, vsB, vsJ, o2, aext, aextB, o3, cacc, caccB, caccM, sgc, sgcB, o4, sus, susB, svn, svnB, acta, actaB, ybin, ybinB, o5, lnm, lnmB, lnr, lnrB, csq, csqB, svg, svgB, o6, atf, atfB, o7, sel, acc, atk, o8, kbuf, vbuf, ptb, o9, xnb, xnbB, xnbJ, rp, rpB, rt1, rt1B, rt2, rt2B, mrg, mrgB, mrgC, sga, sgaB, mt1, mt1B, mt2, mt2B, hbm, hbmB, hbmJ, ae, aeB, aeM, fca, fcaB, fcaM, fga, fgaB, fgaM, gff, gffB, gffC, hbf, hbfB, hbfJ, pt, ptB, ptb16, ptb16B, pT, pTB, gsb, gsbB, wmf, wmfB, sbr, sbrB
        A.begin()
        wsl = [cv("wsl%d" % i, [128, 16 * 512], BF16) for i in range(2)]
        xt, xtB = cv("xt", [128, NJ, D], F32)
        xtJ = [A.sub(xtB, "xt%d" % j, j * D * 4, D * 4) for j in range(NJ)]
        vB, vBB = cv("vB", [128, D], F32)
        sgB, sgBB = cv("sgB", [128, 2, SW], F32)
        vP, vPB = cv("vP", [128, 3, 16], F32)
        cw, cwB = cv("cw", [128, 4, CK + 3], F32)
        wmT, wmTB = cv("wmT", [128, 4, 128], BF16)
        sbrb, sbrbB = cv("sbrb", [1, 4 * 128], BF16)
        onesr, onesrB = cv("onesr", [1, 128], BF16)
        fw, fwB = cv("fw", [128, 44, 4], F32)
        idb, idbB = cv("idb", [128, 128], BF16)
        trib, tribB = cv("trib", [128, 128], BF16)
        pmb, pmbB = cv("pmb", [32, 32], BF16)
        onesf, onesfB = cv("onesf", [128, 128], F32)
        cstf, cstfB = cv("cstf", [128, 3, 128], F32)
        kmean, kmeanB = cv("kmean", [128, NH, 64], BF16)
        Gs, GsB = cv("Gs", [128, NJ, 64], F32)
        ahalo, ahaloB = cv("ahalo", [128, 4, CK - 1], BF16)
        fhalo, fhaloB = cv("fhalo", [128, 44, 2], BF16)
        dg = [cv("dg%d" % i, [128, 128], BF16) for i in range(8)]
        ss, ssB = cv("ss", [128, 8], F32)
        rstd, rstdB = cv("rstd", [128, 8], F32)
        ssq, ssqB = cv("ssq", [128, NJ, 4], F32)
        top8, top8B = cv("top8", [128, 8], F32)
        bst, bstB = cv("bst", [128, 6], F32)
        bag, bagB = cv("bag", [128, 2], F32)
        kms, kmsB = cv("kms", [128, 2], F32)
        rinv, rinvB = cv("rinv", [128, NJ], F32)
        junk, junkB = cv("junk", [128, D], BF16)
        ptmp, ptmpB = cv("ptmp", [128, T], F32)
        R0 = A.top
        xnT, xnTB = cv("xnT", [128, 16, T], BF16, at=R0)
        xnTC = [A.sub(xnTB, "xnT%d" % c, c * T * 2, T * 2) for c in range(16)]
        o = R0 + 16384
        qs, qsB = cv("qs", [128, NH, T], BF16, at=o)
        qsH = [A.sub(qsB, "qs%d" % h, h * T * 2, T * 2) for h in range(NH)]
        ks, ksB = cv("ks", [128, NH, T], BF16, at=o + 8192)
        ksH = [A.sub(ksB, "ks%d" % h, h * T * 2, T * 2) for h in range(NH)]
        vs, vsB = cv("vs", [128, NJ, NH, HD + 1], BF16, at=o + 16384)
        vsJ = [A.sub(vsB, "vs%d" % j, j * NH * (HD + 1) * 2, NH * (HD + 1) * 2) for j in range(NJ)]
        o2 = o + 16384 + 8256
        aext, aextB = cv("aext", [128, 4, CK - 1 + T], BF16, at=o2)
        o3 = o2 + 8672
        cacc, caccB = cv("cacc", [128, 4, T], F32, at=o3)
        caccM = [A.sub(caccB, "cacc%d" % m, m * T * 4, T * 4) for m in range(4)]
        sgc, sgcB = cv("sgc", [128, 4, T], F32, at=o3)
        o4 = o3 + 8192
        sus, susB = cv("sus", [128, 4, T], BF16, at=o4)
        svn, svnB = cv("svn", [128, NJ, SW], BF16, at=o4 + 4096)
        acta, actaB = cv("acta", [128, 4, T], BF16, at=o4 + 8192)
        ybin, ybinB = cv("ybin", [128, 4, T], BF16, at=o4 + 12288)
        o5 = o4 + 16384
        lnm, lnmB = cv("lnm", [128, T], F32, at=o5)
        lnr, lnrB = cv("lnr", [128, T], F32, at=o5 + 2048)
        csq, csqB = cv("csq", [128, T], F32, at=o5 + 4096)
        svg, svgB = cv("svg", [128, SW], F32, at=o5 + 4096)
        o6 = o5 + 6144
        atf, atfB = cv("atf", [128, NH, T], BF16, at=o6)
        o7 = o6 + 8192
        sel = [cv("sel%d" % i, [128, NJ, 64], F32, at=o7 + i * 1024) for i in range(2)]
        acc = [cv("acc%d" % i, [128, NJ, HD + 1], F32, at=o7 + 2048 + i * 2080) for i in range(2)]
        atk = [cv("atk%d" % i, [128, NJ, HD], BF16, at=o7 + 6208 + i * 1024) for i in range(2)]
        o8 = o7 + 8256
        kbuf = [cv("kbuf%d" % i, [128, GRP * BLK], BF16, at=o8 + i * 2048) for i in range(NKB)]
        vbuf = [cv("vbuf%d" % i, [128, GRP * 2, HD + 1], BF16, at=o8 + 6144 + i * 2080) for i in range(NKB)]
        ptb = [cv("ptb%d" % i, [128, T], BF16, at=o8 + 12384 + i * 1024) for i in range(NPT)]
        o9 = o8 + 12384 + 4096
        xnb, xnbB = cv("xnb", [128, NJ, D], BF16, at=o7)
        xnbJ = [A.sub(xnbB, "xnb%d" % j, j * D * 2, D * 2) for j in range(NJ)]
        rp, rpB = cv("rp", [ROPE, 2, T], F32, at=o7 + 16384)
        rt1, rt1B = cv("rt1", [ROPE, T], F32, at=o7 + 16384 + 4096)
        rt2, rt2B = cv("rt2", [ROPE, T], F32, at=o7 + 16384 + 6144)
        assert o7 + 16384 + 8192 <= o9, (o7, o9)
        mrg, mrgB = cv("mrg", [128, 16, T], BF16, at=o)
        mrgC = [A.sub(mrgB, "mrg%d" % c, c * T * 2, T * 2) for c in range(16)]
        sga, sgaB = cv("sga", [128, T], F32, at=o + 16384)
        mt1, mt1B = cv("mt1", [128, T], F32, at=o + 16384 + 2048)
        mt2, mt2B = cv("mt2", [128, T], F32, at=o + 16384 + 4096)
        hbm, hbmB = cv("hbm", [128, NJ, D], F32, at=o2)
        hbmJ = [A.sub(hbmB, "hbm%d" % j, j * D * 4, D * 4) for j in range(NJ)]
        assert o2 + 32768 <= o9
        ae, aeB = cv("ae", [128, 4, T + 2], BF16, at=o)
        aeM = [A.sub(aeB, "ae%d" % m, m * (T + 2) * 2, (T + 2) * 2) for m in range(4)]
        fca, fcaB = cv("fca", [128, 4, T], F32, at=o + 8224)
        fcaM = [A.sub(fcaB, "fca%d" % m, m * T * 4, T * 4) for m in range(4)]
        fga, fgaB = cv("fga", [128, 4, T], BF16, at=o + 16416)
        fgaM = [A.sub(fgaB, "fga%d" % m, m * T * 2, T * 2) for m in range(4)]
        gff, gffB = cv("gff", [128, 44, T], BF16, at=o + 20512)
        gffC = [A.sub(gffB, "gff%d" % c, c * T * 2, T * 2) for c in range(44)]
        assert o + 20512 + 45056 <= o7, (o + 20512 + 45056, o7)
        hbf, hbfB = cv("hbf", [128, NJ, D], F32, at=R0)
        hbfJ = [A.sub(hbfB, "hbf%d" % j, j * D * 4, D * 4) for j in range(NJ)]
        assert R0 + 32768 <= o + 20512
        pt, ptB = cv("pt", [128, NJ, PLE], F32, at=o)
        ptb16, ptb16B = cv("ptb16", [128, NJ, PLE], BF16, at=o + 4096)
        pT, pTB = cv("pT", [128, 2, T], BF16, at=o + 6144)
        gsb, gsbB = cv("gsb", [128, T], F32, at=o + 8192)
        wmf, wmfB = cv("wmf", [128, 4, 128], F32, at=o9 - 4096)
        sbr, sbrB = cv("sbr", [1, 4 * 128], F32, at=o9 - 2048)
        assert o9 <= A.n, (o9, A.n)

    declare()
    A.finalize()

    bankB = []
    for i in range(8):
        bb = Buf("bank%d" % i)
        bb.excl = True
        bankB.append(bb)
    bank = None
    pes = [None]

    def declare_banks():
        nonlocal bank
        if pes[0] is not None:
            pes[0].close()
        pes[0] = ExitStack()
        bank = [(pes[0].enter_context(nc.psum_tensor("bank%d_g%d" % (i, A.gen), [128, 512], F32)), bankB[i]) for i in range(8)]
    declare_banks()

    def bk(i):
        return bank[i][0]

    def bkB(i):
        return bank[i][1]

    sp.dma(cstf, cst[:, :, :], [NOB], [cstfB], cstfB)
    dve.op(lambda e: e.tensor_copy(idb, cstf[:, 0, :]), [cstfB], [idbB])
    dve.op(lambda e: e.tensor_copy(trib, cstf[:, 1, :]), [cstfB], [tribB])
    dve.op(lambda e: e.tensor_copy(pmb, cstf[0:32, 2, 0:32]), [cstfB], [pmbB])
    pool.op(lambda e: e.memset(onesf, 1.0), [], [onesfB])
    pool.op(lambda e: e.memset(onesr, 1.0), [], [onesrB])

    wstate = {"g": 0, "issued": 0}
    NSLOT = 2
    wseq = [(l, bi) for l in range(depth) for ti in range(NT) for bi in range(NWB)]

    def w_issue():
        g = wstate["issued"]
        l, bi = wseq[g]
        s = g % NSLOT
        ncol = WBLKS[bi][1]
        n = sum(p[2] for p in WBLKS[bi][2]) // 128 * ncol
        sp.dma(wsl[s][0][:, 0:n], wb[l][bi, :, 0:n], [wbB[l]], [wsl[s][1]], wsl[s][1])
        wstate["issued"] += 1

    def w_next(expect_tag):
        g = wstate["g"]
        l, bi = wseq[g]
        assert WBLKS[bi][0] == expect_tag, (WBLKS[bi][0], expect_tag)
        while wstate["issued"] < min(g + NSLOT, len(wseq)):
            w_issue()
        wstate["g"] += 1
        t, b = wsl[g % NSLOT]
        ncol = WBLKS[bi][1]
        return t.rearrange("p (c n) -> p c n", n=ncol), b

    evq = {"i": 0}

    def ev_eng():
        evq["i"] += 1
        return act if evq["i"] % 2 else dve

    def rmsnorm_to_xnT(gidx):
        for j in range(NJ):
            act.op(lambda e, j=j: e.activation(out=junk, in_=xt[:, j, :], func=AF.Square, accum_out=ss[:, j:j + 1]),
                   [xtJ[j]], [junkB, ssB])
        act.op(lambda e: e.activation(out=rstd[:, 0:NJ], in_=ss[:, 0:NJ], func=AF.Sqrt, scale=1.0 / D, bias=EPS),
               [ssB], [rstdB])
        dve.op(lambda e: e.reciprocal(rstd[:, 0:NJ], rstd[:, 0:NJ]), [rstdB], [rstdB])
        for j in range(NJ):
            if j % 2 == 0:
                dve.op(lambda e, j=j: e.tensor_scalar(xnb[:, j, :], xt[:, j, :], rstd[:, j:j + 1], None, op0=ALU.mult),
                       [xtJ[j], rstdB], [xnbJ[j]])
            else:
                act.op(lambda e, j=j: e.activation(out=xnb[:, j, :], in_=xt[:, j, :], func=AF.Copy, scale=rstd[:, j:j + 1]),
                       [xtJ[j], rstdB], [xnbJ[j]])
        for c in range(16):
            b = 4 + (c % 4)
            pv = bk(b)[:].bitcast(BF16)
            for j in range(NJ):
                pe.op(lambda e, j=j, c=c, pv=pv: e.transpose(pv[:, j * 128:(j + 1) * 128], xnb[:, j, c * 128:(c + 1) * 128], idb),
                      [xnbJ[j], idbB], [bkB(b)], signal=(j == NJ - 1))
            eng = ev_eng()
            if eng is dve:
                dve.op(lambda e, c=c, pv=pv: e.tensor_scalar(xnT[:, c, :], pv[:, 0:T], vP[:, gidx, c:c + 1], None, op0=ALU.mult),
                       [bkB(b), vPB], [xnTC[c]])
            else:
                act.op(lambda e, c=c, pv=pv: e.activation(out=xnT[:, c, :], in_=pv[:, 0:T], func=AF.Copy, scale=vP[:, gidx, c:c + 1]),
                       [bkB(b), vPB], [xnTC[c]])

    mmq = {"i": 0}
    dgq = {"i": 0}

    def mm_bank():
        mmq["i"] += 1
        return mmq["i"] % 4

    def _sb(srcB, c):
        return [srcB[c]] if isinstance(srcB, list) else [srcB]

    def fm_chunk(wt, wB, m, src, srcB, nkc=16, kc0=0, wcol=128):
        b = mm_bank()
        for c in range(nkc):
            pe.op(lambda e, c=c, b=b: e.matmul(bk(b)[:], wt[:, kc0 + c, m * 128:(m + 1) * 128], src[:, c, :],
                                               start=(c == 0), stop=(c == nkc - 1)),
                  [wB] + _sb(srcB, c), [bkB(b)], signal=(c == nkc - 1))
        return b

    def tm_block(wt, wB, j, src, srcB, nkc=16, b=None, first=True, last=True, c_off=0):
        if b is None:
            b = mm_bank()
        for c in range(nkc):
            pe.op(lambda e, c=c, b=b: e.matmul(bk(b)[:], src[:, c_off + c, j * 128:(j + 1) * 128], wt[:, c, :],
                                               start=(first and c == 0), stop=(last and c == nkc - 1)),
                  [wB] + _sb(srcB, c_off + c), [bkB(b)], signal=(c == nkc - 1))
        return b

    def post_norm_residual(l, gi, hb, hbJ):
        sp.dma(vB, vecB[l, gi], [NOB], [vBB], vBB)
        dve.op(lambda e: e.tensor_reduce(out=ss[:, 4:8], in_=ssq, axis=AX.X, op=ALU.add), [ssqB], [ssB])
        act.op(lambda e: e.activation(out=rstd[:, 4:8], in_=ss[:, 4:8], func=AF.Sqrt, scale=1.0 / D, bias=EPS), [ssB], [rstdB])
        dve.op(lambda e: e.reciprocal(rstd[:, 4:8], rstd[:, 4:8]), [rstdB], [rstdB])
        for j in range(NJ):
            act.op(lambda e, j=j: e.activation(out=hb[:, j, :], in_=hb[:, j, :], func=AF.Copy, scale=rstd[:, 4 + j:5 + j]), [hbJ[j], rstdB], [hbJ[j]])
            dve.op(lambda e, j=j: e.tensor_tensor(hb[:, j, :], hb[:, j, :], vB, ALU.mult), [hbJ[j], vBB], [hbJ[j]])
            pool.op(lambda e, j=j: e.tensor_tensor(xt[:, j, :], xt[:, j, :], hb[:, j, :], ALU.add), [hbJ[j], xtJ[j]], [xtJ[j]])

    def tm_evac(b, j, nb, hb, hbJ):
        act.op(lambda e: e.activation(out=junk[:, 0:512], in_=bk(b)[:], func=AF.Square, accum_out=ssq[:, j, nb:nb + 1]),
               [bkB(b)], [junkB, ssqB])
        dve.op(lambda e: e.tensor_copy(hb[:, j, nb * 512:(nb + 1) * 512], bk(b)[:]), [bkB(b)], [hbJ[j]])

    cnt = {"kv": 0, "pt": 0, "st": 0, "o": 0}

    def attention(l, ti):
        a0 = 2 * ti
        passes = []

        def gate_pre(h):
            def f():
                hs = h % 2
                sel_t, sel_B = sel[hs]
                acc_t, acc_B = acc[hs]
                for j in range(NJ):
                    own = a0 + j // 2
                    if own == 0:
                        continue
                    pe.op(lambda e, j=j: e.matmul(bk(6)[:, j * 64:(j + 1) * 64], qs[:, h, j * 128:(j + 1) * 128], kmean[:, h, :], start=True, stop=True),
                          [qsH[h], kmeanB], [bkB(6)])
                if a0 + 1 > 0:
                    j0 = 0 if a0 > 0 else 2
                    own_max = a0 + 1
                    for j in range(j0, NJ):
                        own = a0 + j // 2
                        dve.op(lambda e, j=j, own=own: e.tensor_copy(Gs[:, j, 0:own], bk(6)[:, j * 64:j * 64 + own]), [bkB(6)], [GsB])
                        if own <= TOPK:
                            pool.op(lambda e, j=j, own=own: e.memset(sel_t[:, j, 0:own], 1.0), [], [sel_B])
                        else:
                            dve.op(lambda e, j=j: e.max(out=top8, in_=Gs[:, j, :]), [GsB], [top8B])
                            dve.op(lambda e, j=j, own=own: e.tensor_scalar(sel_t[:, j, 0:own], Gs[:, j, 0:own], top8[:, TOPK - 1:TOPK], None, op0=ALU.is_ge),
                                   [GsB, top8B], [sel_B])
                pool.op(lambda e: e.memset(acc_t, 0.0), [], [acc_B])
            return f

        def head_post(h):
            def f():
                hs = h % 2
                acc_t, acc_B = acc[hs]
                atk_t, atk_B = atk[hs]
                dve.op(lambda e: e.reciprocal(rinv, acc_t[:, :, HD]), [acc_B], [rinvB])
                for j in range(NJ):
                    eng = pool if j % 2 else dve
                    eng.op(lambda e, j=j: e.tensor_scalar(atk_t[:, j, :], acc_t[:, j, 0:HD], rinv[:, j:j + 1], None, op0=ALU.mult), [acc_B, rinvB], [atk_B])
                pv = bk(7)[:].bitcast(BF16)
                for j in range(NJ):
                    pe.op(lambda e, j=j: e.transpose(pv[:, j * 128:(j + 1) * 128], atk_t[:, j, :], idb), [atk_B, idbB], [bkB(7)], signal=(j == NJ - 1))
                if h % 2:
                    act.op(lambda e: e.activation(out=atf[:, h, :], in_=pv[:, 0:T], func=AF.Copy), [bkB(7)], [atfB])
                else:
                    dve.op(lambda e: e.tensor_copy(atf[:, h, :], pv[:, 0:T]), [bkB(7)], [atfB])
            return f

        for h in range(NH):
            hp = []
            for g0 in range(0, a0, GRP):
                nblk = min(GRP, a0 - g0)

                def load(h=h, g0=g0, nblk=nblk):
                    bi = cnt["kv"] % NKB
                    cnt["kv"] += 1
                    kb_t, kb_B = kbuf[bi]
                    vb_t, vb_B = vbuf[bi]
                    pool.op(lambda e: e.memset(vb_t[:, :, HD:HD + 1], 1.0), [], [vb_B])
                    sp.dma(kb_t[:, 0:nblk * BLK], kT[l][h, :, g0 * BLK:(g0 + nblk) * BLK], [kTB[l]], [kb_B], kb_B)
                    sp.dma(vb_t[:, 0:nblk * 2, 0:HD], vS[l][g0 * BLK:(g0 + nblk) * BLK, h, :].rearrange("(k p) d -> p k d", p=128), [vSB[l]], [vb_B], vb_B)
                    return kb_t, kb_B, vb_t, vb_B
                holder = {}
                for n in range(nblk):
                    def kts(holder=holder, n=n, load=load):
                        if "b" not in holder:
                            holder["b"] = load()
                        kb_t, kb_B, vb_t, vb_B = holder["b"]
                        return [(kb_t[:, (n * 2 + kk) * 128:(n * 2 + kk + 1) * 128], kb_B, vb_t[:, n * 2 + kk, :], vb_B) for kk in range(2)]
                    hp.append(dict(h=h, kts=kts, jl=[0, 1, 2, 3], selcol=g0 + n, diag=None))
            hp.append(dict(h=h, kts=(lambda h=h: [(ks[:, h, kk * 128:(kk + 1) * 128], ksH[h], vs[:, kk, h, :], vsJ[kk]) for kk in range(2)]),
                           jl=[2, 3], selcol=a0, diag=None))
            for mb in range(2):
                hp.append(dict(h=h, kts=(lambda h=h, mb=mb: [(ks[:, h, (2 * mb + kk) * 128:(2 * mb + kk + 1) * 128], ksH[h], vs[:, 2 * mb + kk, h, :], vsJ[2 * mb + kk]) for kk in range(2)]),
                               jl=[2 * mb, 2 * mb + 1], selcol=None, diag={0: 2 * mb, 1: 2 * mb + 1}))
            hp[0]["pre"] = gate_pre(h)
            hp[-1]["post"] = head_post(h)
            passes.extend(hp)

        def stageA(P):
            if "pre" in P:
                P["pre"]()
            h = P["h"]
            kt_list = P["kts"]()
            P["ktl"] = kt_list
            P["pts"] = []
            P["plan"] = []
            for ki, (k_ap, kB_, v_ap, vB_) in enumerate(kt_list):
                js = [j for j in P["jl"] if (P["diag"] is None or j >= P["diag"][ki])]
                P["plan"].append(js)
                q0, q1 = js[0] * 128, (js[-1] + 1) * 128
                sb_ = SBANKS[cnt["st"] % 4]
                cnt["st"] += 1
                pe.op(lambda e, sb_=sb_, k_ap=k_ap, q0=q0, q1=q1: e.matmul(bk(sb_)[:, q0:q1], k_ap, qs[:, h, q0:q1], start=True, stop=True),
                      [kB_, qsH[h]], [bkB(sb_)])
                pi = cnt["pt"] % NPT
                cnt["pt"] += 1
                pt_t, pt_B = ptb[pi]
                act.op(lambda e, sb_=sb_, pt_t=pt_t, q0=q0, q1=q1: e.activation(out=pt_t[:, q0:q1], in_=bk(sb_)[:, q0:q1], func=AF.Exp),
                       [bkB(sb_)], [pt_B])
                if P["diag"] is not None:
                    jd = P["diag"][ki]
                    pool.op(lambda e, pt_t=pt_t, jd=jd: e.tensor_tensor(pt_t[:, jd * 128:(jd + 1) * 128], pt_t[:, jd * 128:(jd + 1) * 128], trib, ALU.mult),
                            [pt_B, tribB], [pt_B])
                P["pts"].append((pt_t, pt_B))

        def stageB(P):
            h = P["h"]
            hs = h % 2
            sel_t, sel_B = sel[hs]
            acc_t, acc_B = acc[hs]
            oset = cnt["o"] % 2
            cnt["o"] += 1
            jl = P["jl"]
            ob = {j: (2 + oset * 2 + (j // 2), (j % 2) * (HD + 1)) for j in jl}
            for j in jl:
                kis = [ki for ki in range(len(P["ktl"])) if j in P["plan"][ki]]
                b_, oo = ob[j]
                lastj = (j == jl[-1]) or (ob[jl[jl.index(j) + 1]][0] != b_)
                for idx, ki in enumerate(kis):
                    pt_t, pt_B = P["pts"][ki]
                    _, _, v_ap, vB_ = P["ktl"][ki]
                    pe.op(lambda e, j=j, b_=b_, oo=oo, pt_t=pt_t, v_ap=v_ap, idx=idx, kis=kis: e.matmul(bk(b_)[:, oo:oo + HD + 1], pt_t[:, j * 128:(j + 1) * 128], v_ap,
                                                                                               start=(idx == 0), stop=(idx == len(kis) - 1)),
                          [pt_B, vB_], [bkB(b_)], signal=(lastj and idx == len(kis) - 1))
            for j in jl:
                b_, oo = ob[j]
                if P["selcol"] is None:
                    dve.op(lambda e, j=j, b_=b_, oo=oo: e.tensor_tensor(acc_t[:, j, :], acc_t[:, j, :], bk(b_)[:, oo:oo + HD + 1], ALU.add),
                           [bkB(b_), acc_B], [acc_B])
                else:
                    sc = P["selcol"]
                    dve.op(lambda e, j=j, b_=b_, oo=oo, sc=sc: e.scalar_tensor_tensor(acc_t[:, j, :], bk(b_)[:, oo:oo + HD + 1], sel_t[:, j, sc:sc + 1], acc_t[:, j, :], ALU.mult, ALU.add),
                           [bkB(b_), sel_B, acc_B], [acc_B])

        stageA(passes[0])
        pending = []
        for i in range(len(passes)):
            if i + 1 < len(passes):
                stageA(passes[i + 1])
            stageB(passes[i])
            if "post" in passes[i]:
                nxt = 0
                while i + 1 + nxt < len(passes) and "post" not in passes[i + 1 + nxt]:
                    nxt += 1
                nxt += 1 if i + 1 + nxt < len(passes) else 0
                pending.append((i + max(2, min(6, nxt - 1)), passes[i]["post"]))
            while pending and pending[0][0] <= i:
                pending.pop(0)[1]()
        for _, fn in pending:
            fn()

    for l in range(depth):
        x_src, x_srcB = (x_in, NOB) if l == 0 else (xs[l - 1], xsB[l - 1])
        x_dst, x_dstB = (out, outB) if l == depth - 1 else (xs[l], xsB[l])
        sp.dma(vP, vecP[l], [NOB], [vPB], vPB)
        sp.dma(cw, convw[l], [NOB], [cwB], cwB)
        sp.dma(sgB, sguB[l].rearrange("a p n -> p a n"), [NOB], [sgBB], sgBB)
        sp.dma(wmf, sguwT[l], [NOB], [wmfB], wmfB)
        sp.dma(sbr, sgub[l], [NOB], [sbrB], sbrB)
        sp.dma(fw, ffnw[l], [NOB], [fwB], fwB)
        for g in range(4):
            dve.op(lambda e, g=g: e.tensor_tensor(wmT[:, g, :], wmf[:, g, :], cstf[:, 1, :], ALU.mult), [wmfB, cstfB], [wmTB])
        dve.op(lambda e: e.tensor_copy(sbrb, sbr), [sbrB], [sbrbB])
        pool.op(lambda e: e.memset(ahalo, 0.0), [], [ahaloB])
        pool.op(lambda e: e.memset(fhalo, 0.0), [], [fhaloB])
        pool.op(lambda e: e.memset(Gs, NEG), [], [GsB])

        for ti in range(NT):
            t0 = ti * T
            declare()
            declare_banks()
            sp.dma(xt, x_src[t0:t0 + T, :].rearrange("(j p) d -> p j d", p=128), [x_srcB], [xtB], xtB)
            rmsnorm_to_xnT(0)
            sp.dma(rp, ropet[:, :, t0:t0 + T], [NOB], [rpB], rpB)

            for qi in range(4):
                wt, wB = w_next("P%d" % qi)
                isq = qi < 2
                for m in range(4):
                    h = (qi % 2) * 4 + m
                    dst, dstH = (qs, qsH) if isq else (ks, ksH)
                    b = fm_chunk(wt, wB, m, xnT, xnTC)
                    sc = HD ** -0.5 if isq else 1.0
                    act.op(lambda e, b=b, h=h, dst=dst, sc=sc: e.activation(out=dst[:, h, :], in_=bk(b)[:], func=AF.Copy, scale=sc),
                           [bkB(b)], [dstH[h]])
                    pe.op(lambda e, h=h, dst=dst: e.matmul(bk(5)[0:32, :], pmb, dst[0:32, h, :], start=True, stop=True),
                          [pmbB, dstH[h]], [bkB(5)])
                    dve.op(lambda e: e.tensor_tensor(rt2, bk(5)[0:32, :], rp[:, 1, :], ALU.mult), [bkB(5), rpB], [rt2B])
                    pool.op(lambda e, h=h, dst=dst: e.tensor_tensor(rt1, dst[0:32, h, :], rp[:, 0, :], ALU.mult), [dstH[h], rpB], [rt1B])
                    dve.op(lambda e, h=h, dst=dst: e.tensor_tensor(dst[0:32, h, :], rt1, rt2, ALU.add), [rt1B, rt2B], [dstH[h]])
                    if not isq:
                        for half in range(2):
                            nblk = 2 * ti + half
                            dve.op(lambda e, h=h, half=half: e.tensor_reduce(out=kms[:, half:half + 1], in_=ks[:, h, half * BLK:(half + 1) * BLK],
                                                                             axis=AX.X, op=ALU.add), [ksH[h]], [kmsB])
                            dve.op(lambda e, h=h, half=half, nblk=nblk: e.tensor_scalar(kmean[:, h, nblk:nblk + 1], kms[:, half:half + 1], 1.0 / BLK, None, op0=ALU.mult),
                                   [kmsB], [kmeanB])
            act.dma(kT[l][:, :, t0:t0 + T].rearrange("h d t -> d h t"), ks, [ksB], [kTB[l]], ksB)
            pool.op(lambda e: e.memset(vs[:, :, :, HD:HD + 1], 1.0), [], [vsB])
            for vi in range(2):
                wt, wB = w_next("P%d" % (4 + vi))
                for j in range(NJ):
                    b = tm_block(wt, wB, j, xnT, xnTC)
                    eng = ev_eng()
                    src_v = bk(b)[:].rearrange("p (h d) -> p h d", h=4)
                    if eng is dve:
                        dve.op(lambda e, j=j, vi=vi, src_v=src_v: e.tensor_copy(vs[:, j, vi * 4:(vi + 1) * 4, 0:HD], src_v), [bkB(b)], [vsJ[j]])
                    else:
                        act.op(lambda e, j=j, vi=vi, src_v=src_v: e.activation(out=vs[:, j, vi * 4:(vi + 1) * 4, 0:HD], in_=src_v, func=AF.Copy), [bkB(b)], [vsJ[j]])
            for j in range(NJ):
                act.dma(vS[l][t0 + j * 128:t0 + (j + 1) * 128, :, :], vs[:, j, :, 0:HD], [vsB], [vSB[l]], vsB)
            wt, wB = w_next("P6")
            for m in range(4):
                b = fm_chunk(wt, wB, m, xnT, xnTC)
                act.op(lambda e, b=b, m=m: e.activation(out=sgc[:, m, :], in_=bk(b)[:], func=AF.Sigmoid), [bkB(b)], [sgcB])
            pool.op(lambda e: e.tensor_copy(aext[:, :, 0:CK - 1], ahalo), [ahaloB], [aextB])
            wt, wB = w_next("P7")
            for m in range(4):
                b = fm_chunk(wt, wB, m, xnT, xnTC)
                dve.op(lambda e, b=b, m=m: e.tensor_tensor(aext[:, m, CK - 1:CK - 1 + T], bk(b)[:], sgc[:, m, :], ALU.mult), [bkB(b), sgcB], [aextB])
            pool.op(lambda e: e.tensor_copy(ahalo, aext[:, :, T:T + CK - 1]), [aextB], [ahaloB])
            wt, wB = w_next("P8")
            for m in range(4):
                b = fm_chunk(wt, wB, m, xnT, xnTC)
                act.op(lambda e, b=b, m=m: e.activation(out=sus[:, m, :], in_=bk(b)[:], func=AF.Gelu_apprx_tanh), [bkB(b)], [susB])
            wt, wB = w_next("P9")
            for j in range(NJ):
                b = tm_block(wt, wB, j, xnT, xnTC)
                act.op(lambda e, b=b: e.activation(out=svg, in_=bk(b)[:], func=AF.Gelu_apprx_tanh), [bkB(b)], [svgB])
                dve.op(lambda e: e.bn_stats(bst, svg), [svgB], [bstB])
                dve.op(lambda e: e.bn_aggr(bag, bst), [bstB], [bagB])
                act.op(lambda e: e.activation(out=bag[:, 1:2], in_=bag[:, 1:2], func=AF.Sqrt, bias=EPS), [bagB], [bagB])
                dve.op(lambda e: e.reciprocal(bag[:, 1:2], bag[:, 1:2]), [bagB], [bagB])
                dve.op(lambda e: e.tensor_scalar(svg, svg, bag[:, 0:1], bag[:, 1:2], op0=ALU.subtract, op1=ALU.mult), [svgB, bagB], [svgB])
                dve.op(lambda e: e.tensor_tensor(svg, svg, sgB[:, 0, :], ALU.mult), [svgB, sgBB], [svgB])
                dve.op(lambda e, j=j: e.tensor_tensor(svn[:, j, :], svg, sgB[:, 1, :], ALU.add), [svgB, sgBB], [svnB])

            for m in range(4):
                b = mm_bank()
                for k in range(CK):
                    dg_t, dg_B = dg[dgq["i"] % 8]
                    dgq["i"] += 1
                    dve.op(lambda e, m=m, k=k, dg_t=dg_t: e.tensor_scalar(dg_t, idb, cw[:, m, k:k + 1], None, op0=ALU.mult), [idbB, cwB], [dg_B])
                    pe.op(lambda e, m=m, k=k, b=b, dg_t=dg_t: e.matmul(bk(b)[:], dg_t, aext[:, m, k:k + T], start=(k == 0), stop=(k == CK - 1)),
                          [dg_B, aextB], [bkB(b)], signal=True)
                act.op(lambda e, m=m, b=b: e.activation(out=cacc[:, m, :], in_=bk(b)[:], func=AF.Identity, bias=cw[:, m, CK:CK + 1]),
                       [bkB(b), cwB], [caccM[m]])

            for g in range(4):
                for j in range(NJ):
                    pe.op(lambda e, g=g, j=j: e.matmul(bk(5)[:, j * 128:(j + 1) * 128], svn[:, j, g * 128:(g + 1) * 128], wmT[:, g, :], start=True, stop=False),
                          [svnB, wmTB], [bkB(5)], signal=False)
                    pe.op(lambda e, g=g, j=j: e.matmul(bk(5)[:, j * 128:(j + 1) * 128], onesr, sbrb[:, g * 128:(g + 1) * 128], start=False, stop=True),
                          [onesrB, sbrbB], [bkB(5)], signal=(j == NJ - 1))
                dve.op(lambda e, g=g: e.tensor_tensor(ybin[:, g, :], bk(5)[:], sus[:, g, :], ALU.mult), [bkB(5), susB], [ybinB])

            attention(l, ti)

            for m in range(4):
                act.op(lambda e, m=m: e.activation(out=csq, in_=cacc[:, m, :], func=AF.Square), [caccB], [csqB])
                pe.op(lambda e, m=m: e.matmul(bk(4)[:], onesf, cacc[:, m, :], start=(m == 0), stop=(m == 3)), [onesfB, caccB], [bkB(4)], signal=(m == 3))
                pe.op(lambda e, m=m: e.matmul(bk(5)[:], onesf, csq, start=(m == 0), stop=(m == 3)), [onesfB, csqB], [bkB(5)], signal=True)
            dve.op(lambda e: e.tensor_scalar(lnm, bk(4)[:], 1.0 / CC, None, op0=ALU.mult), [bkB(4)], [lnmB])
            dve.op(lambda e: e.tensor_tensor(csq, lnm, lnm, ALU.mult), [lnmB], [csqB])
            dve.op(lambda e: e.scalar_tensor_tensor(lnr, bk(5)[:], 1.0 / CC, csq, ALU.mult, ALU.subtract), [bkB(5), csqB], [lnrB])
            act.op(lambda e: e.activation(out=lnr, in_=lnr, func=AF.Sqrt, bias=EPS), [lnrB], [lnrB])
            dve.op(lambda e: e.reciprocal(lnr, lnr), [lnrB], [lnrB])
            for m in range(4):
                dve.op(lambda e, m=m: e.tensor_tensor(cacc[:, m, :], cacc[:, m, :], lnm, ALU.subtract), [caccB, lnmB], [caccB])
                dve.op(lambda e, m=m: e.tensor_tensor(cacc[:, m, :], cacc[:, m, :], lnr, ALU.mult), [caccB, lnrB], [caccB])
                act.op(lambda e, m=m: e.activation(out=acta[:, m, :], in_=cacc[:, m, :], func=AF.Silu, scale=cw[:, m, CK + 1:CK + 2], bias=cw[:, m, CK + 2:CK + 3]),
                       [caccB, cwB], [actaB])

            srcs = [(acta, actaB, 4, 0), (ybin, ybinB, 4, 4), (atf, atfB, 8, 8)]
            for c in range(16):
                wy, wyB = w_next("YG%d" % c)
                for g in range(3):
                    bg = fm_chunk(wy, wyB, 0, xnT, xnTC, nkc=16, kc0=16 + 16 * g)
                    act.op(lambda e, bg=bg: e.activation(out=sga, in_=bk(bg)[:], func=AF.Sigmoid), [bkB(bg)], [sgaB])
                    src, srcB, nkc, kc0 = srcs[g]
                    by = fm_chunk(wy, wyB, 0, src, srcB, nkc=nkc, kc0=kc0)
                    if g == 0:
                        dve.op(lambda e, by=by: e.tensor_tensor(mt1, bk(by)[:], sga, ALU.mult), [bkB(by), sgaB], [mt1B])
                    elif g == 1:
                        dve.op(lambda e, by=by: e.tensor_tensor(mt2, bk(by)[:], sga, ALU.mult), [bkB(by), sgaB], [mt2B])
                        pool.op(lambda e: e.tensor_tensor(mt1, mt1, mt2, ALU.add), [mt1B, mt2B], [mt1B])
                    else:
                        dve.op(lambda e, by=by: e.tensor_tensor(mt2, bk(by)[:], sga, ALU.mult), [bkB(by), sgaB], [mt2B])
                        pool.op(lambda e, c=c: e.tensor_tensor(mrg[:, c, :], mt1, mt2, ALU.add), [mt1B, mt2B], [mrgC[c]])

            for nb in range(4):
                wt, wB = w_next("O%d" % nb)
                for j in range(NJ):
                    b = tm_block(wt, wB, j, mrg, mrgC)
                    tm_evac(b, j, nb, hbm, hbmJ)
            post_norm_residual(l, 0, hbm, hbmJ)

            rmsnorm_to_xnT(1)
            for i in range(11):
                wt, wB = w_next("FA%d" % i)
                for m in range(4):
                    c = i * 4 + m
                    b = fm_chunk(wt, wB, m, xnT, xnTC)
                    act.op(lambda e, b=b, m=m: e.activation(out=ae[:, m, 2:2 + T], in_=bk(b)[:], func=AF.Copy), [bkB(b)], [aeM[m]])
                    pool.op(lambda e, m=m, c=c: e.tensor_copy(ae[:, m, 0:2], fhalo[:, c, :]), [fhaloB], [aeM[m]])
                    pool.op(lambda e, m=m, c=c: e.tensor_copy(fhalo[:, c, :], ae[:, m, T:T + 2]), [aeM[m]], [fhaloB])
                for m in range(4):
                    c = i * 4 + m
                    b2 = mm_bank()
                    for k in range(3):
                        dg_t, dg_B = dg[dgq["i"] % 8]
                        dgq["i"] += 1
                        dve.op(lambda e, c=c, k=k, dg_t=dg_t: e.tensor_scalar(dg_t, idb, fw[:, c, k:k + 1], None, op0=ALU.mult), [idbB, fwB], [dg_B])
                        pe.op(lambda e, m=m, k=k, b2=b2, dg_t=dg_t: e.matmul(bk(b2)[:], dg_t, ae[:, m, k:k + T], start=(k == 0), stop=(k == 2)),
                              [dg_B, aeM[m]], [bkB(b2)], signal=True)
                    act.op(lambda e, m=m, c=c, b2=b2: e.activation(out=fga[:, m, :], in_=bk(b2)[:], func=AF.Gelu_apprx_tanh, bias=fw[:, c, 3:4]),
                           [bkB(b2), fwB], [fgaM[m]])
                wt, wB = w_next("FB%d" % i)
                for m in range(4):
                    c = i * 4 + m
                    b = fm_chunk(wt, wB, m, xnT, xnTC)
                    dve.op(lambda e, b=b, m=m, c=c: e.tensor_tensor(gff[:, c, :], bk(b)[:], fga[:, m, :], ALU.mult), [bkB(b), fgaM[m]], [gffC[c]])
            for nb in range(4):
                for kg in range(4):
                    wt, wB = w_next("FO%d_%d" % (nb, kg))
                    for j in range(NJ):
                        tm_block(wt, wB, j, gff, gffC, nkc=11, b=j, first=(kg == 0), last=(kg == 3), c_off=kg * 11)
                for j in range(NJ):
                    tm_evac(j, j, nb, hbf, hbfJ)
            post_norm_residual(l, 1, hbf, hbfJ)

            rmsnorm_to_xnT(2)
            sp.dma(pt, p_in[l, t0:t0 + T, :].rearrange("(j p) d -> p j d", p=128), [NOB], [ptB], ptB)
            dve.op(lambda e: e.tensor_copy(ptb16, pt), [ptB], [ptb16B])
            pv = bk(5)[:].bitcast(BF16)
            for c in range(2):
                for j in range(NJ):
                    pe.op(lambda e, j=j, c=c: e.transpose(pv[:, j * 128:(j + 1) * 128], ptb16[:, j, c * 128:(c + 1) * 128], idb),
                          [ptb16B, idbB], [bkB(5)], signal=(j == NJ - 1))
                dve.op(lambda e, c=c: e.tensor_copy(pT[:, c, :], pv[:, 0:T]), [bkB(5)], [pTB])
            for nb in range(4):
                wt, wB = w_next("PG%d" % nb)
                gbanks = []
                for j in range(NJ):
                    gbanks.append(tm_block(wt, wB, j, xnT, xnTC, b=j))
                wp, wpB = w_next("PP%d" % nb)
                for j in range(NJ):
                    b = gbanks[j]
                    act.op(lambda e, b=b: e.activation(out=gsb, in_=bk(b)[:], func=AF.Sigmoid), [bkB(b)], [gsbB])
                    b2 = 4 + (j % 2)
                    for c in range(2):
                        pe.op(lambda e, c=c, j=j, b2=b2: e.matmul(bk(b2)[:], pT[:, c, j * 128:(j + 1) * 128], wp[:, c, :], start=(c == 0), stop=(c == 1)),
                              [pTB, wpB], [bkB(b2)], signal=(c == 1))
                    dve.op(lambda e, b2=b2: e.tensor_tensor(gsb, gsb, bk(b2)[:], ALU.mult), [gsbB, bkB(b2)], [gsbB])
                    pool.op(lambda e, j=j, nb=nb: e.tensor_tensor(xt[:, j, nb * 512:(nb + 1) * 512], xt[:, j, nb * 512:(nb + 1) * 512], gsb, ALU.add),
                            [gsbB, xtJ[j]], [xtJ[j]])
            act.dma(x_dst[t0:t0 + T, :].rearrange("(j p) d -> p j d", p=128), xt, [xtB], [x_dstB], xtB)

    sp.wait_all([xtB, ksB, vsB, outB] + kTB + vSB)
    act.wait_all([xtB, ksB, vsB, outB])
    pes[0].close()
    K.es.close()
    global _LAST_KB
    _LAST_KB = K
    return nc


def host_consts(S):
    cst = np.zeros((128, 3, 128), np.float32)
    cst[:, 0, :] = np.eye(128, dtype=np.float32)
    cst[:, 1, :] = np.triu(np.ones((128, 128), np.float32))
    for m in range(32):
        cst[(m + 16) % 32, 2, m] = 1.0
    half = ROPE // 2
    inv = np.float32(500000.0) ** (-np.arange(0, ROPE, 2, dtype=np.float32) / np.float32(ROPE))
    ang = np.arange(S, dtype=np.float32)[:, None] * inv[None, :].astype(np.float32)
    cos = np.cos(ang.astype(np.float64)).astype(np.float32).T
    sin = np.sin(ang.astype(np.float64)).astype(np.float32).T
    ropet = np.zeros((ROPE, 2, S), np.float32)
    ropet[0:half, 0] = cos
    ropet[half:, 0] = cos
    ropet[0:half, 1] = -sin
    ropet[half:, 1] = sin
    return cst, ropet


def layout_params(inp, depth):
    f = lambda a: np.ascontiguousarray(np.asarray(a, dtype=np.float32))
    L = depth
    toP = lambda v: v.reshape(L, -1, 128).transpose(0, 2, 1)
    vecP = np.stack([toP(f(inp["mix_norm_pre"])), toP(f(inp["ffn_norm_pre"])), toP(f(inp["ple_norm"]))], axis=2)
    vecB = np.stack([np.broadcast_to(f(inp["mix_norm_post"])[:, None, :], (L, 128, D)),
                     np.broadcast_to(f(inp["ffn_norm_post"])[:, None, :], (L, 128, D))], axis=1)
    cw = f(inp["conv_dw_w"]).reshape(L, CK, 4, 128).transpose(0, 3, 2, 1)
    extra = np.stack([toP(f(inp["conv_dw_b"])), toP(f(inp["conv_norm_g"])), toP(f(inp["conv_norm_b"]))], axis=3)
    convw = np.concatenate([cw, extra], axis=3)
    sguB = np.stack([np.broadcast_to(f(inp["sgu_norm_g"])[:, None, :], (L, 128, SW)),
                     np.broadcast_to(f(inp["sgu_norm_b"])[:, None, :], (L, 128, SW))], axis=1)
    sguwT = f(inp["sgu_w"]).transpose(0, 3, 1, 2)
    sgub = f(inp["sgu_b"]).reshape(L, 1, 4 * 128)
    fwt = f(inp["ffn_dw_w"]).reshape(L, 3, 44, 128).transpose(0, 3, 2, 1)
    fwb = toP(f(inp["ffn_dw_b"]))[..., None]
    ffnw = np.concatenate([fwt, fwb], axis=3)
    c = np.ascontiguousarray
    return dict(vecP=c(vecP), vecB=c(vecB), convw=c(convw), sguB=c(sguB), sguwT=c(sguwT), sgub=c(sgub), ffnw=c(ffnw))


_CACHE = {}


def run(inp, S, depth, nseq):
    key = (S, depth)
    if key not in _CACHE:
        _CACHE[key] = build(S, depth)
    nc = _CACHE[key]
    cst, ropet = host_consts(S)
    prm = layout_params(inp, depth)
    f = lambda a: np.ascontiguousarray(np.asarray(a, dtype=np.float32))
    shared = dict(prm)
    shared.update(cst=cst, ropet=ropet)
    for k in ("w_in", "conv_out", "sgu_out", "attn_out", "w_o", "ffn_in", "ffn_out", "ple_gate", "ple_proj"):
        shared[k] = f(inp[k])
    x = f(inp["x"])
    p = f(inp["p"])
    in_maps = []
    for b in range(nseq):
        m = dict(shared)
        m["x"] = np.ascontiguousarray(x[b])
        m["p"] = np.ascontiguousarray(p[:, b])
        in_maps.append(m)
    res = run_bass_kernel_spmd(nc, in_maps, core_ids=list(range(nseq)))
    return np.stack([np.asarray(res.results[b]["out"], dtype=np.float32) for b in range(nseq)], axis=0)


def kernel(**inputs):
    x = np.asarray(inputs["x"])
    B, S, _ = x.shape
    depth = np.asarray(inputs["w_in"]).shape[0]
    return run(inputs, S, depth, B)
```
